# Optimizing a Trainium2 kernel written in Bass

```python
import math
import jax, jax.numpy as jnp
from jax import lax
import numpy as np

D_MODEL = 1024
BATCH = 2
SEQ = 8192
DEPTH = 4
DEC_BATCH = 8
DEC_SEQ = 8192
PAST_LEN = 128

N_MEM = 256
HEAD_DIM = 64
D_HYENA = 512
D_CONF = 256
D_XATTN = 256
N_XHEADS = D_XATTN // HEAD_DIM
D_MIX = D_HYENA + D_CONF + D_XATTN
SHORT_K = 3
CONF_K = 31
FILT_BANDS = 16
FILT_EMB = 1 + 2 * FILT_BANDS
FILT_HIDDEN = 64
DECAY_TARGET = 1e-2
FAST_DECAY_PCT = 0.3
SLOW_DECAY_PCT = 1.5
EPS = 1e-6

P_HY_U = 3 * D_HYENA
P_HY_G = D_HYENA
P_CF_GLU = 2 * D_CONF
P_CF_G = D_CONF
P_XA_Q = D_XATTN
P_XA_G = D_XATTN
D_IN = P_HY_U + P_HY_G + P_CF_GLU + P_CF_G + P_XA_Q + P_XA_G
SPLITS = (P_HY_U,
          P_HY_U + P_HY_G,
          P_HY_U + P_HY_G + P_CF_GLU,
          P_HY_U + P_HY_G + P_CF_GLU + P_CF_G,
          P_HY_U + P_HY_G + P_CF_GLU + P_CF_G + P_XA_Q)

kernel_name = "hybrid_hyena_conformer_memattn_encoder"


def rmsnorm(x, g):
    xf = x.astype(jnp.float32)
    y = xf * lax.rsqrt(jnp.mean(xf * xf, axis=-1, keepdims=True) + EPS)
    return (y * g.astype(jnp.float32)).astype(x.dtype)


def layernorm(x, g, b):
    xf = x.astype(jnp.float32)
    mu = jnp.mean(xf, axis=-1, keepdims=True)
    xc = xf - mu
    y = xc * lax.rsqrt(jnp.mean(xc * xc, axis=-1, keepdims=True) + EPS)
    return (y * g.astype(jnp.float32) + b.astype(jnp.float32)).astype(x.dtype)


def depthwise_conv(x, w, b):
    k = w.shape[0]
    y = lax.conv_general_dilated(
        x, w[:, None, :].astype(x.dtype), window_strides=(1,),
        padding=[(k // 2, k // 2)], dimension_numbers=('NWC', 'WIO', 'NWC'),
        feature_group_count=x.shape[-1])
    return y + b.astype(x.dtype)


def hyena_positional(L):
    t = jnp.linspace(0.0, 1.0, L, dtype=jnp.float32)[:, None]
    n = jnp.arange(L, dtype=jnp.float32)[:, None]
    bands = jnp.linspace(1e-4, FILT_BANDS - 1, FILT_BANDS, dtype=jnp.float32)[None, :]
    ang = (2.0 * math.pi / L) * bands * n
    z = jnp.concatenate([t, jnp.cos(ang), -jnp.sin(ang)], axis=-1)
    return t, z


def hyena_two_sided_filter(L, w1, b1, fr1, w2, b2, fr2, w3, b3, fr3, w4):
    t, z = hyena_positional(L)
    f32 = jnp.float32
    h = jnp.sin(fr1.astype(f32) * (z @ w1.astype(f32) + b1.astype(f32)))
    h = jnp.sin(fr2.astype(f32) * (h @ w2.astype(f32) + b2.astype(f32)))
    h = jnp.sin(fr3.astype(f32) * (h @ w3.astype(f32) + b3.astype(f32)))
    h = h @ w4.astype(f32)
    max_decay = math.log(DECAY_TARGET) / FAST_DECAY_PCT
    min_decay = math.log(DECAY_TARGET) / SLOW_DECAY_PCT
    deltas = jnp.abs(jnp.linspace(min_decay, max_decay, D_HYENA, dtype=f32))
    deltas = jnp.concatenate([deltas, deltas])
    h = h * jnp.exp(-t * deltas)
    h_fwd, h_bwd = h[:, :D_HYENA], h[:, D_HYENA:]
    return jnp.concatenate([h_fwd, jnp.zeros((1, D_HYENA), f32), h_bwd[1:][::-1]], axis=0)


def bidir_fftconv(u, k2, skip):
    L = u.shape[1]
    uf32 = u.astype(jnp.float32)
    uf = jnp.fft.rfft(uf32, n=2 * L, axis=1)
    kf = jnp.fft.rfft(k2, n=2 * L, axis=0)
    y = jnp.fft.irfft(uf * kf[None], n=2 * L, axis=1)[:, :L]
    return (y + uf32 * skip.astype(jnp.float32)).astype(u.dtype)


def hyena_branch(u, short_w, short_b, k2, skip):
    u = depthwise_conv(u, short_w, short_b)
    x0, x1, v = jnp.split(u, 3, axis=-1)
    return x0 * bidir_fftconv(v * x1, k2, skip)


def conformer_branch(a, dw_w, dw_b, ln_g, ln_b):
    val, gate = jnp.split(a, 2, axis=-1)
    h = val * jax.nn.sigmoid(gate)
    h = depthwise_conv(h, dw_w, dw_b)
    h = layernorm(h, ln_g, ln_b)
    return jax.nn.silu(h)


def memory_attention(q, mem_n, w_kv):
    B, L, _ = q.shape
    M = mem_n.shape[1]
    k, v = jnp.split(mem_n @ w_kv, 2, axis=-1)
    q = q.reshape(B, L, N_XHEADS, HEAD_DIM)
    k = k.reshape(B, M, N_XHEADS, HEAD_DIM)
    v = v.reshape(B, M, N_XHEADS, HEAD_DIM)
    s = jnp.einsum('blhd,bmhd->bhlm', q, k).astype(jnp.float32) * (HEAD_DIM ** -0.5)
    p = jax.nn.softmax(s, axis=-1).astype(v.dtype)
    o = jnp.einsum('bhlm,bmhd->blhd', p, v)
    return o.reshape(B, L, D_XATTN)


def encoder_layer(x, mem, norm_g, mem_norm_g, w_in, hy_short_w, hy_short_b,
                  hy_f_w1, hy_f_b1, hy_f_fr1, hy_f_w2, hy_f_b2, hy_f_fr2,
                  hy_f_w3, hy_f_b3, hy_f_fr3, hy_f_w4, hy_skip,
                  cf_dw_w, cf_dw_b, cf_ln_g, cf_ln_b, xa_w_kv, w_out):
    L = x.shape[1]
    h = rmsnorm(x, norm_g)
    p = h @ w_in
    hy_u, hy_z, cf_a, cf_z, xa_q, xa_z = jnp.split(p, SPLITS, axis=-1)
    k2 = hyena_two_sided_filter(L, hy_f_w1, hy_f_b1, hy_f_fr1, hy_f_w2, hy_f_b2, hy_f_fr2,
                                hy_f_w3, hy_f_b3, hy_f_fr3, hy_f_w4)
    y_hy = hyena_branch(hy_u, hy_short_w, hy_short_b, k2, hy_skip) * jax.nn.silu(hy_z)
    y_cf = conformer_branch(cf_a, cf_dw_w, cf_dw_b, cf_ln_g, cf_ln_b) * jax.nn.silu(cf_z)
    y_xa = memory_attention(xa_q, rmsnorm(mem, mem_norm_g), xa_w_kv) * jax.nn.silu(xa_z)
    mix = jnp.concatenate([y_hy, y_cf, y_xa], axis=-1)
    return x + mix @ w_out


def run_trunk(x, mem, layer_params, final_g):
    for l in range(DEPTH):
        x = encoder_layer(x, mem, *[prm[l] for prm in layer_params])
    return rmsnorm(x, final_g)


def setup_inputs(seed: int = 0) -> dict:
    key = jax.random.key(seed)
    ks = iter(jax.random.split(key, 40))

    def nrm(shape, scale):
        return jax.random.normal(next(ks), shape, jnp.float32) * scale

    def gain(shape):
        return 1.0 + nrm(shape, 0.02)

    return {
        "x_prompt": nrm((BATCH, SEQ, D_MODEL), 1.0),
        "x_sample": nrm((DEC_BATCH, DEC_SEQ, D_MODEL), 1.0),
        "mem_prompt": nrm((BATCH, N_MEM, D_MODEL), 1.0),
        "mem_sample": nrm((DEC_BATCH, N_MEM, D_MODEL), 1.0),
        "norm_g": gain((DEPTH, D_MODEL)),
        "mem_norm_g": gain((DEPTH, D_MODEL)),
        "w_in": nrm((DEPTH, D_MODEL, D_IN), D_MODEL ** -0.5),
        "hy_short_w": nrm((DEPTH, SHORT_K, P_HY_U), SHORT_K ** -0.5),
        "hy_short_b": nrm((DEPTH, P_HY_U), 0.01),
        "hy_f_w1": nrm((DEPTH, FILT_EMB, FILT_HIDDEN), FILT_EMB ** -0.5),
        "hy_f_b1": nrm((DEPTH, FILT_HIDDEN), 0.1),
        "hy_f_fr1": 1.0 + nrm((DEPTH, FILT_HIDDEN), 0.05),
        "hy_f_w2": nrm((DEPTH, FILT_HIDDEN, FILT_HIDDEN), FILT_HIDDEN ** -0.5),
        "hy_f_b2": nrm((DEPTH, FILT_HIDDEN), 0.1),
        "hy_f_fr2": 1.0 + nrm((DEPTH, FILT_HIDDEN), 0.05),
        "hy_f_w3": nrm((DEPTH, FILT_HIDDEN, FILT_HIDDEN), FILT_HIDDEN ** -0.5),
        "hy_f_b3": nrm((DEPTH, FILT_HIDDEN), 0.1),
        "hy_f_fr3": 1.0 + nrm((DEPTH, FILT_HIDDEN), 0.05),
        "hy_f_w4": nrm((DEPTH, FILT_HIDDEN, 2 * D_HYENA), 0.03 * FILT_HIDDEN ** -0.5),
        "hy_skip": nrm((DEPTH, D_HYENA), 0.1),
        "cf_dw_w": nrm((DEPTH, CONF_K, D_CONF), CONF_K ** -0.5),
        "cf_dw_b": nrm((DEPTH, D_CONF), 0.01),
        "cf_ln_g": gain((DEPTH, D_CONF)),
        "cf_ln_b": nrm((DEPTH, D_CONF), 0.01),
        "xa_w_kv": nrm((DEPTH, D_MODEL, 2 * D_XATTN), D_MODEL ** -0.5),
        "w_out": nrm((DEPTH, D_MIX, D_MODEL), D_MIX ** -0.5),
        "final_g": gain((D_MODEL,)),
    }


def reference(x_prompt, x_sample, mem_prompt, mem_sample, norm_g, mem_norm_g, w_in,
              hy_short_w, hy_short_b, hy_f_w1, hy_f_b1, hy_f_fr1, hy_f_w2, hy_f_b2, hy_f_fr2,
              hy_f_w3, hy_f_b3, hy_f_fr3, hy_f_w4, hy_skip, cf_dw_w, cf_dw_b, cf_ln_g, cf_ln_b,
              xa_w_kv, w_out, final_g):
    layer_params = (norm_g, mem_norm_g, w_in, hy_short_w, hy_short_b,
                    hy_f_w1, hy_f_b1, hy_f_fr1, hy_f_w2, hy_f_b2, hy_f_fr2,
                    hy_f_w3, hy_f_b3, hy_f_fr3, hy_f_w4, hy_skip,
                    cf_dw_w, cf_dw_b, cf_ln_g, cf_ln_b, xa_w_kv, w_out)
    y_prompt = run_trunk(x_prompt, mem_prompt, layer_params, final_g)
    y_sample = run_trunk(x_sample, mem_sample, layer_params, final_g)
    return (y_prompt, y_sample)
```

```python
import math
from contextlib import ExitStack
import numpy as np
import ml_dtypes
import concourse.bass as bass
import concourse.mybir as mybir
from concourse.bass_utils import run_bass_kernel_spmd

F32 = mybir.dt.float32
BF16 = mybir.dt.bfloat16
AF = mybir.ActivationFunctionType
ALU = mybir.AluOpType

L = 8192
D = 1024
DIN = 3328
NMEM = 256
NT = L // 512
EPS = 1e-6
NSP = 160
N_CORES = 8


class Res:
    __slots__ = ("w", "r")

    def __init__(self):
        self.w = None
        self.r = {}


class Eng:
    def __init__(self, eng, sid):
        self.eng = eng
        self.sid = sid
        self.cnt = 0
        self.waited = {}


class K:
    def __init__(self, nc, nl, nseq, debug):
        self.nc = nc
        self.nl = nl
        self.nseq = nseq
        self.debug = debug
        self.sems = []
        self.es = ExitStack()

        def newsem(name):
            s = self.es.enter_context(nc.semaphore(name))
            self.sems.append(s)
            return len(self.sems) - 1

        self.PE = Eng(nc.tensor, newsem("s_pe"))
        self.ACT = Eng(nc.scalar, newsem("s_act"))
        self.DVE = Eng(nc.vector, newsem("s_dve"))
        self.POOL = Eng(nc.gpsimd, newsem("s_pool"))
        self.SP = Eng(nc.sync, None)
        self.engs = [self.PE, self.ACT, self.DVE, self.POOL, self.SP]
        self.ND = 40
        self.dma_sid = [newsem("s_dma%d" % i) for i in range(self.ND)]
        self.dma_uses = [0] * self.ND
        self.dma_next = 0

    def _need(self, r, w, extra=None):
        need = dict(extra or {})

        def add(sid, val):
            if need.get(sid, 0) < val:
                need[sid] = val

        for x in r:
            if x.w is not None:
                add(*x.w)
        for x in w:
            if x.w is not None:
                add(*x.w)
            for sid, val in x.r.items():
                add(sid, val)
        return need

    def _waits(self, E, need):
        for sid, val in need.items():
            if E.waited.get(sid, 0) < val:
                E.eng.wait_ge(self.sems[sid], val)
                E.waited[sid] = val

    def op(self, E, fn, r=(), w=()):
        self._waits(E, self._need(r, w))
        ins = fn()
        E.cnt += 1
        ins.then_inc(self.sems[E.sid], 1)
        for x in r:
            x.r[E.sid] = E.cnt
        for x in w:
            x.w = (E.sid, E.cnt)
            x.r = {}

    def mmg(self, mms, r=(), w=()):
        E = self.PE
        self._waits(E, self._need(r, w))
        n = len(mms)
        for i, (o, a, b) in enumerate(mms):
            ins = self.nc.tensor.matmul(o, a, b, start=(i == 0), stop=(i == n - 1))
        E.cnt += 1
        ins.then_inc(self.sems[E.sid], 1)
        for x in r:
            x.r[E.sid] = E.cnt
        for x in w:
            x.w = (E.sid, E.cnt)
            x.r = {}

    def dma(self, out, in_, r=(), w=(), Q=None):
        Q = Q or self.SP
        k = self.dma_next
        self.dma_next = (k + 1) % self.ND
        sid = self.dma_sid[k]
        j = self.dma_uses[k]
        self._waits(Q, self._need(r, w, {sid: 16 * j} if j else None))
        Q.eng.dma_start(out=out, in_=in_).then_inc(self.sems[sid], 16)
        self.dma_uses[k] = j + 1
        val = 16 * (j + 1)
        for x in r:
            x.r[sid] = val
        for x in w:
            x.w = (sid, val)
            x.r = {}

    def barrier(self):
        tg = {}
        for E in self.engs:
            if E.sid is not None and E.cnt:
                tg[E.sid] = E.cnt
        for k in range(self.ND):
            if self.dma_uses[k]:
                tg[self.dma_sid[k]] = 16 * self.dma_uses[k]
        for E in self.engs:
            self._waits(E, tg)

    def sb(self, st, name, shape, dt):
        self.uid = getattr(self, "uid", 0) + 1
        return st.enter_context(self.nc.sbuf_tensor("%s_%d" % (name, self.uid), list(shape), dt))

    def ps(self, st, name, shape):
        self.uid = getattr(self, "uid", 0) + 1
        return st.enter_context(self.nc.psum_tensor("%s_%d" % (name, self.uid), list(shape), F32))


def build_program(nl=4, nseq=2, debug=False):
    nc = bass.Bass("TRN2", target_bir_lowering=False)
    k = K(nc, nl, nseq, debug)
    PE, ACT, DVE, POOL, SP = k.PE, k.ACT, k.DVE, k.POOL, k.SP
    V, S, G, T = nc.vector, nc.scalar, nc.gpsimd, nc.tensor
    op, mmg, dma = k.op, k.mmg, k.dma

    def din(name, shape, dt=F32):
        return nc.dram_tensor(name, list(shape), dt, kind="ExternalInput").ap()

    skind = "ExternalOutput" if debug else "Internal"

    def dscr(name, shape, dt):
        return nc.dram_tensor(name, list(shape), dt, kind=skind).ap()

    xT = din("xT", [nseq, D, L])
    memT = din("memT", [nseq, D, NMEM])
    w_in = din("w_in", [nl, D, DIN])
    w_kv = din("w_kv", [nl, D, 512])
    w_out = din("w_out", [nl, D, D])
    fw1 = din("fw1", [nl, 33, 64])
    fw2 = din("fw2", [nl, 64, 64])
    fw3 = din("fw3", [nl, 64, 64])
    fw4 = din("fw4", [nl, 64, 1024])
    spar = din("spar", [nl, 128, NSP])
    cfft = din("cfft", [128, 1156], BF16)
    ctw = din("ctw", [128, 2, 128])
    cz = din("cz", [2, 33, L])
    ctp = din("ctp", [2, L])
    cnd = din("cnd", [128, 4])
    cid = din("cid", [128, 128], BF16)
    yT = nc.dram_tensor("yT", [nseq, D, L], F32, kind="ExternalOutput").ap()

    pscr = dscr("pscr", [22 * 128, L], BF16)
    mixs = dscr("mixs", [D, L], BF16)
    vx1s = dscr("vx1s", [512, L], BF16)
    g0s = dscr("g0s", [512, L], BF16)
    yscr = dscr("yscr", [512, L], F32)
    kfil = dscr("kfil", [512, 2 * L], BF16)
    kfs = dscr("kfs", [128, 2, 512, 65], BF16)

    top = ExitStack()
    fftc = k.sb(top, "fftc", [128, 1156], BF16)
    twt = k.sb(top, "twt", [128, 2, 128], F32)
    twb = k.sb(top, "twb", [128, 2, 128], BF16)
    ident = k.sb(top, "ident", [128, 128], BF16)
    ones_d = k.sb(top, "ones_d", [128, 128], BF16)
    ones_c = k.sb(top, "ones_c", [128, 128], BF16)
    ind = k.sb(top, "ind", [128, 2, 128], BF16)
    ndl = k.sb(top, "ndl", [128, 4], F32)
    sp = k.sb(top, "sp", [128, NSP], F32)
    frb = k.sb(top, "frb", [128, 3], F32)
    epsb = k.sb(top, "epsb", [128, 1], F32)
    win = k.sb(top, "win", [128, 8, DIN], BF16)
    wkv = k.sb(top, "wkv", [128, 8, 512], BF16)
    r_win, r_wkv = Res(), Res()

    def load_layer_weights(l):
        dma(wkv[:], w_kv[l].rearrange("(kt p) c -> p kt c", p=128), w=[r_wkv], Q=POOL)
        for kt in range(8):
            dma(win[:, kt, :], w_in[l][kt * 128:(kt + 1) * 128, :], w=[r_win], Q=POOL)
    Rc = Res()
    Rsp = Res()
    dma(fftc[:], cfft, w=[Rc])
    dma(twt[:], ctw, w=[Rc])
    dma(ident[:], cid, w=[Rc])
    dma(ndl[:], cnd, w=[Rc])
    op(DVE, lambda: V.tensor_copy(out=twb[:], in_=twt[:]), r=[Rc], w=[Rc])
    op(DVE, lambda: V.memset(epsb[:], EPS), w=[Rc])
    op(DVE, lambda: V.memset(ones_d[:], 1.0 / 1024.0), w=[Rc])
    op(DVE, lambda: V.memset(ones_c[:], 1.0 / 256.0), w=[Rc])
    op(DVE, lambda: V.memset(ind[:], 0.0), w=[Rc])
    op(DVE, lambda: V.memset(ind[:, 0, 0:64], 1.0), w=[Rc])
    op(DVE, lambda: V.memset(ind[:, 1, 64:128], 1.0), w=[Rc])
    k.barrier()
    Cm = fftc[:, 0:128]
    NSm = fftc[:, 128:256]
    Sm = fftc[:, 256:384]
    P1h = fftc[:, 384:514]
    C4w = fftc[:, 514:578]
    NS4w = fftc[:, 578:642]
    P2 = fftc[:, 644:900]
    P3 = fftc[:, 900:1156]

    C_NG, C_MNG, C_SHW, C_SHB, C_SKIP, C_DWW, C_DWB, C_LNG, C_LNB, C_FG, C_FB = 0, 8, 16, 52, 64, 68, 130, 132, 134, 136, 144

    KH = 65
    SBC = 8
    NSUB = 4

    def fft_pipeline(nsb, Kdim, load_in, is_filter, store_out=None):
        st = ExitStack()
        R = lambda n: [Res() for _ in range(n)]
        xin = [k.sb(st, "xin%d" % i, [128, SBC, 128], BF16) for i in range(2)]
        tA = [k.sb(st, "tA%d" % i, [128, SBC, 2, KH], BF16) for i in range(2)]
        mm = [k.sb(st, "cm%d" % i, [128, SBC, KH], BF16) for i in range(4)]
        Bb = [k.sb(st, "Bb%d" % i, [128, 2, SBC, KH], BF16) for i in range(2)]
        tX = [k.sb(st, "tX%d" % i, [128, 2, SBC, KH], BF16) for i in range(2)]
        psA = [k.ps(st, "psA%d" % i, [128, 512]) for i in range(2)]
        psX = [k.ps(st, "psX%d" % i, [128, 512]) for i in range(2)]
        r_xin, r_B, r_psA, r_psX = R(2), R(2), R(2), R(2)
        r_tA = [R(NSUB), R(NSUB)]
        r_tX = [R(NSUB), R(NSUB)]
        r_mm = R(4)
        if not is_filter:
            kft = [k.sb(st, "kft%d" % i, [128, 2, SBC, KH], BF16) for i in range(2)]
            Yb = [k.sb(st, "Yb%d" % i, [128, 2, SBC, KH], BF16) for i in range(2)]
            tC = [k.sb(st, "tC%d" % i, [KH, SBC, 2, 128], BF16) for i in range(2)]
            Dd = [k.sb(st, "Dd%d" % i, [KH, SBC, 2, 128], BF16) for i in range(2)]
            mT = [k.sb(st, "cmT%d" % i, [KH, SBC, 128], BF16) for i in range(4)]
            yb = [k.sb(st, "yb%d" % i, [64, SBC, 128], F32) for i in range(2)]
            psC = [k.ps(st, "psC%d" % i, [128, 512]) for i in range(2)]
            psY = k.ps(st, "psY", [128, 512])
            r_kft, r_Y, r_D, r_psC = R(2), R(2), R(2), R(2)
            r_tC = [R(NSUB), R(NSUB)]
            r_yb = [R(2), R(2)]
            r_psY = Res()
            r_mT = R(4)
            TreT = twb[0:KH, 0, :].unsqueeze(1).to_broadcast([KH, SBC, 128])
            TimT = twb[0:KH, 1, :].unsqueeze(1).to_broadcast([KH, SBC, 128])
        TreH = twb[:, 0, 0:KH].unsqueeze(1).to_broadcast([128, SBC, KH])
        TimH = twb[:, 1, 0:KH].unsqueeze(1).to_broadcast([128, SBC, KH])

        def cmul(a_re, a_im, b_re, b_im, o_re, o_im, conj, rs, ro, mm=mm, r_mm=r_mm):
            op(DVE, lambda: V.tensor_tensor(out=mm[0][:], in0=a_re, in1=b_re, op=ALU.mult), r=rs, w=[r_mm[0]])
            op(DVE, lambda: V.tensor_tensor(out=mm[1][:], in0=a_im, in1=b_im, op=ALU.mult), r=rs, w=[r_mm[1]])
            op(DVE, lambda: V.tensor_tensor(out=mm[2][:], in0=a_re, in1=b_im, op=ALU.mult), r=rs, w=[r_mm[2]])
            op(DVE, lambda: V.tensor_tensor(out=mm[3][:], in0=a_im, in1=b_re, op=ALU.mult), r=rs, w=[r_mm[3]])
            if not conj:
                op(DVE, lambda: V.tensor_tensor(out=o_re, in0=mm[0][:], in1=mm[1][:], op=ALU.subtract), r=[r_mm[0], r_mm[1]], w=[ro])
                op(DVE, lambda: V.tensor_tensor(out=o_im, in0=mm[2][:], in1=mm[3][:], op=ALU.add), r=[r_mm[2], r_mm[3]], w=[ro])
            else:
                op(DVE, lambda: V.tensor_tensor(out=o_re, in0=mm[0][:], in1=mm[1][:], op=ALU.add), r=[r_mm[0], r_mm[1]], w=[ro])
                op(DVE, lambda: V.tensor_tensor(out=o_im, in0=mm[3][:], in1=mm[2][:], op=ALU.subtract), r=[r_mm[2], r_mm[3]], w=[ro])

        def s_load(b):
            load_in(b, xin[b % 2], r_xin[b % 2])

        def s_S1(b):
            i = b % 2
            for sub in range(NSUB):
                s_ = sub % 2
                pa = psA[s_][:, 0:2 * 2 * KH].rearrange("p (c n) -> p c n", c=2)
                k._waits(PE, k._need([], [r_psA[s_]]))
                for cl in range(2):
                    c = sub * 2 + cl
                    mmg([(pa[:, cl, :], xin[i][0:Kdim, c, :], P1h[0:Kdim, :])], r=[r_xin[i], Rc], w=[r_psA[s_]] if cl == 1 else [])
                op(ACT, lambda sub=sub, pa=pa: S.copy(out=tA[i][:, sub * 2:sub * 2 + 2].rearrange("p c r k -> p c (r k)"), in_=pa),
                   r=[r_psA[s_]], w=[r_tA[i][sub]])

        def s_TW1(b):
            i = b % 2
            cmul(tA[i][:, :, 0, :], tA[i][:, :, 1, :], TreH, TimH, Bb[i][:, 0], Bb[i][:, 1], False, r_tA[i] + [Rc], r_B[i])

        def table_stage(src, r_src, ps, r_ps, dst, r_dst, i, fwd):
            for sub in range(NSUB):
                s_ = sub % 2
                px = ps[s_][:, 0:2 * 2 * KH].rearrange("p (r n) -> p r n", r=2)
                bre = src[i][:, 0, sub * 2:sub * 2 + 2, :].rearrange("p c k -> p (c k)")
                bim = src[i][:, 1, sub * 2:sub * 2 + 2, :].rearrange("p c k -> p (c k)")
                k._waits(PE, k._need([], [r_ps[s_]]))
                if fwd:
                    mmg([(px[:, 0, :], Cm, bre), (px[:, 0, :], Sm, bim)], r=[r_src[i], Rc], w=[])
                    mmg([(px[:, 1, :], NSm, bre), (px[:, 1, :], Cm, bim)], r=[r_src[i], Rc], w=[r_ps[s_]])
                else:
                    mmg([(px[:, 0, :], Cm, bre), (px[:, 0, :], NSm, bim)], r=[r_src[i], Rc], w=[])
                    mmg([(px[:, 1, :], Sm, bre), (px[:, 1, :], Cm, bim)], r=[r_src[i], Rc], w=[r_ps[s_]])
                op(ACT, lambda sub=sub, px=px: S.copy(out=dst[i][:, :, sub * 2:sub * 2 + 2, :],
                                                      in_=px.rearrange("p r (c k) -> p r c k", c=2)),
                   r=[r_ps[s_]], w=[r_dst[i][sub]])

        def s_S2(b):
            i = b % 2
            if not is_filter:
                dma(kft[i][:], kfs[:, :, b * SBC:(b + 1) * SBC, :], w=[r_kft[i]])
            table_stage(Bb, r_B, psX, r_psX, tX, r_tX, i, True)

        def s_PM(b):
            i = b % 2
            if is_filter:
                dma(kfs[:, :, b * SBC:(b + 1) * SBC, :], tX[i][:], r=r_tX[i])
            else:
                cmul(tX[i][:, 0], tX[i][:, 1], kft[i][:, 0], kft[i][:, 1], Yb[i][:, 0], Yb[i][:, 1], False, r_tX[i] + [r_kft[i]], r_Y[i])

        def s_S3(b):
            i = b % 2
            for sub in range(NSUB):
                s_ = sub % 2
                pc_ = psC[s_][0:KH, :].rearrange("p (c n) -> p c n", c=2)
                k._waits(PE, k._need([], [r_psC[s_]]))
                for cl in range(2):
                    c = sub * 2 + cl
                    mmg([(pc_[:, cl, :], Yb[i][:, 0, c, :], P2), (pc_[:, cl, :], Yb[i][:, 1, c, :], P3)],
                        r=[r_Y[i], Rc], w=[r_psC[s_]] if cl == 1 else [])
                op(ACT, lambda sub=sub, s_=s_: S.copy(out=tC[i][:, sub * 2:sub * 2 + 2].rearrange("p c r n -> p (c r n)"), in_=psC[s_][0:KH, :]),
                   r=[r_psC[s_]], w=[r_tC[i][sub]])

        def s_TW2(b):
            i = b % 2
            cmul(tC[i][:, :, 0, :], tC[i][:, :, 1, :], TreT, TimT, Dd[i][:, :, 0, :], Dd[i][:, :, 1, :], True, r_tC[i] + [Rc], r_D[i],
                 mm=mT, r_mm=r_mT)

        def s_S4(b):
            i = b % 2
            for h in range(2):
                mmg([(psY[0:64, :], C4w[0:KH, :], Dd[i][:, h * 4:(h + 1) * 4, 0, :]),
                     (psY[0:64, :], NS4w[0:KH, :], Dd[i][:, h * 4:(h + 1) * 4, 1, :])],
                    r=[r_D[i], Rc], w=[r_psY])
                op(ACT, lambda h=h: S.copy(out=yb[i][:, h * 4:(h + 1) * 4, :].rearrange("p c k -> p (c k)"), in_=psY[0:64, :]),
                   r=[r_psY], w=[r_yb[i][h]])
            store_out(b, yb[i], r_yb[i])

        stages = [s_load, s_S1, s_TW1, s_S2, s_PM]
        if not is_filter:
            stages += [s_S3, s_TW2, s_S4]
        ns = len(stages)
        for step in range(nsb + ns - 1):
            for si in range(ns - 1, -1, -1):
                b = step - si
                if 0 <= b < nsb:
                    stages[si](b)
        k.barrier()
        st.close()

    def gen_filter(l):
        st = ExitStack()
        w1 = k.sb(st, "fw1", [66, 128], F32)
        w2 = k.sb(st, "fw2", [128, 128], F32)
        w3 = k.sb(st, "fw3", [128, 128], F32)
        w4 = k.sb(st, "fw4", [128, 1024], F32)
        zt = [k.sb(st, "zt%d" % i, [66, 512], F32) for i in range(2)]
        tb = [k.sb(st, "tb%d" % i, [128, 2, 512], F32) for i in range(2)]
        ha = [k.sb(st, "ha%d" % i, [128, 512], F32) for i in range(3)]
        hb = [k.sb(st, "hb%d" % i, [128, 512], F32) for i in range(3)]
        hc = [k.sb(st, "hc%d" % i, [128, 512], F32) for i in range(3)]
        hh = [[k.sb(st, "hh%d%d" % (j, i), [128, 512], F32) for i in range(2)] for j in range(3)]
        win = [k.sb(st, "fwin%d" % i, [128, 512], F32) for i in range(2)]
        kst = [k.sb(st, "kst%d" % i, [128, 2, 4, 512], BF16) for i in range(2)]
        psh = [k.ps(st, "psh%d" % i, [128, 512]) for i in range(3)]
        psk = [k.ps(st, "psk%d" % i, [128, 512]) for i in range(2)]
        R3 = lambda: [Res(), Res(), Res()]
        Rw = Res()
        r_ha, r_hb, r_hc, r_psh = R3(), R3(), R3(), R3()
        r_zt, r_tb, r_kst, r_psk, r_win = [Res(), Res()], [Res(), Res()], [Res(), Res()], [Res(), Res()], [Res(), Res()]
        r_hh = [[Res(), Res()] for _ in range(3)]
        op(DVE, lambda: V.memset(w1[:], 0.0), w=[Rw])
        op(DVE, lambda: V.memset(w2[:], 0.0), w=[Rw])
        op(DVE, lambda: V.memset(w3[:], 0.0), w=[Rw])
        for hf in range(2):
            dma(w1[hf * 33:(hf + 1) * 33, hf * 64:(hf + 1) * 64], fw1[l], w=[Rw])
            dma(w2[hf * 64:(hf + 1) * 64, hf * 64:(hf + 1) * 64], fw2[l], w=[Rw])
            dma(w3[hf * 64:(hf + 1) * 64, hf * 64:(hf + 1) * 64], fw3[l], w=[Rw])
            dma(w4[hf * 64:(hf + 1) * 64, :], fw4[l], w=[Rw])
        for j in range(3):
            op(DVE, lambda j=j: V.tensor_tensor(out=frb[:, j:j + 1], in0=sp[:, C_FB + 2 * j:C_FB + 2 * j + 1],
                                                in1=sp[:, C_FB + 2 * j + 1:C_FB + 2 * j + 2], op=ALU.mult), r=[Rsp], w=[Rw])
        wl = [w1, w2, w3]

        def f_load(ch):
            i = ch % 2
            Q = []
            for hf in range(2):
                Q.append(lambda hf=hf: dma(zt[i][hf * 33:(hf + 1) * 33, :], cz[hf, :, ch * 512:(ch + 1) * 512], w=[r_zt[i]]))
            return Q

        def f_layer(j):
            def f(ch):
                i = ch % 2
                src = zt[i][:] if j == 0 else hh[j - 1][i][:]
                rsrc = r_zt[i] if j == 0 else r_hh[j - 1][i]
                Q = []
                Q.append(lambda: mmg([(psh[j][:], wl[j][:], src)], r=[Rw, rsrc], w=[r_psh[j]]))
                Q.append(lambda: op(ACT, lambda: S.activation(out=ha[j][:], in_=psh[j][:], func=AF.Identity, bias=frb[:, j:j + 1],
                                                              scale=sp[:, C_FB + 2 * j + 1:C_FB + 2 * j + 2]), r=[r_psh[j], Rw, Rsp], w=[r_ha[j]]))
                Q.append(lambda: op(DVE, lambda: V.tensor_scalar(out=hb[j][:], in0=ha[j][:], scalar1=math.pi, scalar2=-2.0 * math.pi,
                                                                 op0=ALU.is_gt, op1=ALU.mult), r=[r_ha[j]], w=[r_hb[j]]))
                Q.append(lambda: op(DVE, lambda: V.tensor_scalar(out=hc[j][:], in0=ha[j][:], scalar1=-math.pi, scalar2=2.0 * math.pi,
                                                                 op0=ALU.is_lt, op1=ALU.mult), r=[r_ha[j]], w=[r_hc[j]]))
                Q.append(lambda: op(DVE, lambda: V.tensor_tensor(out=hb[j][:], in0=hb[j][:], in1=hc[j][:], op=ALU.add), r=[r_hb[j], r_hc[j]], w=[r_hb[j]]))
                Q.append(lambda: op(DVE, lambda: V.tensor_tensor(out=hb[j][:], in0=hb[j][:], in1=ha[j][:], op=ALU.add), r=[r_hb[j], r_ha[j]], w=[r_hb[j]]))
                Q.append(lambda: op(ACT, lambda: S.activation(out=hh[j][i][:], in_=hb[j][:], func=AF.Sin), r=[r_hb[j]], w=[r_hh[j][i]]))
                if j == 2:
                    for hf in range(2):
                        Q.append(lambda hf=hf: dma(tb[i][:, hf, :], ctp[hf, ch * 512:(ch + 1) * 512].partition_broadcast(128), w=[r_tb[i]]))
                return Q
            return f

        def f_out(ch):
            i = ch % 2
            Q = []
            n = 0
            for hf in range(2):
                for j4 in range(4):
                    pi = n % 2
                    n += 1
                    Q.append(lambda hf=hf, j4=j4, pi=pi: mmg(
                        [(psk[pi][:], w4[hf * 64:(hf + 1) * 64, hf * 512 + j4 * 128: hf * 512 + (j4 + 1) * 128], hh[2][i][hf * 64:(hf + 1) * 64, :])],
                        r=[Rw, r_hh[2][i]], w=[r_psk[pi]]))
                    Q.append(lambda hf=hf, j4=j4, pi=pi: op(ACT, lambda: S.activation(out=win[pi][:], in_=tb[i][:, hf, :], func=AF.Exp, scale=ndl[:, j4:j4 + 1]),
                                                            r=[r_tb[i], Rc], w=[r_win[pi]]))
                    Q.append(lambda hf=hf, j4=j4, pi=pi: op(DVE, lambda: V.tensor_tensor(out=kst[i][:, hf, j4, :], in0=psk[pi][:], in1=win[pi][:], op=ALU.mult),
                                                            r=[r_psk[pi], r_win[pi]], w=[r_kst[i]]))
            if ch == 0:
                Q.append(lambda: op(DVE, lambda: V.memset(kst[i][:, 1, :, 0:1], 0.0), w=[r_kst[i]]))
            for hf in range(2):
                c0 = hf * L + ch * 512
                Q.append(lambda hf=hf, c0=c0: dma(kfil.rearrange("(j p) t -> p j t", p=128)[:, :, c0:c0 + 512], kst[i][:, hf], r=[r_kst[i]]))
            return Q

        fst = [f_load, f_layer(0), f_layer(1), f_layer(2), f_out]
        for step in range(NT + len(fst) - 1):
            qs = []
            for si in range(len(fst) - 1, -1, -1):
                ch = step - si
                if 0 <= ch < NT:
                    qs.append(fst[si](ch))
            while any(qs):
                for q in qs:
                    take = 3 if len(q) > 12 else 1
                    for _ in range(take):
                        if q:
                            q.pop(0)()
        k.barrier()
        st.close()
        kv = kfil.rearrange("c (a b) -> a c b", b=128)

        def load_f(b, dst, rdst):
            dma(dst[:], kv[:, b * SBC:(b + 1) * SBC, :], w=[rdst])

        fft_pipeline(512 // SBC, 128, load_f, True)

    def load_cast(dst, src_l, ncols, gcol, wst, r_wst, r_dst):
        srcv = src_l.rearrange("(kt p) c -> p kt c", p=128)
        half = 1664
        it = 0
        for kt in range(8):
            for c0 in range(0, ncols, half):
                cw = min(half, ncols - c0)
                i = it % 2
                it += 1
                dma(wst[i][:, 0:cw], srcv[:, kt, c0:c0 + cw], w=[r_wst[i]])
                if gcol is None:
                    if it % 2:
                        op(ACT, lambda i=i, kt=kt, c0=c0, cw=cw: S.copy(out=dst[:, kt, c0:c0 + cw], in_=wst[i][:, 0:cw]),
                           r=[r_wst[i]], w=[r_dst])
                    else:
                        op(DVE, lambda i=i, kt=kt, c0=c0, cw=cw: V.tensor_copy(out=dst[:, kt, c0:c0 + cw], in_=wst[i][:, 0:cw]),
                           r=[r_wst[i]], w=[r_dst])
                else:
                    if it % 2:
                        op(ACT, lambda i=i, kt=kt, c0=c0, cw=cw: S.activation(out=dst[:, kt, c0:c0 + cw], in_=wst[i][:, 0:cw],
                                                                              func=AF.Identity, scale=sp[:, gcol + kt:gcol + kt + 1]),
                           r=[r_wst[i], Rsp], w=[r_dst])
                    else:
                        op(DVE, lambda i=i, kt=kt, c0=c0, cw=cw: V.tensor_scalar(out=dst[:, kt, c0:c0 + cw], in0=wst[i][:, 0:cw],
                                                                                 scalar1=sp[:, gcol + kt:gcol + kt + 1], scalar2=None,
                                                                                 op0=ALU.mult),
                           r=[r_wst[i], Rsp], w=[r_dst])

    kT = k.sb(top, "kT", [128, 2, 256], BF16)
    vpad = k.sb(top, "vpad", [128, 4, 2, 128], BF16)
    r_kT, r_vpad = Res(), Res()

    def kv_prep(l, s, with_barrier):
        st2 = ExitStack()
        memx = k.sb(st2, "memx", [128, 8, 256], F32)
        msq = k.sb(st2, "msq", [128, 8, 256], BF16)
        memn = k.sb(st2, "memn", [128, 8, 256], BF16)
        mrs = k.sb(st2, "mrs", [128, 256], F32)
        pkv = k.ps(st2, "pkv", [128, 256])
        r_memx, r_msq, r_memn, r_mrs, r_pkv = Res(), Res(), Res(), Res(), Res()
        dma(memx[:], memT[s].rearrange("(kt p) m -> p kt m", p=128), w=[r_memx])
        op(ACT, lambda: S.activation(out=msq[:], in_=memx[:], func=AF.Square), r=[r_memx], w=[r_msq])
        mmg([(pkv[:], ones_d[:], msq[:, kt, :]) for kt in range(8)], r=[r_msq, Rc], w=[r_pkv])
        op(ACT, lambda: S.activation(out=mrs[:], in_=pkv[:], func=AF.Ln, bias=epsb[:, 0:1], scale=1.0), r=[r_pkv, Rc], w=[r_mrs])
        op(ACT, lambda: S.activation(out=mrs[:], in_=mrs[:], func=AF.Exp, scale=-0.5), r=[r_mrs], w=[r_mrs])
        for kt in range(8):
            op(DVE, lambda kt=kt: V.scalar_tensor_tensor(out=memn[:, kt, :], in0=memx[:, kt, :], scalar=sp[:, C_MNG + kt:C_MNG + kt + 1],
                                                         in1=mrs[:], op0=ALU.mult, op1=ALU.mult), r=[r_memx, r_mrs, Rsp], w=[r_memn])
        for g in range(2):
            mmg([(pkv[:], wkv[:, kt, g * 128:(g + 1) * 128], memn[:, kt, :]) for kt in range(8)], r=[r_wkv, r_memn], w=[r_pkv])
            op(DVE, lambda g=g: V.tensor_copy(out=kT[:, g, :], in_=pkv[:]), r=[r_pkv], w=[r_kT])
        op(POOL, lambda: G.memset(vpad[:], 0.0), w=[r_vpad])
        for mt in range(2):
            mmg([(pkv[:], memn[:, kt, mt * 128:(mt + 1) * 128], wkv[:, kt, 256:512]) for kt in range(8)], r=[r_wkv, r_memn], w=[r_pkv])
            for h in range(4):
                hb = (h % 2) * 64
                op(DVE, lambda h=h, hb=hb, mt=mt: V.tensor_copy(out=vpad[:, h, mt, hb:hb + 64], in_=pkv[:, h * 64:(h + 1) * 64]),
                   r=[r_pkv], w=[r_vpad])
        return st2

    def phase_a(l, s, xsrc):
        st = ExitStack()
        xt = [k.sb(st, "xt%d" % i, [128, 8, 512], F32) for i in range(2)]
        sq = k.sb(st, "sq", [128, 8, 512], BF16)
        hh = [k.sb(st, "hh%d" % i, [128, 8, 512], BF16) for i in range(2)]
        stage = k.sb(st, "stage", [128, 4, 512], BF16)
        sg = k.sb(st, "sg", [128, 2, 512], F32)
        r_sg = [Res(), Res()]
        GORD = list(range(16)) + [18, 19, 16, 17] + list(range(20, 26))
        hst = k.sb(st, "hst", [128, 12, 515], BF16)
        zst = k.sb(st, "zst", [128, 4, 513], BF16)
        tm1 = k.sb(st, "tm1", [128, 4, 513], F32)
        tm2 = k.sb(st, "tm2", [128, 4, 513], F32)
        vxo = k.sb(st, "vxo", [128, 4, 513], BF16)
        g0o = k.sb(st, "g0o", [128, 4, 513], BF16)
        qT = [k.sb(st, "qT%d" % i, [128, 2, 512], BF16) for i in range(2)]
        zs = [k.sb(st, "zs%d" % i, [128, 2, 512], BF16) for i in range(2)]
        pT = k.sb(st, "pT", [128, 4, 512], BF16)
        rstd = k.sb(st, "rstd", [128, 512], F32)
        rden = k.sb(st, "rden", [128, 512], F32)
        ot = k.sb(st, "ot", [128, 512], F32)
        yxa = [k.sb(st, "yxa%d" % i, [128, 2, 512], BF16) for i in range(2)]
        pj = [k.ps(st, "pj%d" % i, [128, 512]) for i in range(4)]
        sc = [k.ps(st, "sc_%d" % i, [128, 512]) for i in range(2)]
        pv = k.ps(st, "pv", [128, 512])
        dn = k.ps(st, "dn", [128, 512])
        nrm = dn
        r_xt, r_pj, r_sc = [Res(), Res()], [Res() for _ in range(4)], [Res(), Res()]
        r_rstd, r_rden, r_ot, r_pv, r_dn, r_tm1, r_tm2, r_vxo, r_g0o, r_zst = (Res() for _ in range(10))
        r_nrm = r_dn
        r_sq = [Res() for _ in range(8)]
        r_hh, r_qT, r_zs, r_yxa = [Res(), Res()], [Res(), Res()], [Res(), Res()], [Res(), Res()]
        r_stage = [Res() for _ in range(4)]
        r_hst = [Res() for _ in range(12)]
        r_pT = [Res() for _ in range(4)]
        xv = xsrc.rearrange("(kt p) t -> p kt t", p=128)
        pv3 = pscr.rearrange("(g p) t -> p g t", p=128)
        mv3 = mixs.rearrange("(g p) t -> p g t", p=128)
        vxv = vx1s.rearrange("(j p) t -> p j t", p=128)
        g0v = g0s.rearrange("(j p) t -> p j t", p=128)
        op(POOL, lambda: G.memset(hst[:], 0.0), w=r_hst)
        op(POOL, lambda: G.memset(zst[:], 0.0), w=[r_zst])

        def norm_a(i, q):
            b = i % 2
            op(ACT, lambda: S.activation(out=sq[:, q, :], in_=xt[b][:, q, :], func=AF.Square), r=[r_xt[b]], w=[r_sq[q]])

        def norm(i):
            mmg([(nrm[:], ones_d[:], sq[:, kt, :]) for kt in range(8)], r=r_sq + [Rc], w=[r_nrm])
            op(ACT, lambda: S.activation(out=rstd[:], in_=nrm[:], func=AF.Ln, bias=epsb[:, 0:1], scale=1.0), r=[r_nrm, Rc], w=[r_rstd])
            op(ACT, lambda: S.activation(out=rstd[:], in_=rstd[:], func=AF.Exp, scale=-0.5), r=[r_rstd], w=[r_rstd])

        def norm_h(i, q):
            b = i % 2
            for kt in (2 * q, 2 * q + 1):
                op(DVE, lambda kt=kt: V.scalar_tensor_tensor(out=hh[b][:, kt, :], in0=xt[b][:, kt, :], scalar=sp[:, C_NG + kt:C_NG + kt + 1],
                                                             in1=rstd[:], op0=ALU.mult, op1=ALU.mult), r=[r_xt[b], r_rstd, Rsp], w=[r_hh[b]])

        def attn_micro(i):
            b = i % 2
            t0 = i * 512
            steps = []
            for pr in range(2):
                for hl in range(2):
                    for mt in range(2):
                        def f(pr=pr, hl=hl, mt=mt):
                            hb = hl * 64
                            pi = hl * 2 + mt
                            mmg([(sc[mt][:], kT[hb:hb + 64, pr, mt * 128:(mt + 1) * 128], qT[b][hb:hb + 64, pr, :])],
                                r=[r_kT, r_qT[b]], w=[r_sc[mt]])
                            op(ACT, lambda: S.activation(out=pT[:, pi, :], in_=sc[mt][:], func=AF.Exp, scale=0.125),
                               r=[r_sc[mt]], w=[r_pT[pi]])
                        steps.append(f)

                def fin(pr=pr):
                    mmg([(pv[:], vpad[:, 2 * pr + (pi // 2), pi % 2, :], pT[:, pi, :]) for pi in range(4)], r=[r_vpad] + r_pT, w=[r_pv])
                    mmg([(dn[:], ind[:, pi // 2, :], pT[:, pi, :]) for pi in range(4)], r=[Rc] + r_pT, w=[r_dn])
                    op(ACT, lambda: S.activation(out=rden[:], in_=dn[:], func=AF.Ln), r=[r_dn], w=[r_rden])
                    op(ACT, lambda: S.activation(out=rden[:], in_=rden[:], func=AF.Exp, scale=-1.0), r=[r_rden], w=[r_rden])
                    op(DVE, lambda: V.tensor_tensor(out=ot[:], in0=pv[:], in1=rden[:], op=ALU.mult), r=[r_pv, r_rden], w=[r_ot])
                    op(DVE, lambda: V.tensor_tensor(out=yxa[b][:, pr, :], in0=ot[:], in1=zs[b][:, pr, :], op=ALU.mult),
                       r=[r_ot, r_zs[b]], w=[r_yxa[b]])
                    if pr == 1:
                        dma(mv3[:, 6:8, t0:t0 + 512], yxa[b][:], r=[r_yxa[b]])
                steps.append(fin)
            return steps

        def hy_micro(i):
            t0 = i * 512
            W = 513 if i == NT - 1 else 512
            steps = []

            def conv_step(g, dst, rdst, jj):
                def f():
                    w0 = sp[:, C_SHW + g * 3:C_SHW + g * 3 + 1]
                    w1 = sp[:, C_SHW + g * 3 + 1:C_SHW + g * 3 + 2]
                    w2 = sp[:, C_SHW + g * 3 + 2:C_SHW + g * 3 + 3]
                    bb = sp[:, C_SHB + g:C_SHB + g + 1]
                    o = dst[:, jj, 0:W]
                    op(ACT, lambda: S.activation(out=o, in_=hst[:, g, 1:1 + W], func=AF.Identity, bias=bb, scale=w1),
                       r=[r_hst[g], Rsp], w=[rdst[jj]])
                    op(DVE, lambda: V.scalar_tensor_tensor(out=o, in0=hst[:, g, 0:W], scalar=w0, in1=o, op0=ALU.mult, op1=ALU.add),
                       r=[r_hst[g], Rsp, rdst[jj]], w=[rdst[jj]])
                    op(DVE, lambda: V.scalar_tensor_tensor(out=o, in0=hst[:, g, 2:2 + W], scalar=w2, in1=o, op0=ALU.mult, op1=ALU.add),
                       r=[r_hst[g], Rsp, rdst[jj]], w=[rdst[jj]])
                    op(DVE, lambda: V.tensor_copy(out=hst[:, g, 0:2], in_=hst[:, g, 512:514]), r=[r_hst[g]], w=[r_hst[g]])
                return f

            def store(dstv, src, rsrc):
                if i == 0:
                    dma(dstv[:, :, 0:W - 1], src[:, :, 1:W], r=rsrc)
                else:
                    dma(dstv[:, :, t0 - 1:t0 - 1 + W], src[:, :, 0:W], r=rsrc)

            def silu_z():
                op(ACT, lambda: S.activation(out=tm2[:, :, 0:W], in_=zst[:, :, 0:W], func=AF.Silu), r=[r_zst], w=r_t2)
                op(DVE, lambda: V.tensor_copy(out=zst[:, :, 0:1], in_=zst[:, :, 512:513]), r=[r_zst], w=[r_zst])
            steps.append(silu_z)
            for jj in range(4):
                steps.append(conv_step(jj, tm1, r_t1, jj))

            def g0_prod():
                op(DVE, lambda: V.tensor_tensor(out=g0o[:, :, 0:W], in0=tm1[:, :, 0:W], in1=tm2[:, :, 0:W], op=ALU.mult),
                   r=r_t1 + r_t2, w=[r_g0o])
                store(g0v, g0o, [r_g0o])
            steps.append(g0_prod)
            for jj in range(4):
                steps.append(conv_step(4 + jj, tm1, r_t1, jj))
            for jj in range(4):
                steps.append(conv_step(8 + jj, tm2, r_t2, jj))

            def vx_prod():
                op(DVE, lambda: V.tensor_tensor(out=vxo[:, :, 0:W], in0=tm1[:, :, 0:W], in1=tm2[:, :, 0:W], op=ALU.mult),
                   r=r_t1 + r_t2, w=[r_vxo])
                store(vxv, vxo, [r_vxo])
            steps.append(vx_prod)
            return steps

        r_t1 = [Res() for _ in range(4)]
        r_t2 = [Res() for _ in range(4)]
        dma(xt[0][:], xv[:, :, 0:512], w=[r_xt[0]])
        for q in range(8):
            norm_a(0, q)
        norm(0)
        for q in range(4):
            norm_h(0, q)
        pending = []
        ev = 0
        gi = 0
        for i in range(NT):
            b = i % 2
            t0 = i * 512
            if i + 1 < NT:
                dma(xt[1 - b][:], xv[:, :, t0 + 512:t0 + 1024], w=[r_xt[1 - b]])
            for gq, g in enumerate(GORD):
                pi = gi % 4
                gi += 1
                mmg([(pj[pi][:], win[:, kt, g * 128:(g + 1) * 128], hh[b][:, kt, :]) for kt in range(8)], r=[r_win, r_hh[b]], w=[r_pj[pi]])
                if g < 16:
                    if g < 12:
                        dst, rd = hst[:, g, 2:514], r_hst[g]
                    else:
                        dst, rd = zst[:, g - 12, 1:513], r_zst
                    ev += 1
                    if ev % 4:
                        op(ACT, lambda dst=dst, pi=pi: S.copy(out=dst, in_=pj[pi][:]), r=[r_pj[pi]], w=[rd])
                    else:
                        op(DVE, lambda dst=dst, pi=pi: V.tensor_copy(out=dst, in_=pj[pi][:]), r=[r_pj[pi]], w=[rd])
                elif g in (18, 19):
                    op(ACT, lambda g=g, pi=pi: S.activation(out=sg[:, g - 18, :], in_=pj[pi][:], func=AF.Sigmoid), r=[r_pj[pi]], w=[r_sg[g - 18]])
                elif g in (16, 17):
                    op(DVE, lambda g=g, pi=pi: V.tensor_tensor(out=stage[:, g - 16, :], in0=pj[pi][:], in1=sg[:, g - 16, :], op=ALU.mult),
                       r=[r_pj[pi], r_sg[g - 16]], w=[r_stage[g - 16]])
                elif g in (20, 21):
                    op(ACT, lambda g=g, pi=pi: S.activation(out=stage[:, g - 18, :], in_=pj[pi][:], func=AF.Silu), r=[r_pj[pi]], w=[r_stage[g - 18]])
                    if g == 21:
                        dma(pv3[:, 16:20, t0:t0 + 512], stage[:], r=r_stage)
                elif g < 24:
                    op(DVE, lambda g=g, pi=pi: V.tensor_copy(out=qT[b][:, g - 22, :], in_=pj[pi][:]), r=[r_pj[pi]], w=[r_qT[b]])
                else:
                    op(ACT, lambda g=g, pi=pi: S.activation(out=zs[b][:, g - 24, :], in_=pj[pi][:], func=AF.Silu), r=[r_pj[pi]], w=[r_zs[b]])
                if gq < 8 and i + 1 < NT:
                    norm_a(i + 1, gq)
                if gq == 10 and i + 1 < NT:
                    norm(i + 1)
                if 13 <= gq < 17 and i + 1 < NT:
                    norm_h(i + 1, gq - 13)
                if gq == 15:
                    pending = pending + hy_micro(i)
                if pending and (gq >= 16 or i > 0):
                    pending.pop(0)()
            att = attn_micro(i)
            mix_ = []
            while pending or att:
                if pending:
                    mix_.append(pending.pop(0))
                if att:
                    mix_.append(att.pop(0))
            pending = mix_
        while pending:
            pending.pop(0)()
        k.barrier()
        st.close()

    def phase_b2():
        xv = vx1s.rearrange("c (a b) -> a c b", b=128)
        yv = yscr.rearrange("c (a b) -> a c b", b=128)

        def load_d(b, dst, rdst):
            dma(dst[0:64, :, :], xv[:, b * SBC:(b + 1) * SBC, :], w=[rdst])

        def store_d(b, src, rsrc):
            dma(yv[:, b * SBC:(b + 1) * SBC, :], src[:], r=rsrc)

        fft_pipeline(512 // SBC, 64, load_d, False, store_d)

    CH = 2048

    def phase_b3():
        st = ExitStack()
        CW = 1024
        NB = 3
        g0c = [k.sb(st, "g0c%d" % i, [128, CW], BF16) for i in range(NB)]
        vxc = [k.sb(st, "vxc%d" % i, [128, CW], BF16) for i in range(NB)]
        yc = [k.sb(st, "yc%d" % i, [128, CW], F32) for i in range(NB)]
        ttc = [k.sb(st, "ttc%d" % i, [128, CW], F32) for i in range(NB)]
        ohc = [k.sb(st, "ohc%d" % i, [128, CW], BF16) for i in range(NB)]
        RN = lambda: [Res() for _ in range(NB)]
        r_g0c, r_vxc, r_yc, r_ttc, r_ohc = RN(), RN(), RN(), RN(), RN()
        chunks = [(j, c0) for j in range(4) for c0 in range(0, L, CW)]

        def b3_load(n):
            j, c0 = chunks[n]
            i = n % NB
            dma(g0c[i][:], g0s[j * 128:(j + 1) * 128, c0:c0 + CW], w=[r_g0c[i]])
            dma(vxc[i][:], vx1s[j * 128:(j + 1) * 128, c0:c0 + CW], w=[r_vxc[i]])
            dma(yc[i][:], yscr[j * 128:(j + 1) * 128, c0:c0 + CW], w=[r_yc[i]], Q=ACT)

        def b3_comp(n):
            j, c0 = chunks[n]
            i = n % NB
            op(DVE, lambda: V.scalar_tensor_tensor(out=ttc[i][:], in0=vxc[i][:], scalar=sp[:, C_SKIP + j:C_SKIP + j + 1],
                                                   in1=yc[i][:], op0=ALU.mult, op1=ALU.add), r=[r_vxc[i], r_yc[i], Rsp], w=[r_ttc[i]])
            op(DVE, lambda: V.tensor_tensor(out=ohc[i][:], in0=ttc[i][:], in1=g0c[i][:], op=ALU.mult), r=[r_ttc[i], r_g0c[i]], w=[r_ohc[i]])
            dma(mixs[j * 128:(j + 1) * 128, c0:c0 + CW], ohc[i][:], r=[r_ohc[i]], Q=POOL)

        for n in range(min(NB - 1, len(chunks))):
            b3_load(n)
        for n in range(len(chunks)):
            if n + NB - 1 < len(chunks):
                b3_load(n + NB - 1)
            b3_comp(n)
        k.barrier()
        st.close()

    def phase_b4():
        st = ExitStack()
        hrow = k.sb(st, "hrow", [128, 2, L + 30], BF16)
        r_hrow = [Res(), Res()]
        diag = k.sb(st, "diag", [128, 62, 128], BF16)
        r_diag = [Res() for _ in range(62)]
        for jk in range(62):
            op(DVE, lambda jk=jk: V.tensor_scalar(out=diag[:, jk, :], in0=ident[:], scalar1=sp[:, C_DWW + jk:C_DWW + jk + 1], scalar2=None,
                                                  op0=ALU.mult), r=[Rc, Rsp], w=[r_diag[jk]])
        for j in range(2):
            op(POOL, lambda j=j: G.memset(hrow[:, j, 0:15], 0.0), w=[r_hrow[j]])
            op(POOL, lambda j=j: G.memset(hrow[:, j, L + 15:L + 30], 0.0), w=[r_hrow[j]])
            dma(hrow[:, j, 15:15 + L], pscr[(16 + j) * 128:(17 + j) * 128, :], w=[r_hrow[j]])
        czc = [k.sb(st, "czc%d" % i, [128, 2, 512], BF16) for i in range(4)]
        ouc = [k.sb(st, "ouc%d" % i, [128, 2, 512], BF16) for i in range(2)]
        r_czc, r_ouc = [Res() for _ in range(4)], [Res(), Res()]
        czv = pscr[18 * 128:20 * 128, :].rearrange("(j p) t -> p j t", p=128)
        mxv = mixs[512:768, :].rearrange("(j p) t -> p j t", p=128)
        P2_ = range(2)
        cb = [k.sb(st, "cb%d" % i, [128, 2, 512], F32) for i in range(4)]
        cbb = [k.sb(st, "cbb%d" % i, [128, 2, 512], BF16) for i in P2_]
        sqb = [k.sb(st, "sqb%d" % i, [128, 2, 512], BF16) for i in P2_]
        m2 = [k.sb(st, "m2%d" % i, [128, 512], F32) for i in P2_]
        var = [k.sb(st, "var%d" % i, [128, 512], F32) for i in P2_]
        dd = [k.sb(st, "dd%d" % i, [128, 2, 512], F32) for i in P2_]
        zz = [k.sb(st, "zz%d" % i, [128, 2, 512], F32) for i in P2_]
        pc = [[k.ps(st, "pc%d%d" % (i, j), [128, 512]) for j in range(2)] for i in P2_]
        pm = [k.ps(st, "pm%d" % i, [128, 512]) for i in P2_]
        pq = [k.ps(st, "pq%d" % i, [128, 512]) for i in P2_]
        RR = lambda: [Res(), Res()]
        r_cb, r_cbb, r_sqb, r_m2, r_var, r_zz, r_pm, r_pq = [Res() for _ in range(4)], RR(), RR(), RR(), RR(), RR(), RR(), RR()
        r_dd = [RR(), RR()]
        r_pc = [RR(), RR()]

        def c0_(ci):
            i = ci % 2
            t0 = ci * 512
            for j in range(2):
                mmg([(pc[i][j][:], diag[:, j * 31 + kk, :], hrow[:, j, t0 + kk:t0 + kk + 512]) for kk in range(31)],
                    r=r_diag[j * 31:(j + 1) * 31] + [r_hrow[j]], w=[r_pc[i][j]])

        def c1_(ci):
            i = ci % 2
            for j in range(2):
                op(ACT, lambda j=j: S.activation(out=cb[ci % 4][:, j, :], in_=pc[i][j][:], func=AF.Identity, bias=sp[:, C_DWB + j:C_DWB + j + 1], scale=1.0),
                   r=[r_pc[i][j], Rsp], w=[r_cb[ci % 4]])

        def c2_(ci):
            i = ci % 2
            op(DVE, lambda: V.tensor_copy(out=cbb[i][:], in_=cb[ci % 4][:]), r=[r_cb[ci % 4]], w=[r_cbb[i]])
            op(ACT, lambda: S.activation(out=sqb[i][:], in_=cb[ci % 4][:], func=AF.Square), r=[r_cb[ci % 4]], w=[r_sqb[i]])
            mmg([(pm[i][:], ones_c[:], cbb[i][:, j, :]) for j in range(2)], r=[Rc, r_cbb[i]], w=[r_pm[i]])
            mmg([(pq[i][:], ones_c[:], sqb[i][:, j, :]) for j in range(2)], r=[Rc, r_sqb[i]], w=[r_pq[i]])

        def c3_(ci):
            i = ci % 2
            dma(czc[ci % 4][:], czv[:, :, ci * 512:ci * 512 + 512], w=[r_czc[ci % 4]], Q=ACT)
            op(ACT, lambda: S.activation(out=m2[i][:], in_=pm[i][:], func=AF.Square), r=[r_pm[i]], w=[r_m2[i]])
            op(DVE, lambda: V.tensor_tensor(out=var[i][:], in0=pq[i][:], in1=m2[i][:], op=ALU.subtract), r=[r_pq[i], r_m2[i]], w=[r_var[i]])
            op(DVE, lambda: V.tensor_scalar(out=var[i][:], in0=var[i][:], scalar1=0.0, scalar2=None, op0=ALU.max), r=[r_var[i]], w=[r_var[i]])
            op(ACT, lambda: S.activation(out=var[i][:], in_=var[i][:], func=AF.Sqrt, bias=EPS, scale=1.0), r=[r_var[i]], w=[r_var[i]])
            op(DVE, lambda: V.reciprocal(out=var[i][:], in_=var[i][:]), r=[r_var[i]], w=[r_var[i]])

        def c4_(ci):
            i = ci % 2
            for j in range(2):
                op(DVE, lambda j=j: V.tensor_tensor(out=dd[i][:, j, :], in0=cb[ci % 4][:, j, :], in1=pm[i][:], op=ALU.subtract),
                   r=[r_cb[ci % 4], r_pm[i]], w=[r_dd[i][j]])
            for j in range(2):
                op(DVE, lambda j=j: V.tensor_tensor(out=dd[i][:, j, :], in0=dd[i][:, j, :], in1=var[i][:], op=ALU.mult),
                   r=[r_dd[i][j], r_var[i]], w=[r_dd[i][j]])
            for j in range(2):
                op(ACT, lambda j=j: S.activation(out=dd[i][:, j, :], in_=dd[i][:, j, :], func=AF.Silu, bias=sp[:, C_LNB + j:C_LNB + j + 1],
                                                 scale=sp[:, C_LNG + j:C_LNG + j + 1]), r=[r_dd[i][j], Rsp], w=[r_dd[i][j]])

        def c5_(ci):
            i = ci % 2
            t0 = ci * 512
            op(DVE, lambda: V.tensor_tensor(out=ouc[i][:], in0=dd[i][:], in1=czc[ci % 4][:], op=ALU.mult),
               r=r_dd[i] + [r_czc[ci % 4]], w=[r_ouc[i]])
            dma(mxv[:, :, t0:t0 + 512], ouc[i][:], r=[r_ouc[i]])

        cst = [c0_, c1_, c2_, c3_, c4_, c5_]
        nsteps = NT + len(cst) - 1
        for step in range(nsteps):
            for si in range(len(cst) - 1, -1, -1):
                ci = step - si
                if 0 <= ci < NT:
                    cst[si](ci)
        k.barrier()
        st.close()

    def phase_c(l, s, xsrc, last, wo, r_wo, nxt):
        st_kv = kv_prep(nxt[0], nxt[1], False) if nxt is not None else None
        st = ExitStack()
        mt_ = [k.sb(st, "mixt%d" % i, [128, 8, 512], BF16) for i in range(2)]
        NXB = 4 if last else 3
        xt = [k.sb(st, "xtc%d" % i, [128, 8, 512], F32) for i in range(NXB)]
        xo = xt
        po = [k.ps(st, "po%d" % i, [128, 512]) for i in range(4)]
        r_xt = [Res() for _ in range(NXB)]
        r_xo = r_xt
        r_mt, r_po = [Res(), Res()], [Res() for _ in range(4)]
        xv = xsrc.rearrange("(kt p) t -> p kt t", p=128)
        ov = yT[s].rearrange("(kt p) t -> p kt t", p=128)
        mv3 = mixs.rearrange("(g p) t -> p g t", p=128)
        if last:
            sq = k.sb(st, "sqc", [128, 8, 512], BF16)
            rstd = k.sb(st, "rstdc", [128, 512], F32)
            pn = k.ps(st, "pn", [128, 512])
            r_sq, r_rstd, r_pn = [Res() for _ in range(4)], Res(), Res()

        def fin_a(i, q):
            b = i % NXB
            op(ACT, lambda: S.activation(out=sq[:, 2 * q:2 * q + 2, :], in_=xo[b][:, 2 * q:2 * q + 2, :], func=AF.Square), r=[r_xo[b]], w=[r_sq[q]])

        def fin_b(i):
            mmg([(pn[:], ones_d[:], sq[:, kt, :]) for kt in range(8)], r=r_sq + [Rc], w=[r_pn])
            op(ACT, lambda: S.activation(out=rstd[:], in_=pn[:], func=AF.Ln, bias=epsb[:, 0:1], scale=1.0), r=[r_pn, Rc], w=[r_rstd])
            op(ACT, lambda: S.activation(out=rstd[:], in_=rstd[:], func=AF.Exp, scale=-0.5), r=[r_rstd], w=[r_rstd])

        def fin_c(i, g):
            b = i % NXB
            op(DVE, lambda: V.scalar_tensor_tensor(out=xo[b][:, g, :], in0=xo[b][:, g, :], scalar=sp[:, C_FG + g:C_FG + g + 1],
                                                   in1=rstd[:], op0=ALU.mult, op1=ALU.mult), r=[r_xo[b], r_rstd, Rsp], w=[r_xo[b]])

        def store(i):
            dma(ov[:, :, i * 512:(i + 1) * 512], xo[i % NXB][:], r=[r_xo[i % NXB]])

        r_mh = [Res(), Res()]

        def loads(i):
            bb = i % 2
            t1 = i * 512
            dma(mt_[bb][:], mv3[:, :, t1:t1 + 512], w=[r_mt[bb]])
            dma(xt[i % NXB][:], xv[:, :, t1:t1 + 512], w=[r_xt[i % NXB]], Q=ACT)

        loads(0)
        for i in range(NT):
            b = i % 2
            t0 = i * 512
            if i + 1 < NT:
                loads(i + 1)
            for g in range(8):
                pi = g % 4
                mmg([(po[pi][:], wo[:, kt, g * 128:(g + 1) * 128], mt_[b][:, kt, :]) for kt in range(8)], r=[r_wo, r_mt[b], r_mh[b]], w=[r_po[pi]])

                op(DVE, lambda g=g, pi=pi, i=i: V.tensor_tensor(out=xt[i % NXB][:, g, :], in0=po[pi][:], in1=xt[i % NXB][:, g, :], op=ALU.add),
                   r=[r_po[pi], r_xt[i % NXB]], w=[r_xt[i % NXB]])
                if last and i > 0:
                    if g < 4:
                        fin_a(i - 1, g)
                    if g == 4:
                        fin_b(i - 1)
            if last:
                if i > 0:
                    for g in range(8):
                        fin_c(i - 1, g)
                    store(i - 1)
            else:
                store(i)
        if last:
            i = NT - 1
            for q in range(4):
                fin_a(i, q)
            fin_b(i)
            for g in range(8):
                fin_c(i, g)
            store(i)
        k.barrier()
        st.close()
        if st_kv is not None:
            st_kv.close()

    load_layer_weights(0)
    order = [(l, s) for l in range(nl) for s in range(nseq)]
    for idx, (l, s) in enumerate(order):
        if s == 0:
            dma(sp[:], spar[l], w=[Rsp])
            k.barrier()
            st_kv0 = kv_prep(l, s, True)
            gen_filter(l)
            st_kv0.close()
        xsrc = xT[s] if l == 0 else yT[s]
        phase_a(l, s, xsrc)
        if s == nseq - 1 and l + 1 < nl:
            load_layer_weights(l + 1)
        wst_ = ExitStack()
        wo = k.sb(wst_, "wo", [128, 8, D], BF16)
        r_wo = Res()
        dma(wo[:], w_out[l].rearrange("(kt p) c -> p kt c", p=128), w=[r_wo], Q=POOL)
        phase_b2()
        phase_b3()
        phase_b4()
        nxt = order[idx + 1] if idx + 1 < len(order) else None
        phase_c(l, s, xsrc, l == nl - 1, wo, r_wo, nxt if (nxt is not None and nxt[0] == l) else None)
        if nxt is not None and nxt[0] != l:
            pass
        wst_.close()
        if nxt is not None and nxt[0] != l:
            pass
    k.barrier()
    top.close()
    k.es.close()
    return nc


def _consts():
    a = np.arange(128, dtype=np.float64)
    th = 2.0 * np.pi * np.outer(a, a) / 128.0
    C = np.cos(th)
    S = np.sin(th)
    wk = np.full((128, 1), 2.0); wk[0] = 1.0; wk[64] = 1.0; wk[65:] = 0.0
    cf = np.concatenate([C, -S, S, C[:, 0:65], -S[:, 0:65], wk * C[:, 0:64] / 16384.0, -wk * S[:, 0:64] / 16384.0,
                         np.zeros((128, 2)), C, S, -S, C], axis=1)
    cfft = cf.astype(np.float32).astype(ml_dtypes.bfloat16)
    tt = 2.0 * np.pi * np.outer(a, a) / 16384.0
    ctw = np.stack([np.cos(tt), -np.sin(tt)], axis=1).astype(np.float32)
    t = np.linspace(0.0, 1.0, L, dtype=np.float32)[:, None]
    n = np.arange(L, dtype=np.float32)[:, None]
    bands = np.linspace(1e-4, 15, 16, dtype=np.float32)[None, :]
    ang = (np.float32(2.0 * math.pi / L) * bands * n).astype(np.float32)
    z = np.concatenate([t, np.cos(ang), -np.sin(ang)], axis=-1).astype(np.float32)
    idx_b = (L - np.arange(L)) % L
    cz = np.stack([z.T, z[idx_b].T], axis=0).astype(np.float32)
    ctp = np.stack([t[:, 0], t[idx_b, 0]], axis=0).astype(np.float32)
    max_decay = math.log(1e-2) / 0.3
    min_decay = math.log(1e-2) / 1.5
    deltas = np.abs(np.linspace(min_decay, max_decay, 512, dtype=np.float32))
    cnd = np.ascontiguousarray((-deltas).reshape(4, 128).T).astype(np.float32)
    cid = np.eye(128, dtype=np.float32).astype(ml_dtypes.bfloat16)
    return dict(cfft=cfft, ctw=ctw, cz=np.ascontiguousarray(cz), ctp=np.ascontiguousarray(ctp), cnd=cnd, cid=cid)


def _spar(inp, nl):
    sp = np.zeros((nl, 128, NSP), np.float32)
    f = lambda a: np.asarray(a, np.float32)
    for l in range(nl):
        sp[l, :, 0:8] = f(inp["norm_g"])[l].reshape(8, 128).T
        sp[l, :, 8:16] = f(inp["mem_norm_g"])[l].reshape(8, 128).T
        sw = f(inp["hy_short_w"])[l]
        sp[l, :, 16:52] = sw.reshape(3, 12, 128).transpose(2, 1, 0).reshape(128, 36)
        sp[l, :, 52:64] = f(inp["hy_short_b"])[l].reshape(12, 128).T
        sp[l, :, 64:68] = f(inp["hy_skip"])[l].reshape(4, 128).T
        dw = f(inp["cf_dw_w"])[l]
        sp[l, :, 68:130] = dw.reshape(31, 2, 128).transpose(2, 1, 0).reshape(128, 62)
        sp[l, :, 130:132] = f(inp["cf_dw_b"])[l].reshape(2, 128).T
        sp[l, :, 132:134] = f(inp["cf_ln_g"])[l].reshape(2, 128).T
        sp[l, :, 134:136] = f(inp["cf_ln_b"])[l].reshape(2, 128).T
        sp[l, :, 136:144] = f(inp["final_g"]).reshape(8, 128).T
        for j, (bn, fn) in enumerate((("hy_f_b1", "hy_f_fr1"), ("hy_f_b2", "hy_f_fr2"), ("hy_f_b3", "hy_f_fr3"))):
            sp[l, 0:64, 144 + 2 * j] = f(inp[bn])[l]
            sp[l, 0:64, 144 + 2 * j + 1] = f(inp[fn])[l]
            sp[l, 64:128, 144 + 2 * j] = f(inp[bn])[l]
            sp[l, 64:128, 144 + 2 * j + 1] = f(inp[fn])[l]
    return sp


_NC_CACHE = {}


def run(inp, nl=4, nseq=2, seq_lists=None, debug=False, trace=False):
    key = (nl, nseq, debug)
    if key not in _NC_CACHE:
        _NC_CACHE[key] = build_program(nl, nseq, debug)
    nc = _NC_CACHE[key]
    f = lambda a: np.ascontiguousarray(np.asarray(a, np.float32))
    shared = _consts()
    shared.update(
        w_in=f(inp["w_in"])[:nl], w_kv=f(inp["xa_w_kv"])[:nl], w_out=f(inp["w_out"])[:nl],
        fw1=f(inp["hy_f_w1"])[:nl], fw2=f(inp["hy_f_w2"])[:nl], fw3=f(inp["hy_f_w3"])[:nl], fw4=f(inp["hy_f_w4"])[:nl],
        spar=_spar(inp, nl),
    )
    srcs = {"s": (inp["x_sample"], inp["mem_sample"]), "p": (inp["x_prompt"], inp["mem_prompt"])}
    tcache = {}

    def getT(kind, idx):
        if (kind, idx) not in tcache:
            x, m = srcs[kind]
            tcache[(kind, idx)] = (np.ascontiguousarray(np.asarray(x[idx], np.float32).T),
                                   np.ascontiguousarray(np.asarray(m[idx], np.float32).T))
        return tcache[(kind, idx)]

    in_maps = []
    for c, sl in enumerate(seq_lists):
        xs = np.stack([getT(*q)[0] for q in sl], axis=0)
        ms = np.stack([getT(*q)[1] for q in sl], axis=0)
        d = dict(shared)
        d["xT"] = xs
        d["memT"] = ms
        in_maps.append(d)
    res = run_bass_kernel_spmd(nc, in_maps, core_ids=list(range(len(seq_lists))), **({"trace": True} if trace else {}))
    return res


def kernel(**inp):
    seq_lists = [[("s", c), ("p", c % 2)] for c in range(N_CORES)]
    res = run(inp, 4, 2, seq_lists)
    y_s = np.empty((8, L, D), np.float32)
    y_p = np.empty((2, L, D), np.float32)
    for c in range(N_CORES):
        yt = res.results[c]["yT"]
        y_s[c] = yt[0].T
        if c < 2:
            y_p[c] = yt[1].T
    return (y_p, y_s)
```

```python
import math
from contextlib import ExitStack
import numpy as np
import ml_dtypes
import concourse.bass as bass
import concourse.mybir as mybir
from concourse.bass_utils import run_bass_kernel_spmd

F32 = mybir.dt.float32
BF16 = mybir.dt.bfloat16
AF = mybir.ActivationFunctionType
ALU = mybir.AluOpType

L = 8192
D = 1024
DIN = 3328
NMEM = 256
NT = L // 512
EPS = 1e-6
NSP = 160
N_CORES = 8


class Res:
    __slots__ = ("w", "r")

    def __init__(self):
        self.w = None
        self.r = {}


class Eng:
    def __init__(self, eng, sid):
        self.eng = eng
        self.sid = sid
        self.cnt = 0
        self.waited = {}


class K:
    def __init__(self, nc, nl, nseq, debug):
        self.nc = nc
        self.nl = nl
        self.nseq = nseq
        self.debug = debug
        self.sems = []
        self.es = ExitStack()

        def newsem(name):
            s = self.es.enter_context(nc.semaphore(name))
            self.sems.append(s)
            return len(self.sems) - 1

        self.PE = Eng(nc.tensor, newsem("s_pe"))
        self.ACT = Eng(nc.scalar, newsem("s_act"))
        self.DVE = Eng(nc.vector, newsem("s_dve"))
        self.POOL = Eng(nc.gpsimd, newsem("s_pool"))
        self.SP = Eng(nc.sync, None)
        self.engs = [self.PE, self.ACT, self.DVE, self.POOL, self.SP]
        self.ND = 40
        self.NDSW = 8
        self.dma_sid = [newsem("s_dma%d" % i) for i in range(self.ND)]
        self.dma_uses = [0] * self.ND
        self.dma_next = 0
        self.dma_next_sw = 0

    def _need(self, r, w, extra=None):
        need = dict(extra or {})

        def add(sid, val):
            if need.get(sid, 0) < val:
                need[sid] = val

        for x in r:
            if x.w is not None:
                add(*x.w)
        for x in w:
            if x.w is not None:
                add(*x.w)
            for sid, val in x.r.items():
                add(sid, val)
        return need

    def _waits(self, E, need):
        for sid, val in need.items():
            if E.waited.get(sid, 0) < val:
                E.eng.wait_ge(self.sems[sid], val)
                E.waited[sid] = val

    def op(self, E, fn, r=(), w=()):
        self._waits(E, self._need(r, w))
        ins = fn()
        E.cnt += 1
        ins.then_inc(self.sems[E.sid], 1)
        for x in r:
            x.r[E.sid] = E.cnt
        for x in w:
            x.w = (E.sid, E.cnt)
            x.r = {}

    def mmg(self, mms, r=(), w=()):
        E = self.PE
        self._waits(E, self._need(r, w))
        n = len(mms)
        for i, (o, a, b) in enumerate(mms):
            ins = self.nc.tensor.matmul(o, a, b, start=(i == 0), stop=(i == n - 1))
        E.cnt += 1
        ins.then_inc(self.sems[E.sid], 1)
        for x in r:
            x.r[E.sid] = E.cnt
        for x in w:
            x.w = (E.sid, E.cnt)
            x.r = {}

    def dma(self, out, in_, r=(), w=(), Q=None):
        Q = Q or self.SP
        if Q is self.POOL:
            k = self.ND - self.NDSW + self.dma_next_sw
            self.dma_next_sw = (self.dma_next_sw + 1) % self.NDSW
        else:
            k = self.dma_next
            self.dma_next = (k + 1) % (self.ND - self.NDSW)
        sid = self.dma_sid[k]
        j = self.dma_uses[k]
        self._waits(Q, self._need(r, w, {sid: 16 * j} if j else None))
        Q.eng.dma_start(out=out, in_=in_).then_inc(self.sems[sid], 16)
        self.dma_uses[k] = j + 1
        val = 16 * (j + 1)
        for x in r:
            x.r[sid] = val
        for x in w:
            x.w = (sid, val)
            x.r = {}

    def barrier(self):
        tg = {}
        for E in self.engs:
            if E.sid is not None and E.cnt:
                tg[E.sid] = E.cnt
        for k in range(self.ND):
            if self.dma_uses[k]:
                tg[self.dma_sid[k]] = 16 * self.dma_uses[k]
        for E in self.engs:
            self._waits(E, tg)

    def sb(self, st, name, shape, dt):
        self.uid = getattr(self, "uid", 0) + 1
        return st.enter_context(self.nc.sbuf_tensor("%s_%d" % (name, self.uid), list(shape), dt))

    def ps(self, st, name, shape):
        self.uid = getattr(self, "uid", 0) + 1
        return st.enter_context(self.nc.psum_tensor("%s_%d" % (name, self.uid), list(shape), F32))


def build_program(nl=4, nseq=2, debug=False):
    nc = bass.Bass("TRN2", target_bir_lowering=False)
    k = K(nc, nl, nseq, debug)
    PE, ACT, DVE, POOL, SP = k.PE, k.ACT, k.DVE, k.POOL, k.SP
    V, S, G, T = nc.vector, nc.scalar, nc.gpsimd, nc.tensor
    op, mmg, dma = k.op, k.mmg, k.dma

    def din(name, shape, dt=F32):
        return nc.dram_tensor(name, list(shape), dt, kind="ExternalInput").ap()

    skind = "ExternalOutput" if debug else "Internal"

    def dscr(name, shape, dt):
        return nc.dram_tensor(name, list(shape), dt, kind=skind).ap()

    xT = din("xT", [nseq, D, L])
    memT = din("memT", [nseq, D, NMEM])
    w_in = din("w_in", [nl, D, DIN])
    w_kv = din("w_kv", [nl, D, 512])
    w_out = din("w_out", [nl, D, D])
    fw1 = din("fw1", [nl, 33, 64])
    fw2 = din("fw2", [nl, 64, 64])
    fw3 = din("fw3", [nl, 64, 64])
    fw4 = din("fw4", [nl, 64, 1024])
    spar = din("spar", [nl, 128, NSP])
    cfft = din("cfft", [128, 1156], BF16)
    ctw = din("ctw", [128, 2, 128])
    cz = din("cz", [2, 33, L])
    ctp = din("ctp", [2, L])
    cnd = din("cnd", [128, 4])
    cid = din("cid", [128, 128], BF16)
    yT = nc.dram_tensor("yT", [nseq, D, L], F32, kind="ExternalOutput").ap()

    pscr = dscr("pscr", [22 * 128, L], BF16)
    mixs = dscr("mixs", [D, L], BF16)
    vx1s = dscr("vx1s", [512, L], BF16)
    g0s = dscr("g0s", [512, L], BF16)
    yscr = dscr("yscr", [512, L], F32)
    kfil = dscr("kfil", [512, 2 * L], BF16)
    kfs = dscr("kfs", [128, 2, 512, 65], BF16)

    top = ExitStack()
    fftc = k.sb(top, "fftc", [128, 1156], BF16)
    twt = k.sb(top, "twt", [128, 2, 128], F32)
    twb = k.sb(top, "twb", [128, 2, 128], BF16)
    ident = k.sb(top, "ident", [128, 128], BF16)
    ones_d = k.sb(top, "ones_d", [128, 128], BF16)
    ones_c = k.sb(top, "ones_c", [128, 128], BF16)
    ind = k.sb(top, "ind", [128, 2, 128], BF16)
    ndl = k.sb(top, "ndl", [128, 4], F32)
    sp = k.sb(top, "sp", [128, NSP], F32)
    frb = k.sb(top, "frb", [128, 3], F32)
    epsb = k.sb(top, "epsb", [128, 1], F32)
    win = k.sb(top, "win", [128, 8, DIN], BF16)
    wkv = k.sb(top, "wkv", [128, 8, 512], BF16)
    r_win, r_wkv = Res(), Res()

    def load_layer_weights(l):
        dma(wkv[:], w_kv[l].rearrange("(kt p) c -> p kt c", p=128), w=[r_wkv], Q=POOL)
        for kt in range(8):
            dma(win[:, kt, :], w_in[l][kt * 128:(kt + 1) * 128, :], w=[r_win], Q=POOL)
    Rc = Res()
    Rsp = Res()
    dma(fftc[:], cfft, w=[Rc])
    dma(twt[:], ctw, w=[Rc])
    dma(ident[:], cid, w=[Rc])
    dma(ndl[:], cnd, w=[Rc])
    op(DVE, lambda: V.tensor_copy(out=twb[:], in_=twt[:]), r=[Rc], w=[Rc])
    op(DVE, lambda: V.memset(epsb[:], EPS), w=[Rc])
    op(DVE, lambda: V.memset(ones_d[:], 1.0 / 1024.0), w=[Rc])
    op(DVE, lambda: V.memset(ones_c[:], 1.0 / 256.0), w=[Rc])
    op(DVE, lambda: V.memset(ind[:], 0.0), w=[Rc])
    op(DVE, lambda: V.memset(ind[:, 0, 0:64], 1.0), w=[Rc])
    op(DVE, lambda: V.memset(ind[:, 1, 64:128], 1.0), w=[Rc])
    k.barrier()
    Cm = fftc[:, 0:128]
    NSm = fftc[:, 128:256]
    Sm = fftc[:, 256:384]
    P1h = fftc[:, 384:514]
    C4w = fftc[:, 514:578]
    NS4w = fftc[:, 578:642]
    P2 = fftc[:, 644:900]
    P3 = fftc[:, 900:1156]

    C_NG, C_MNG, C_SHW, C_SHB, C_SKIP, C_DWW, C_DWB, C_LNG, C_LNB, C_FG, C_FB = 0, 8, 16, 52, 64, 68, 130, 132, 134, 136, 144

    KH = 65
    SBC = 8
    NSUB = 4

    def fft_pipeline(nsb, Kdim, load_in, is_filter, store_out=None):
        st = ExitStack()
        R = lambda n: [Res() for _ in range(n)]
        xin = [k.sb(st, "xin%d" % i, [128, SBC, 128], BF16) for i in range(2)]
        tA = [k.sb(st, "tA%d" % i, [128, SBC, 2, KH], BF16) for i in range(2)]
        mm = [k.sb(st, "cm%d" % i, [128, SBC, KH], BF16) for i in range(4)]
        Bb = [k.sb(st, "Bb%d" % i, [128, 2, SBC, KH], BF16) for i in range(2)]
        tX = [k.sb(st, "tX%d" % i, [128, 2, SBC, KH], BF16) for i in range(2)]
        psA = [k.ps(st, "psA%d" % i, [128, 512]) for i in range(2)]
        psX = [k.ps(st, "psX%d" % i, [128, 512]) for i in range(2)]
        r_xin, r_B, r_psA, r_psX = R(2), R(2), R(2), R(2)
        r_tA = [R(NSUB), R(NSUB)]
        r_tX = [R(NSUB), R(NSUB)]
        r_mm = R(4)
        if not is_filter:
            kft = [k.sb(st, "kft%d" % i, [128, 2, SBC, KH], BF16) for i in range(2)]
            Yb = [k.sb(st, "Yb%d" % i, [128, 2, SBC, KH], BF16) for i in range(2)]
            tC = [k.sb(st, "tC%d" % i, [KH, SBC, 2, 128], BF16) for i in range(2)]
            Dd = [k.sb(st, "Dd%d" % i, [KH, SBC, 2, 128], BF16) for i in range(2)]
            mT = [k.sb(st, "cmT%d" % i, [KH, SBC, 128], BF16) for i in range(4)]
            yb = [k.sb(st, "yb%d" % i, [64, SBC, 128], F32) for i in range(2)]
            psC = [k.ps(st, "psC%d" % i, [128, 512]) for i in range(2)]
            psY = k.ps(st, "psY", [128, 512])
            r_kft, r_Y, r_D, r_psC = R(2), R(2), R(2), R(2)
            r_tC = [R(NSUB), R(NSUB)]
            r_yb = [R(2), R(2)]
            r_psY = Res()
            r_mT = R(4)
            TreT = twb[0:KH, 0, :].unsqueeze(1).to_broadcast([KH, SBC, 128])
            TimT = twb[0:KH, 1, :].unsqueeze(1).to_broadcast([KH, SBC, 128])
        TreH = twb[:, 0, 0:KH].unsqueeze(1).to_broadcast([128, SBC, KH])
        TimH = twb[:, 1, 0:KH].unsqueeze(1).to_broadcast([128, SBC, KH])

        def cmul(a_re, a_im, b_re, b_im, o_re, o_im, conj, rs, ro, mm=mm, r_mm=r_mm):
            op(DVE, lambda: V.tensor_tensor(out=mm[0][:], in0=a_re, in1=b_re, op=ALU.mult), r=rs, w=[r_mm[0]])
            op(DVE, lambda: V.tensor_tensor(out=mm[1][:], in0=a_im, in1=b_im, op=ALU.mult), r=rs, w=[r_mm[1]])
            op(DVE, lambda: V.tensor_tensor(out=mm[2][:], in0=a_re, in1=b_im, op=ALU.mult), r=rs, w=[r_mm[2]])
            op(DVE, lambda: V.tensor_tensor(out=mm[3][:], in0=a_im, in1=b_re, op=ALU.mult), r=rs, w=[r_mm[3]])
            if not conj:
                op(DVE, lambda: V.tensor_tensor(out=o_re, in0=mm[0][:], in1=mm[1][:], op=ALU.subtract), r=[r_mm[0], r_mm[1]], w=[ro])
                op(DVE, lambda: V.tensor_tensor(out=o_im, in0=mm[2][:], in1=mm[3][:], op=ALU.add), r=[r_mm[2], r_mm[3]], w=[ro])
            else:
                op(DVE, lambda: V.tensor_tensor(out=o_re, in0=mm[0][:], in1=mm[1][:], op=ALU.add), r=[r_mm[0], r_mm[1]], w=[ro])
                op(DVE, lambda: V.tensor_tensor(out=o_im, in0=mm[3][:], in1=mm[2][:], op=ALU.subtract), r=[r_mm[2], r_mm[3]], w=[ro])

        def s_load(b):
            load_in(b, xin[b % 2], r_xin[b % 2])

        def s_S1(b):
            i = b % 2
            for sub in range(NSUB):
                s_ = sub % 2
                pa = psA[s_][:, 0:2 * 2 * KH].rearrange("p (c n) -> p c n", c=2)
                k._waits(PE, k._need([], [r_psA[s_]]))
                for cl in range(2):
                    c = sub * 2 + cl
                    mmg([(pa[:, cl, :], xin[i][0:Kdim, c, :], P1h[0:Kdim, :])], r=[r_xin[i], Rc], w=[r_psA[s_]] if cl == 1 else [])
                op(ACT, lambda sub=sub, pa=pa: S.copy(out=tA[i][:, sub * 2:sub * 2 + 2].rearrange("p c r k -> p c (r k)"), in_=pa),
                   r=[r_psA[s_]], w=[r_tA[i][sub]])

        def s_TW1(b):
            i = b % 2
            cmul(tA[i][:, :, 0, :], tA[i][:, :, 1, :], TreH, TimH, Bb[i][:, 0], Bb[i][:, 1], False, r_tA[i] + [Rc], r_B[i])

        def table_stage(src, r_src, ps, r_ps, dst, r_dst, i, fwd):
            for sub in range(NSUB):
                s_ = sub % 2
                px = ps[s_][:, 0:2 * 2 * KH].rearrange("p (r n) -> p r n", r=2)
                bre = src[i][:, 0, sub * 2:sub * 2 + 2, :].rearrange("p c k -> p (c k)")
                bim = src[i][:, 1, sub * 2:sub * 2 + 2, :].rearrange("p c k -> p (c k)")
                k._waits(PE, k._need([], [r_ps[s_]]))
                if fwd:
                    mmg([(px[:, 0, :], Cm, bre), (px[:, 0, :], Sm, bim)], r=[r_src[i], Rc], w=[])
                    mmg([(px[:, 1, :], NSm, bre), (px[:, 1, :], Cm, bim)], r=[r_src[i], Rc], w=[r_ps[s_]])
                else:
                    mmg([(px[:, 0, :], Cm, bre), (px[:, 0, :], NSm, bim)], r=[r_src[i], Rc], w=[])
                    mmg([(px[:, 1, :], Sm, bre), (px[:, 1, :], Cm, bim)], r=[r_src[i], Rc], w=[r_ps[s_]])
                op(ACT, lambda sub=sub, px=px: S.copy(out=dst[i][:, :, sub * 2:sub * 2 + 2, :],
                                                      in_=px.rearrange("p r (c k) -> p r c k", c=2)),
                   r=[r_ps[s_]], w=[r_dst[i][sub]])

        def s_S2(b):
            i = b % 2
            if not is_filter:
                dma(kft[i][:], kfs[:, :, b * SBC:(b + 1) * SBC, :], w=[r_kft[i]])
            table_stage(Bb, r_B, psX, r_psX, tX, r_tX, i, True)

        def s_PM(b):
            i = b % 2
            if is_filter:
                dma(kfs[:, :, b * SBC:(b + 1) * SBC, :], tX[i][:], r=r_tX[i])
            else:
                cmul(tX[i][:, 0], tX[i][:, 1], kft[i][:, 0], kft[i][:, 1], Yb[i][:, 0], Yb[i][:, 1], False, r_tX[i] + [r_kft[i]], r_Y[i])

        def s_S3(b):
            i = b % 2
            for sub in range(NSUB):
                s_ = sub % 2
                pc_ = psC[s_][0:KH, :].rearrange("p (c n) -> p c n", c=2)
                k._waits(PE, k._need([], [r_psC[s_]]))
                for cl in range(2):
                    c = sub * 2 + cl
                    mmg([(pc_[:, cl, :], Yb[i][:, 0, c, :], P2), (pc_[:, cl, :], Yb[i][:, 1, c, :], P3)],
                        r=[r_Y[i], Rc], w=[r_psC[s_]] if cl == 1 else [])
                op(ACT, lambda sub=sub, s_=s_: S.copy(out=tC[i][:, sub * 2:sub * 2 + 2].rearrange("p c r n -> p (c r n)"), in_=psC[s_][0:KH, :]),
                   r=[r_psC[s_]], w=[r_tC[i][sub]])

        def s_TW2(b):
            i = b % 2
            cmul(tC[i][:, :, 0, :], tC[i][:, :, 1, :], TreT, TimT, Dd[i][:, :, 0, :], Dd[i][:, :, 1, :], True, r_tC[i] + [Rc], r_D[i],
                 mm=mT, r_mm=r_mT)

        def s_S4(b):
            i = b % 2
            for h in range(2):
                mmg([(psY[0:64, :], C4w[0:KH, :], Dd[i][:, h * 4:(h + 1) * 4, 0, :]),
                     (psY[0:64, :], NS4w[0:KH, :], Dd[i][:, h * 4:(h + 1) * 4, 1, :])],
                    r=[r_D[i], Rc], w=[r_psY])
                op(ACT, lambda h=h: S.copy(out=yb[i][:, h * 4:(h + 1) * 4, :].rearrange("p c k -> p (c k)"), in_=psY[0:64, :]),
                   r=[r_psY], w=[r_yb[i][h]])
            store_out(b, yb[i], r_yb[i])

        stages = [s_load, s_S1, s_TW1, s_S2, s_PM]
        if not is_filter:
            stages += [s_S3, s_TW2, s_S4]
        ns = len(stages)
        for step in range(nsb + ns - 1):
            for si in range(ns - 1, -1, -1):
                b = step - si
                if 0 <= b < nsb:
                    stages[si](b)
        k.barrier()
        st.close()

    def gen_filter(l):
        st = ExitStack()
        w1 = k.sb(st, "fw1", [66, 128], F32)
        w2 = k.sb(st, "fw2", [128, 128], F32)
        w3 = k.sb(st, "fw3", [128, 128], F32)
        w4 = k.sb(st, "fw4", [128, 1024], F32)
        zt = [k.sb(st, "zt%d" % i, [66, 512], F32) for i in range(2)]
        tb = [k.sb(st, "tb%d" % i, [128, 2, 512], F32) for i in range(2)]
        ha = [k.sb(st, "ha%d" % i, [128, 512], F32) for i in range(3)]
        hb = [k.sb(st, "hb%d" % i, [128, 512], F32) for i in range(3)]
        hc = [k.sb(st, "hc%d" % i, [128, 512], F32) for i in range(3)]
        hh = [[k.sb(st, "hh%d%d" % (j, i), [128, 512], F32) for i in range(2)] for j in range(3)]
        win = [k.sb(st, "fwin%d" % i, [128, 512], F32) for i in range(2)]
        kst = [k.sb(st, "kst%d" % i, [128, 2, 4, 512], BF16) for i in range(2)]
        psh = [k.ps(st, "psh%d" % i, [128, 512]) for i in range(3)]
        psk = [k.ps(st, "psk%d" % i, [128, 512]) for i in range(2)]
        R3 = lambda: [Res(), Res(), Res()]
        Rw = Res()
        r_ha, r_hb, r_hc, r_psh = R3(), R3(), R3(), R3()
        r_zt, r_tb, r_kst, r_psk, r_win = [Res(), Res()], [Res(), Res()], [Res(), Res()], [Res(), Res()], [Res(), Res()]
        r_hh = [[Res(), Res()] for _ in range(3)]
        op(DVE, lambda: V.memset(w1[:], 0.0), w=[Rw])
        op(DVE, lambda: V.memset(w2[:], 0.0), w=[Rw])
        op(DVE, lambda: V.memset(w3[:], 0.0), w=[Rw])
        for hf in range(2):
            dma(w1[hf * 33:(hf + 1) * 33, hf * 64:(hf + 1) * 64], fw1[l], w=[Rw])
            dma(w2[hf * 64:(hf + 1) * 64, hf * 64:(hf + 1) * 64], fw2[l], w=[Rw])
            dma(w3[hf * 64:(hf + 1) * 64, hf * 64:(hf + 1) * 64], fw3[l], w=[Rw])
            dma(w4[hf * 64:(hf + 1) * 64, :], fw4[l], w=[Rw])
        for j in range(3):
            op(DVE, lambda j=j: V.tensor_tensor(out=frb[:, j:j + 1], in0=sp[:, C_FB + 2 * j:C_FB + 2 * j + 1],
                                                in1=sp[:, C_FB + 2 * j + 1:C_FB + 2 * j + 2], op=ALU.mult), r=[Rsp], w=[Rw])
        wl = [w1, w2, w3]

        def f_load(ch):
            i = ch % 2
            Q = []
            for hf in range(2):
                Q.append(lambda hf=hf: dma(zt[i][hf * 33:(hf + 1) * 33, :], cz[hf, :, ch * 512:(ch + 1) * 512], w=[r_zt[i]]))
            return Q

        def f_layer(j):
            def f(ch):
                i = ch % 2
                src = zt[i][:] if j == 0 else hh[j - 1][i][:]
                rsrc = r_zt[i] if j == 0 else r_hh[j - 1][i]
                Q = []
                Q.append(lambda: mmg([(psh[j][:], wl[j][:], src)], r=[Rw, rsrc], w=[r_psh[j]]))
                Q.append(lambda: op(ACT, lambda: S.activation(out=ha[j][:], in_=psh[j][:], func=AF.Identity, bias=frb[:, j:j + 1],
                                                              scale=sp[:, C_FB + 2 * j + 1:C_FB + 2 * j + 2]), r=[r_psh[j], Rw, Rsp], w=[r_ha[j]]))
                Q.append(lambda: op(DVE, lambda: V.tensor_scalar(out=hb[j][:], in0=ha[j][:], scalar1=math.pi, scalar2=-2.0 * math.pi,
                                                                 op0=ALU.is_gt, op1=ALU.mult), r=[r_ha[j]], w=[r_hb[j]]))
                Q.append(lambda: op(DVE, lambda: V.tensor_scalar(out=hc[j][:], in0=ha[j][:], scalar1=-math.pi, scalar2=2.0 * math.pi,
                                                                 op0=ALU.is_lt, op1=ALU.mult), r=[r_ha[j]], w=[r_hc[j]]))
                Q.append(lambda: op(DVE, lambda: V.tensor_tensor(out=hb[j][:], in0=hb[j][:], in1=hc[j][:], op=ALU.add), r=[r_hb[j], r_hc[j]], w=[r_hb[j]]))
                Q.append(lambda: op(DVE, lambda: V.tensor_tensor(out=hb[j][:], in0=hb[j][:], in1=ha[j][:], op=ALU.add), r=[r_hb[j], r_ha[j]], w=[r_hb[j]]))
                Q.append(lambda: op(ACT, lambda: S.activation(out=hh[j][i][:], in_=hb[j][:], func=AF.Sin), r=[r_hb[j]], w=[r_hh[j][i]]))
                if j == 2:
                    for hf in range(2):
                        Q.append(lambda hf=hf: dma(tb[i][:, hf, :], ctp[hf, ch * 512:(ch + 1) * 512].partition_broadcast(128), w=[r_tb[i]]))
                return Q
            return f

        def f_out(ch):
            i = ch % 2
            Q = []
            n = 0
            for hf in range(2):
                for j4 in range(4):
                    pi = n % 2
                    n += 1
                    Q.append(lambda hf=hf, j4=j4, pi=pi: mmg(
                        [(psk[pi][:], w4[hf * 64:(hf + 1) * 64, hf * 512 + j4 * 128: hf * 512 + (j4 + 1) * 128], hh[2][i][hf * 64:(hf + 1) * 64, :])],
                        r=[Rw, r_hh[2][i]], w=[r_psk[pi]]))
                    Q.append(lambda hf=hf, j4=j4, pi=pi: op(ACT, lambda: S.activation(out=win[pi][:], in_=tb[i][:, hf, :], func=AF.Exp, scale=ndl[:, j4:j4 + 1]),
                                                            r=[r_tb[i], Rc], w=[r_win[pi]]))
                    Q.append(lambda hf=hf, j4=j4, pi=pi: op(DVE, lambda: V.tensor_tensor(out=kst[i][:, hf, j4, :], in0=psk[pi][:], in1=win[pi][:], op=ALU.mult),
                                                            r=[r_psk[pi], r_win[pi]], w=[r_kst[i]]))
            if ch == 0:
                Q.append(lambda: op(DVE, lambda: V.memset(kst[i][:, 1, :, 0:1], 0.0), w=[r_kst[i]]))
            for hf in range(2):
                c0 = hf * L + ch * 512
                Q.append(lambda hf=hf, c0=c0: dma(kfil.rearrange("(j p) t -> p j t", p=128)[:, :, c0:c0 + 512], kst[i][:, hf], r=[r_kst[i]]))
            return Q

        fst = [f_load, f_layer(0), f_layer(1), f_layer(2), f_out]
        for step in range(NT + len(fst) - 1):
            qs = []
            for si in range(len(fst) - 1, -1, -1):
                ch = step - si
                if 0 <= ch < NT:
                    qs.append(fst[si](ch))
            while any(qs):
                for q in qs:
                    take = 3 if len(q) > 12 else 1
                    for _ in range(take):
                        if q:
                            q.pop(0)()
        k.barrier()
        st.close()
        kv = kfil.rearrange("c (a b) -> a c b", b=128)

        def load_f(b, dst, rdst):
            dma(dst[:], kv[:, b * SBC:(b + 1) * SBC, :], w=[rdst])

        fft_pipeline(512 // SBC, 128, load_f, True)

    def load_cast(dst, src_l, ncols, gcol, wst, r_wst, r_dst):
        srcv = src_l.rearrange("(kt p) c -> p kt c", p=128)
        half = 1664
        it = 0
        for kt in range(8):
            for c0 in range(0, ncols, half):
                cw = min(half, ncols - c0)
                i = it % 2
                it += 1
                dma(wst[i][:, 0:cw], srcv[:, kt, c0:c0 + cw], w=[r_wst[i]])
                if gcol is None:
                    if it % 2:
                        op(ACT, lambda i=i, kt=kt, c0=c0, cw=cw: S.copy(out=dst[:, kt, c0:c0 + cw], in_=wst[i][:, 0:cw]),
                           r=[r_wst[i]], w=[r_dst])
                    else:
                        op(DVE, lambda i=i, kt=kt, c0=c0, cw=cw: V.tensor_copy(out=dst[:, kt, c0:c0 + cw], in_=wst[i][:, 0:cw]),
                           r=[r_wst[i]], w=[r_dst])
                else:
                    if it % 2:
                        op(ACT, lambda i=i, kt=kt, c0=c0, cw=cw: S.activation(out=dst[:, kt, c0:c0 + cw], in_=wst[i][:, 0:cw],
                                                                              func=AF.Identity, scale=sp[:, gcol + kt:gcol + kt + 1]),
                           r=[r_wst[i], Rsp], w=[r_dst])
                    else:
                        op(DVE, lambda i=i, kt=kt, c0=c0, cw=cw: V.tensor_scalar(out=dst[:, kt, c0:c0 + cw], in0=wst[i][:, 0:cw],
                                                                                 scalar1=sp[:, gcol + kt:gcol + kt + 1], scalar2=None,
                                                                                 op0=ALU.mult),
                           r=[r_wst[i], Rsp], w=[r_dst])

    kT = k.sb(top, "kT", [128, 2, 256], BF16)
    vpad = k.sb(top, "vpad", [128, 4, 2, 128], BF16)
    r_kT, r_vpad = Res(), Res()

    def kv_prep(l, s, with_barrier):
        st2 = ExitStack()
        memx = k.sb(st2, "memx", [128, 8, 256], F32)
        msq = k.sb(st2, "msq", [128, 8, 256], BF16)
        memn = k.sb(st2, "memn", [128, 8, 256], BF16)
        mrs = k.sb(st2, "mrs", [128, 256], F32)
        pkv = k.ps(st2, "pkv", [128, 256])
        r_memx, r_msq, r_memn, r_mrs, r_pkv = Res(), Res(), Res(), Res(), Res()
        dma(memx[:], memT[s].rearrange("(kt p) m -> p kt m", p=128), w=[r_memx])
        op(ACT, lambda: S.activation(out=msq[:], in_=memx[:], func=AF.Square), r=[r_memx], w=[r_msq])
        mmg([(pkv[:], ones_d[:], msq[:, kt, :]) for kt in range(8)], r=[r_msq, Rc], w=[r_pkv])
        op(ACT, lambda: S.activation(out=mrs[:], in_=pkv[:], func=AF.Ln, bias=epsb[:, 0:1], scale=1.0), r=[r_pkv, Rc], w=[r_mrs])
        op(ACT, lambda: S.activation(out=mrs[:], in_=mrs[:], func=AF.Exp, scale=-0.5), r=[r_mrs], w=[r_mrs])
        for kt in range(8):
            op(DVE, lambda kt=kt: V.scalar_tensor_tensor(out=memn[:, kt, :], in0=memx[:, kt, :], scalar=sp[:, C_MNG + kt:C_MNG + kt + 1],
                                                         in1=mrs[:], op0=ALU.mult, op1=ALU.mult), r=[r_memx, r_mrs, Rsp], w=[r_memn])
        for g in range(2):
            mmg([(pkv[:], wkv[:, kt, g * 128:(g + 1) * 128], memn[:, kt, :]) for kt in range(8)], r=[r_wkv, r_memn], w=[r_pkv])
            op(DVE, lambda g=g: V.tensor_copy(out=kT[:, g, :], in_=pkv[:]), r=[r_pkv], w=[r_kT])
        op(POOL, lambda: G.memset(vpad[:], 0.0), w=[r_vpad])
        for mt in range(2):
            mmg([(pkv[:], memn[:, kt, mt * 128:(mt + 1) * 128], wkv[:, kt, 256:512]) for kt in range(8)], r=[r_wkv, r_memn], w=[r_pkv])
            for h in range(4):
                hb = (h % 2) * 64
                op(DVE, lambda h=h, hb=hb, mt=mt: V.tensor_copy(out=vpad[:, h, mt, hb:hb + 64], in_=pkv[:, h * 64:(h + 1) * 64]),
                   r=[r_pkv], w=[r_vpad])
        return st2

    def phase_a(l, s, xsrc):
        st = ExitStack()
        xt = [k.sb(st, "xt%d" % i, [128, 8, 512], F32) for i in range(2)]
        sq = k.sb(st, "sq", [128, 8, 512], BF16)
        hh = [k.sb(st, "hh%d" % i, [128, 8, 512], BF16) for i in range(2)]
        stage = k.sb(st, "stage", [128, 4, 512], BF16)
        sg = k.sb(st, "sg", [128, 2, 512], F32)
        r_sg = [Res(), Res()]
        GORD = list(range(16)) + [18, 19, 16, 17] + list(range(20, 26))
        hst = k.sb(st, "hst", [128, 12, 515], BF16)
        zst = k.sb(st, "zst", [128, 4, 513], BF16)
        tm1 = k.sb(st, "tm1", [128, 4, 513], F32)
        tm2 = k.sb(st, "tm2", [128, 4, 513], F32)
        vxo = k.sb(st, "vxo", [128, 4, 513], BF16)
        g0o = k.sb(st, "g0o", [128, 4, 513], BF16)
        qT = [k.sb(st, "qT%d" % i, [128, 2, 512], BF16) for i in range(2)]
        zs = [k.sb(st, "zs%d" % i, [128, 2, 512], BF16) for i in range(2)]
        pT = k.sb(st, "pT", [128, 4, 512], BF16)
        rstd = k.sb(st, "rstd", [128, 512], F32)
        rden = k.sb(st, "rden", [128, 512], F32)
        ot = k.sb(st, "ot", [128, 512], F32)
        yxa = [k.sb(st, "yxa%d" % i, [128, 2, 512], BF16) for i in range(2)]
        pj = [k.ps(st, "pj%d" % i, [128, 512]) for i in range(4)]
        sc = [k.ps(st, "sc_%d" % i, [128, 512]) for i in range(2)]
        pv = k.ps(st, "pv", [128, 512])
        dn = k.ps(st, "dn", [128, 512])
        nrm = dn
        r_xt, r_pj, r_sc = [Res(), Res()], [Res() for _ in range(4)], [Res(), Res()]
        r_rstd, r_rden, r_ot, r_pv, r_dn, r_tm1, r_tm2, r_vxo, r_g0o, r_zst = (Res() for _ in range(10))
        r_nrm = r_dn
        r_sq = [Res() for _ in range(4)]
        r_hh, r_qT, r_zs, r_yxa = [Res(), Res()], [Res(), Res()], [Res(), Res()], [Res(), Res()]
        r_stage = [Res() for _ in range(4)]
        r_hst = [Res() for _ in range(12)]
        r_pT = [Res() for _ in range(4)]
        xv = xsrc.rearrange("(kt p) t -> p kt t", p=128)
        pv3 = pscr.rearrange("(g p) t -> p g t", p=128)
        mv3 = mixs.rearrange("(g p) t -> p g t", p=128)
        vxv = vx1s.rearrange("(j p) t -> p j t", p=128)
        g0v = g0s.rearrange("(j p) t -> p j t", p=128)
        op(POOL, lambda: G.memset(hst[:], 0.0), w=r_hst)
        op(POOL, lambda: G.memset(zst[:], 0.0), w=[r_zst])

        def norm_a(i, q):
            b = i % 2
            op(ACT, lambda: S.activation(out=sq[:, 2 * q:2 * q + 2, :], in_=xt[b][:, 2 * q:2 * q + 2, :], func=AF.Square), r=[r_xt[b]], w=[r_sq[q]])

        def norm(i):
            mmg([(nrm[:], ones_d[:], sq[:, kt, :]) for kt in range(8)], r=r_sq + [Rc], w=[r_nrm])
            op(ACT, lambda: S.activation(out=rstd[:], in_=nrm[:], func=AF.Ln, bias=epsb[:, 0:1], scale=1.0), r=[r_nrm, Rc], w=[r_rstd])
            op(ACT, lambda: S.activation(out=rstd[:], in_=rstd[:], func=AF.Exp, scale=-0.5), r=[r_rstd], w=[r_rstd])

        def norm_h(i, q):
            b = i % 2
            for kt in (2 * q, 2 * q + 1):
                op(DVE, lambda kt=kt: V.scalar_tensor_tensor(out=hh[b][:, kt, :], in0=xt[b][:, kt, :], scalar=sp[:, C_NG + kt:C_NG + kt + 1],
                                                             in1=rstd[:], op0=ALU.mult, op1=ALU.mult), r=[r_xt[b], r_rstd, Rsp], w=[r_hh[b]])

        def attn_micro(i):
            b = i % 2
            t0 = i * 512
            steps = []
            for pr in range(2):
                for hl in range(2):
                    for mt in range(2):
                        def f(pr=pr, hl=hl, mt=mt):
                            hb = hl * 64
                            pi = hl * 2 + mt
                            mmg([(sc[mt][:], kT[hb:hb + 64, pr, mt * 128:(mt + 1) * 128], qT[b][hb:hb + 64, pr, :])],
                                r=[r_kT, r_qT[b]], w=[r_sc[mt]])
                            op(ACT, lambda: S.activation(out=pT[:, pi, :], in_=sc[mt][:], func=AF.Exp, scale=0.125),
                               r=[r_sc[mt]], w=[r_pT[pi]])
                        steps.append(f)

                def fin(pr=pr):
                    mmg([(pv[:], vpad[:, 2 * pr + (pi // 2), pi % 2, :], pT[:, pi, :]) for pi in range(4)], r=[r_vpad] + r_pT, w=[r_pv])
                    mmg([(dn[:], ind[:, pi // 2, :], pT[:, pi, :]) for pi in range(4)], r=[Rc] + r_pT, w=[r_dn])
                    op(ACT, lambda: S.activation(out=rden[:], in_=dn[:], func=AF.Ln), r=[r_dn], w=[r_rden])
                    op(ACT, lambda: S.activation(out=rden[:], in_=rden[:], func=AF.Exp, scale=-1.0), r=[r_rden], w=[r_rden])
                    op(DVE, lambda: V.tensor_tensor(out=ot[:], in0=pv[:], in1=rden[:], op=ALU.mult), r=[r_pv, r_rden], w=[r_ot])
                    op(DVE, lambda: V.tensor_tensor(out=yxa[b][:, pr, :], in0=ot[:], in1=zs[b][:, pr, :], op=ALU.mult),
                       r=[r_ot, r_zs[b]], w=[r_yxa[b]])
                    if pr == 1:
                        dma(mv3[:, 6:8, t0:t0 + 512], yxa[b][:], r=[r_yxa[b]])
                steps.append(fin)
            return steps

        def hy_micro(i):
            t0 = i * 512
            W = 513 if i == NT - 1 else 512
            steps = []

            def conv_step(g, dst, rdst, jj):
                def f():
                    w0 = sp[:, C_SHW + g * 3:C_SHW + g * 3 + 1]
                    w1 = sp[:, C_SHW + g * 3 + 1:C_SHW + g * 3 + 2]
                    w2 = sp[:, C_SHW + g * 3 + 2:C_SHW + g * 3 + 3]
                    bb = sp[:, C_SHB + g:C_SHB + g + 1]
                    o = dst[:, jj, 0:W]
                    op(ACT, lambda: S.activation(out=o, in_=hst[:, g, 1:1 + W], func=AF.Identity, bias=bb, scale=w1),
                       r=[r_hst[g], Rsp], w=[rdst[jj]])
                    op(DVE, lambda: V.scalar_tensor_tensor(out=o, in0=hst[:, g, 0:W], scalar=w0, in1=o, op0=ALU.mult, op1=ALU.add),
                       r=[r_hst[g], Rsp, rdst[jj]], w=[rdst[jj]])
                    op(DVE, lambda: V.scalar_tensor_tensor(out=o, in0=hst[:, g, 2:2 + W], scalar=w2, in1=o, op0=ALU.mult, op1=ALU.add),
                       r=[r_hst[g], Rsp, rdst[jj]], w=[rdst[jj]])
                    op(DVE, lambda: V.tensor_copy(out=hst[:, g, 0:2], in_=hst[:, g, 512:514]), r=[r_hst[g]], w=[r_hst[g]])
                return f

            def store(dstv, src, rsrc):
                if i == 0:
                    dma(dstv[:, :, 0:W - 1], src[:, :, 1:W], r=rsrc)
                else:
                    dma(dstv[:, :, t0 - 1:t0 - 1 + W], src[:, :, 0:W], r=rsrc)

            def silu_z():
                op(ACT, lambda: S.activation(out=tm2[:, :, 0:W], in_=zst[:, :, 0:W], func=AF.Silu), r=[r_zst], w=r_t2)
                op(DVE, lambda: V.tensor_copy(out=zst[:, :, 0:1], in_=zst[:, :, 512:513]), r=[r_zst], w=[r_zst])
            steps.append(silu_z)
            for jj in range(4):
                steps.append(conv_step(jj, tm1, r_t1, jj))

            def g0_prod():
                op(DVE, lambda: V.tensor_tensor(out=g0o[:, :, 0:W], in0=tm1[:, :, 0:W], in1=tm2[:, :, 0:W], op=ALU.mult),
                   r=r_t1 + r_t2, w=[r_g0o])
                store(g0v, g0o, [r_g0o])
            steps.append(g0_prod)
            for jj in range(4):
                steps.append(conv_step(4 + jj, tm1, r_t1, jj))
            for jj in range(4):
                steps.append(conv_step(8 + jj, tm2, r_t2, jj))

            def vx_prod():
                op(DVE, lambda: V.tensor_tensor(out=vxo[:, :, 0:W], in0=tm1[:, :, 0:W], in1=tm2[:, :, 0:W], op=ALU.mult),
                   r=r_t1 + r_t2, w=[r_vxo])
                store(vxv, vxo, [r_vxo])
            steps.append(vx_prod)
            return steps

        r_t1 = [Res() for _ in range(4)]
        r_t2 = [Res() for _ in range(4)]
        dma(xt[0][:], xv[:, :, 0:512], w=[r_xt[0]])
        for q in range(4):
            norm_a(0, q)
        norm(0)
        for q in range(4):
            norm_h(0, q)
        pending = []
        ev = 0
        gi = 0
        for i in range(NT):
            b = i % 2
            t0 = i * 512
            if i + 1 < NT:
                dma(xt[1 - b][:], xv[:, :, t0 + 512:t0 + 1024], w=[r_xt[1 - b]])
            for gq, g in enumerate(GORD):
                pi = gi % 4
                gi += 1
                mmg([(pj[pi][:], win[:, kt, g * 128:(g + 1) * 128], hh[b][:, kt, :]) for kt in range(8)], r=[r_win, r_hh[b]], w=[r_pj[pi]])
                if g < 16:
                    if g < 12:
                        dst, rd = hst[:, g, 2:514], r_hst[g]
                    else:
                        dst, rd = zst[:, g - 12, 1:513], r_zst
                    ev += 1
                    if ev % 4:
                        op(ACT, lambda dst=dst, pi=pi: S.copy(out=dst, in_=pj[pi][:]), r=[r_pj[pi]], w=[rd])
                    else:
                        op(DVE, lambda dst=dst, pi=pi: V.tensor_copy(out=dst, in_=pj[pi][:]), r=[r_pj[pi]], w=[rd])
                elif g in (18, 19):
                    op(ACT, lambda g=g, pi=pi: S.activation(out=sg[:, g - 18, :], in_=pj[pi][:], func=AF.Sigmoid), r=[r_pj[pi]], w=[r_sg[g - 18]])
                elif g in (16, 17):
                    op(DVE, lambda g=g, pi=pi: V.tensor_tensor(out=stage[:, g - 16, :], in0=pj[pi][:], in1=sg[:, g - 16, :], op=ALU.mult),
                       r=[r_pj[pi], r_sg[g - 16]], w=[r_stage[g - 16]])
                elif g in (20, 21):
                    op(ACT, lambda g=g, pi=pi: S.activation(out=stage[:, g - 18, :], in_=pj[pi][:], func=AF.Silu), r=[r_pj[pi]], w=[r_stage[g - 18]])
                    if g == 21:
                        dma(pv3[:, 16:20, t0:t0 + 512], stage[:], r=r_stage)
                elif g < 24:
                    op(DVE, lambda g=g, pi=pi: V.tensor_copy(out=qT[b][:, g - 22, :], in_=pj[pi][:]), r=[r_pj[pi]], w=[r_qT[b]])
                else:
                    op(ACT, lambda g=g, pi=pi: S.activation(out=zs[b][:, g - 24, :], in_=pj[pi][:], func=AF.Silu), r=[r_pj[pi]], w=[r_zs[b]])
                if gq < 4 and i + 1 < NT:
                    norm_a(i + 1, gq)
                if gq == 7 and i + 1 < NT:
                    norm(i + 1)
                if 10 <= gq < 14 and i + 1 < NT:
                    norm_h(i + 1, gq - 10)
                if gq == 15:
                    pending = pending + hy_micro(i)
                if pending and (gq >= 16 or i > 0):
                    pending.pop(0)()
            pending = pending + attn_micro(i)
        while pending:
            pending.pop(0)()
        k.barrier()
        st.close()

    def phase_b2():
        xv = vx1s.rearrange("c (a b) -> a c b", b=128)
        yv = yscr.rearrange("c (a b) -> a c b", b=128)

        def load_d(b, dst, rdst):
            dma(dst[0:64, :, :], xv[:, b * SBC:(b + 1) * SBC, :], w=[rdst])

        def store_d(b, src, rsrc):
            dma(yv[:, b * SBC:(b + 1) * SBC, :], src[:], r=rsrc)

        fft_pipeline(512 // SBC, 64, load_d, False, store_d)

    CH = 2048

    def phase_b3():
        st = ExitStack()
        CW = 1024
        NB = 3
        g0c = [k.sb(st, "g0c%d" % i, [128, CW], BF16) for i in range(NB)]
        vxc = [k.sb(st, "vxc%d" % i, [128, CW], BF16) for i in range(NB)]
        yc = [k.sb(st, "yc%d" % i, [128, CW], F32) for i in range(NB)]
        ttc = [k.sb(st, "ttc%d" % i, [128, CW], F32) for i in range(NB)]
        ohc = [k.sb(st, "ohc%d" % i, [128, CW], BF16) for i in range(NB)]
        RN = lambda: [Res() for _ in range(NB)]
        r_g0c, r_vxc, r_yc, r_ttc, r_ohc = RN(), RN(), RN(), RN(), RN()
        chunks = [(j, c0) for j in range(4) for c0 in range(0, L, CW)]

        def b3_load(n):
            j, c0 = chunks[n]
            i = n % NB
            dma(g0c[i][:], g0s[j * 128:(j + 1) * 128, c0:c0 + CW], w=[r_g0c[i]])
            dma(vxc[i][:], vx1s[j * 128:(j + 1) * 128, c0:c0 + CW], w=[r_vxc[i]])
            dma(yc[i][:], yscr[j * 128:(j + 1) * 128, c0:c0 + CW], w=[r_yc[i]], Q=ACT)

        def b3_comp(n):
            j, c0 = chunks[n]
            i = n % NB
            op(DVE, lambda: V.scalar_tensor_tensor(out=ttc[i][:], in0=vxc[i][:], scalar=sp[:, C_SKIP + j:C_SKIP + j + 1],
                                                   in1=yc[i][:], op0=ALU.mult, op1=ALU.add), r=[r_vxc[i], r_yc[i], Rsp], w=[r_ttc[i]])
            op(DVE, lambda: V.tensor_tensor(out=ohc[i][:], in0=ttc[i][:], in1=g0c[i][:], op=ALU.mult), r=[r_ttc[i], r_g0c[i]], w=[r_ohc[i]])
            dma(mixs[j * 128:(j + 1) * 128, c0:c0 + CW], ohc[i][:], r=[r_ohc[i]], Q=POOL)

        for n in range(min(NB - 1, len(chunks))):
            b3_load(n)
        for n in range(len(chunks)):
            if n + NB - 1 < len(chunks):
                b3_load(n + NB - 1)
            b3_comp(n)
        k.barrier()
        st.close()

    def phase_b4():
        st = ExitStack()
        hrow = k.sb(st, "hrow", [128, 2, L + 30], BF16)
        r_hrow = [Res(), Res()]
        diag = k.sb(st, "diag", [128, 62, 128], BF16)
        r_diag = [Res() for _ in range(62)]
        for jk in range(62):
            op(DVE, lambda jk=jk: V.tensor_scalar(out=diag[:, jk, :], in0=ident[:], scalar1=sp[:, C_DWW + jk:C_DWW + jk + 1], scalar2=None,
                                                  op0=ALU.mult), r=[Rc, Rsp], w=[r_diag[jk]])
        for j in range(2):
            op(POOL, lambda j=j: G.memset(hrow[:, j, 0:15], 0.0), w=[r_hrow[j]])
            op(POOL, lambda j=j: G.memset(hrow[:, j, L + 15:L + 30], 0.0), w=[r_hrow[j]])
            dma(hrow[:, j, 15:15 + L], pscr[(16 + j) * 128:(17 + j) * 128, :], w=[r_hrow[j]])
        czc = [k.sb(st, "czc%d" % i, [128, 2, 512], BF16) for i in range(4)]
        ouc = [k.sb(st, "ouc%d" % i, [128, 2, 512], BF16) for i in range(2)]
        r_czc, r_ouc = [Res() for _ in range(4)], [Res(), Res()]
        czv = pscr[18 * 128:20 * 128, :].rearrange("(j p) t -> p j t", p=128)
        mxv = mixs[512:768, :].rearrange("(j p) t -> p j t", p=128)
        P2_ = range(2)
        cb = [k.sb(st, "cb%d" % i, [128, 2, 512], F32) for i in range(4)]
        cbb = [k.sb(st, "cbb%d" % i, [128, 2, 512], BF16) for i in P2_]
        sqb = [k.sb(st, "sqb%d" % i, [128, 2, 512], BF16) for i in P2_]
        m2 = [k.sb(st, "m2%d" % i, [128, 512], F32) for i in P2_]
        var = [k.sb(st, "var%d" % i, [128, 512], F32) for i in P2_]
        dd = [k.sb(st, "dd%d" % i, [128, 2, 512], F32) for i in P2_]
        zz = [k.sb(st, "zz%d" % i, [128, 2, 512], F32) for i in P2_]
        pc = [[k.ps(st, "pc%d%d" % (i, j), [128, 512]) for j in range(2)] for i in P2_]
        pm = [k.ps(st, "pm%d" % i, [128, 512]) for i in P2_]
        pq = [k.ps(st, "pq%d" % i, [128, 512]) for i in P2_]
        RR = lambda: [Res(), Res()]
        r_cb, r_cbb, r_sqb, r_m2, r_var, r_zz, r_pm, r_pq = [Res() for _ in range(4)], RR(), RR(), RR(), RR(), RR(), RR(), RR()
        r_dd = [RR(), RR()]
        r_pc = [RR(), RR()]

        def c0_(ci):
            i = ci % 2
            t0 = ci * 512
            for j in range(2):
                mmg([(pc[i][j][:], diag[:, j * 31 + kk, :], hrow[:, j, t0 + kk:t0 + kk + 512]) for kk in range(31)],
                    r=r_diag[j * 31:(j + 1) * 31] + [r_hrow[j]], w=[r_pc[i][j]])

        def c1_(ci):
            i = ci % 2
            for j in range(2):
                op(ACT, lambda j=j: S.activation(out=cb[ci % 4][:, j, :], in_=pc[i][j][:], func=AF.Identity, bias=sp[:, C_DWB + j:C_DWB + j + 1], scale=1.0),
                   r=[r_pc[i][j], Rsp], w=[r_cb[ci % 4]])

        def c2_(ci):
            i = ci % 2
            op(DVE, lambda: V.tensor_copy(out=cbb[i][:], in_=cb[ci % 4][:]), r=[r_cb[ci % 4]], w=[r_cbb[i]])
            op(ACT, lambda: S.activation(out=sqb[i][:], in_=cb[ci % 4][:], func=AF.Square), r=[r_cb[ci % 4]], w=[r_sqb[i]])
            mmg([(pm[i][:], ones_c[:], cbb[i][:, j, :]) for j in range(2)], r=[Rc, r_cbb[i]], w=[r_pm[i]])
            mmg([(pq[i][:], ones_c[:], sqb[i][:, j, :]) for j in range(2)], r=[Rc, r_sqb[i]], w=[r_pq[i]])

        def c3_(ci):
            i = ci % 2
            dma(czc[ci % 4][:], czv[:, :, ci * 512:ci * 512 + 512], w=[r_czc[ci % 4]], Q=ACT)
            op(ACT, lambda: S.activation(out=m2[i][:], in_=pm[i][:], func=AF.Square), r=[r_pm[i]], w=[r_m2[i]])
            op(DVE, lambda: V.tensor_tensor(out=var[i][:], in0=pq[i][:], in1=m2[i][:], op=ALU.subtract), r=[r_pq[i], r_m2[i]], w=[r_var[i]])
            op(DVE, lambda: V.tensor_scalar(out=var[i][:], in0=var[i][:], scalar1=0.0, scalar2=None, op0=ALU.max), r=[r_var[i]], w=[r_var[i]])
            op(ACT, lambda: S.activation(out=var[i][:], in_=var[i][:], func=AF.Sqrt, bias=EPS, scale=1.0), r=[r_var[i]], w=[r_var[i]])
            op(DVE, lambda: V.reciprocal(out=var[i][:], in_=var[i][:]), r=[r_var[i]], w=[r_var[i]])

        def c4_(ci):
            i = ci % 2
            for j in range(2):
                op(DVE, lambda j=j: V.tensor_tensor(out=dd[i][:, j, :], in0=cb[ci % 4][:, j, :], in1=pm[i][:], op=ALU.subtract),
                   r=[r_cb[ci % 4], r_pm[i]], w=[r_dd[i][j]])
            for j in range(2):
                op(DVE, lambda j=j: V.tensor_tensor(out=dd[i][:, j, :], in0=dd[i][:, j, :], in1=var[i][:], op=ALU.mult),
                   r=[r_dd[i][j], r_var[i]], w=[r_dd[i][j]])
            for j in range(2):
                op(ACT, lambda j=j: S.activation(out=dd[i][:, j, :], in_=dd[i][:, j, :], func=AF.Silu, bias=sp[:, C_LNB + j:C_LNB + j + 1],
                                                 scale=sp[:, C_LNG + j:C_LNG + j + 1]), r=[r_dd[i][j], Rsp], w=[r_dd[i][j]])

        def c5_(ci):
            i = ci % 2
            t0 = ci * 512
            op(DVE, lambda: V.tensor_tensor(out=ouc[i][:], in0=dd[i][:], in1=czc[ci % 4][:], op=ALU.mult),
               r=r_dd[i] + [r_czc[ci % 4]], w=[r_ouc[i]])
            dma(mxv[:, :, t0:t0 + 512], ouc[i][:], r=[r_ouc[i]])

        cst = [c0_, c1_, c2_, c3_, c4_, c5_]
        nsteps = NT + len(cst) - 1
        for step in range(nsteps):
            for si in range(len(cst) - 1, -1, -1):
                ci = step - si
                if 0 <= ci < NT:
                    cst[si](ci)
        k.barrier()
        st.close()

    def phase_c(l, s, xsrc, last, wo, r_wo, nxt):
        st_kv = kv_prep(nxt[0], nxt[1], False) if nxt is not None else None
        st = ExitStack()
        mt_ = [k.sb(st, "mixt%d" % i, [128, 8, 512], BF16) for i in range(2)]
        NXB = 4 if last else 3
        xt = [k.sb(st, "xtc%d" % i, [128, 8, 512], F32) for i in range(NXB)]
        xo = xt
        po = [k.ps(st, "po%d" % i, [128, 512]) for i in range(4)]
        r_xt = [Res() for _ in range(NXB)]
        r_xo = r_xt
        r_mt, r_po = [Res(), Res()], [Res() for _ in range(4)]
        xv = xsrc.rearrange("(kt p) t -> p kt t", p=128)
        ov = yT[s].rearrange("(kt p) t -> p kt t", p=128)
        mv3 = mixs.rearrange("(g p) t -> p g t", p=128)
        if last:
            sq = k.sb(st, "sqc", [128, 8, 512], BF16)
            rstd = k.sb(st, "rstdc", [128, 512], F32)
            pn = k.ps(st, "pn", [128, 512])
            r_sq, r_rstd, r_pn = [Res() for _ in range(4)], Res(), Res()

        def fin_a(i, q):
            b = i % NXB
            op(ACT, lambda: S.activation(out=sq[:, 2 * q:2 * q + 2, :], in_=xo[b][:, 2 * q:2 * q + 2, :], func=AF.Square), r=[r_xo[b]], w=[r_sq[q]])

        def fin_b(i):
            mmg([(pn[:], ones_d[:], sq[:, kt, :]) for kt in range(8)], r=r_sq + [Rc], w=[r_pn])
            op(ACT, lambda: S.activation(out=rstd[:], in_=pn[:], func=AF.Ln, bias=epsb[:, 0:1], scale=1.0), r=[r_pn, Rc], w=[r_rstd])
            op(ACT, lambda: S.activation(out=rstd[:], in_=rstd[:], func=AF.Exp, scale=-0.5), r=[r_rstd], w=[r_rstd])

        def fin_c(i, g):
            b = i % NXB
            op(DVE, lambda: V.scalar_tensor_tensor(out=xo[b][:, g, :], in0=xo[b][:, g, :], scalar=sp[:, C_FG + g:C_FG + g + 1],
                                                   in1=rstd[:], op0=ALU.mult, op1=ALU.mult), r=[r_xo[b], r_rstd, Rsp], w=[r_xo[b]])

        def store(i):
            dma(ov[:, :, i * 512:(i + 1) * 512], xo[i % NXB][:], r=[r_xo[i % NXB]])

        r_mh = [Res(), Res()]

        def loads(i):
            bb = i % 2
            t1 = i * 512
            dma(mt_[bb][:], mv3[:, :, t1:t1 + 512], w=[r_mt[bb]])
            dma(xt[i % NXB][:], xv[:, :, t1:t1 + 512], w=[r_xt[i % NXB]], Q=ACT)

        loads(0)
        for i in range(NT):
            b = i % 2
            t0 = i * 512
            if i + 1 < NT:
                loads(i + 1)
            for g in range(8):
                pi = g % 4
                mmg([(po[pi][:], wo[:, kt, g * 128:(g + 1) * 128], mt_[b][:, kt, :]) for kt in range(8)], r=[r_wo, r_mt[b], r_mh[b]], w=[r_po[pi]])

                op(DVE, lambda g=g, pi=pi, i=i: V.tensor_tensor(out=xt[i % NXB][:, g, :], in0=po[pi][:], in1=xt[i % NXB][:, g, :], op=ALU.add),
                   r=[r_po[pi], r_xt[i % NXB]], w=[r_xt[i % NXB]])
                if last and i > 0:
                    if g < 4:
                        fin_a(i - 1, g)
                    if g == 4:
                        fin_b(i - 1)
            if last:
                if i > 0:
                    for g in range(8):
                        fin_c(i - 1, g)
                    store(i - 1)
            else:
                store(i)
        if last:
            i = NT - 1
            for q in range(4):
                fin_a(i, q)
            fin_b(i)
            for g in range(8):
                fin_c(i, g)
            store(i)
        k.barrier()
        st.close()
        if st_kv is not None:
            st_kv.close()

    load_layer_weights(0)
    order = [(l, s) for l in range(nl) for s in range(nseq)]
    for idx, (l, s) in enumerate(order):
        if s == 0:
            dma(sp[:], spar[l], w=[Rsp])
            k.barrier()
            st_kv0 = kv_prep(l, s, True)
            gen_filter(l)
            st_kv0.close()
        xsrc = xT[s] if l == 0 else yT[s]
        phase_a(l, s, xsrc)
        if s == nseq - 1 and l + 1 < nl:
            load_layer_weights(l + 1)
        wst_ = ExitStack()
        wo = k.sb(wst_, "wo", [128, 8, D], BF16)
        r_wo = Res()
        dma(wo[:], w_out[l].rearrange("(kt p) c -> p kt c", p=128), w=[r_wo], Q=POOL)
        phase_b2()
        phase_b3()
        phase_b4()
        nxt = order[idx + 1] if idx + 1 < len(order) else None
        phase_c(l, s, xsrc, l == nl - 1, wo, r_wo, nxt if (nxt is not None and nxt[0] == l) else None)
        if nxt is not None and nxt[0] != l:
            pass
        wst_.close()
        if nxt is not None and nxt[0] != l:
            pass
    k.barrier()
    top.close()
    k.es.close()
    return nc


def _consts():
    a = np.arange(128, dtype=np.float64)
    th = 2.0 * np.pi * np.outer(a, a) / 128.0
    C = np.cos(th)
    S = np.sin(th)
    wk = np.full((128, 1), 2.0); wk[0] = 1.0; wk[64] = 1.0; wk[65:] = 0.0
    cf = np.concatenate([C, -S, S, C[:, 0:65], -S[:, 0:65], wk * C[:, 0:64] / 16384.0, -wk * S[:, 0:64] / 16384.0,
                         np.zeros((128, 2)), C, S, -S, C], axis=1)
    cfft = cf.astype(np.float32).astype(ml_dtypes.bfloat16)
    tt = 2.0 * np.pi * np.outer(a, a) / 16384.0
    ctw = np.stack([np.cos(tt), -np.sin(tt)], axis=1).astype(np.float32)
    t = np.linspace(0.0, 1.0, L, dtype=np.float32)[:, None]
    n = np.arange(L, dtype=np.float32)[:, None]
    bands = np.linspace(1e-4, 15, 16, dtype=np.float32)[None, :]
    ang = (np.float32(2.0 * math.pi / L) * bands * n).astype(np.float32)
    z = np.concatenate([t, np.cos(ang), -np.sin(ang)], axis=-1).astype(np.float32)
    idx_b = (L - np.arange(L)) % L
    cz = np.stack([z.T, z[idx_b].T], axis=0).astype(np.float32)
    ctp = np.stack([t[:, 0], t[idx_b, 0]], axis=0).astype(np.float32)
    max_decay = math.log(1e-2) / 0.3
    min_decay = math.log(1e-2) / 1.5
    deltas = np.abs(np.linspace(min_decay, max_decay, 512, dtype=np.float32))
    cnd = np.ascontiguousarray((-deltas).reshape(4, 128).T).astype(np.float32)
    cid = np.eye(128, dtype=np.float32).astype(ml_dtypes.bfloat16)
    return dict(cfft=cfft, ctw=ctw, cz=np.ascontiguousarray(cz), ctp=np.ascontiguousarray(ctp), cnd=cnd, cid=cid)


def _spar(inp, nl):
    sp = np.zeros((nl, 128, NSP), np.float32)
    f = lambda a: np.asarray(a, np.float32)
    for l in range(nl):
        sp[l, :, 0:8] = f(inp["norm_g"])[l].reshape(8, 128).T
        sp[l, :, 8:16] = f(inp["mem_norm_g"])[l].reshape(8, 128).T
        sw = f(inp["hy_short_w"])[l]
        sp[l, :, 16:52] = sw.reshape(3, 12, 128).transpose(2, 1, 0).reshape(128, 36)
        sp[l, :, 52:64] = f(inp["hy_short_b"])[l].reshape(12, 128).T
        sp[l, :, 64:68] = f(inp["hy_skip"])[l].reshape(4, 128).T
        dw = f(inp["cf_dw_w"])[l]
        sp[l, :, 68:130] = dw.reshape(31, 2, 128).transpose(2, 1, 0).reshape(128, 62)
        sp[l, :, 130:132] = f(inp["cf_dw_b"])[l].reshape(2, 128).T
        sp[l, :, 132:134] = f(inp["cf_ln_g"])[l].reshape(2, 128).T
        sp[l, :, 134:136] = f(inp["cf_ln_b"])[l].reshape(2, 128).T
        sp[l, :, 136:144] = f(inp["final_g"]).reshape(8, 128).T
        for j, (bn, fn) in enumerate((("hy_f_b1", "hy_f_fr1"), ("hy_f_b2", "hy_f_fr2"), ("hy_f_b3", "hy_f_fr3"))):
            sp[l, 0:64, 144 + 2 * j] = f(inp[bn])[l]
            sp[l, 0:64, 144 + 2 * j + 1] = f(inp[fn])[l]
            sp[l, 64:128, 144 + 2 * j] = f(inp[bn])[l]
            sp[l, 64:128, 144 + 2 * j + 1] = f(inp[fn])[l]
    return sp


_NC_CACHE = {}


def run(inp, nl=4, nseq=2, seq_lists=None, debug=False, trace=False):
    key = (nl, nseq, debug)
    if key not in _NC_CACHE:
        _NC_CACHE[key] = build_program(nl, nseq, debug)
    nc = _NC_CACHE[key]
    f = lambda a: np.ascontiguousarray(np.asarray(a, np.float32))
    shared = _consts()
    shared.update(
        w_in=f(inp["w_in"])[:nl], w_kv=f(inp["xa_w_kv"])[:nl], w_out=f(inp["w_out"])[:nl],
        fw1=f(inp["hy_f_w1"])[:nl], fw2=f(inp["hy_f_w2"])[:nl], fw3=f(inp["hy_f_w3"])[:nl], fw4=f(inp["hy_f_w4"])[:nl],
        spar=_spar(inp, nl),
    )
    srcs = {"s": (inp["x_sample"], inp["mem_sample"]), "p": (inp["x_prompt"], inp["mem_prompt"])}
    tcache = {}

    def getT(kind, idx):
        if (kind, idx) not in tcache:
            x, m = srcs[kind]
            tcache[(kind, idx)] = (np.ascontiguousarray(np.asarray(x[idx], np.float32).T),
                                   np.ascontiguousarray(np.asarray(m[idx], np.float32).T))
        return tcache[(kind, idx)]

    in_maps = []
    for c, sl in enumerate(seq_lists):
        xs = np.stack([getT(*q)[0] for q in sl], axis=0)
        ms = np.stack([getT(*q)[1] for q in sl], axis=0)
        d = dict(shared)
        d["xT"] = xs
        d["memT"] = ms
        in_maps.append(d)
    res = run_bass_kernel_spmd(nc, in_maps, core_ids=list(range(len(seq_lists))), **({"trace": True} if trace else {}))
    return res


def kernel(**inp):
    seq_lists = [[("s", c), ("p", c % 2)] for c in range(N_CORES)]
    res = run(inp, 4, 2, seq_lists)
    y_s = np.empty((8, L, D), np.float32)
    y_p = np.empty((2, L, D), np.float32)
    for c in range(N_CORES):
        yt = res.results[c]["yT"]
        y_s[c] = yt[0].T
        if c < 2:
            y_p[c] = yt[1].T
    return (y_p, y_s)
```

```python
import math
from contextlib import ExitStack
import numpy as np
import ml_dtypes
import concourse.bass as bass
import concourse.mybir as mybir
from concourse.bass_utils import run_bass_kernel_spmd

F32 = mybir.dt.float32
BF16 = mybir.dt.bfloat16
AF = mybir.ActivationFunctionType
ALU = mybir.AluOpType

L = 8192
D = 1024
DIN = 3328
NMEM = 256
NT = L // 512
EPS = 1e-6
NSP = 160
N_CORES = 8


class Res:
    __slots__ = ("w", "r")

    def __init__(self):
        self.w = None
        self.r = {}


class Eng:
    def __init__(self, eng, sid):
        self.eng = eng
        self.sid = sid
        self.cnt = 0
        self.waited = {}


class K:
    def __init__(self, nc, nl, nseq, debug):
        self.nc = nc
        self.nl = nl
        self.nseq = nseq
        self.debug = debug
        self.sems = []
        self.es = ExitStack()

        def newsem(name):
            s = self.es.enter_context(nc.semaphore(name))
            self.sems.append(s)
            return len(self.sems) - 1

        self.PE = Eng(nc.tensor, newsem("s_pe"))
        self.ACT = Eng(nc.scalar, newsem("s_act"))
        self.DVE = Eng(nc.vector, newsem("s_dve"))
        self.POOL = Eng(nc.gpsimd, newsem("s_pool"))
        self.SP = Eng(nc.sync, None)
        self.engs = [self.PE, self.ACT, self.DVE, self.POOL, self.SP]
        self.ND = 40
        self.NDSW = 8
        self.dma_sid = [newsem("s_dma%d" % i) for i in range(self.ND)]
        self.dma_uses = [0] * self.ND
        self.dma_next = 0
        self.dma_next_sw = 0

    def _need(self, r, w, extra=None):
        need = dict(extra or {})

        def add(sid, val):
            if need.get(sid, 0) < val:
                need[sid] = val

        for x in r:
            if x.w is not None:
                add(*x.w)
        for x in w:
            if x.w is not None:
                add(*x.w)
            for sid, val in x.r.items():
                add(sid, val)
        return need

    def _waits(self, E, need):
        for sid, val in need.items():
            if E.waited.get(sid, 0) < val:
                E.eng.wait_ge(self.sems[sid], val)
                E.waited[sid] = val

    def op(self, E, fn, r=(), w=()):
        self._waits(E, self._need(r, w))
        ins = fn()
        E.cnt += 1
        ins.then_inc(self.sems[E.sid], 1)
        for x in r:
            x.r[E.sid] = E.cnt
        for x in w:
            x.w = (E.sid, E.cnt)
            x.r = {}

    def mmg(self, mms, r=(), w=()):
        E = self.PE
        self._waits(E, self._need(r, w))
        n = len(mms)
        for i, (o, a, b) in enumerate(mms):
            ins = self.nc.tensor.matmul(o, a, b, start=(i == 0), stop=(i == n - 1))
        E.cnt += 1
        ins.then_inc(self.sems[E.sid], 1)
        for x in r:
            x.r[E.sid] = E.cnt
        for x in w:
            x.w = (E.sid, E.cnt)
            x.r = {}

    def dma(self, out, in_, r=(), w=(), Q=None):
        Q = Q or self.SP
        if Q is self.POOL:
            k = self.ND - self.NDSW + self.dma_next_sw
            self.dma_next_sw = (self.dma_next_sw + 1) % self.NDSW
        else:
            k = self.dma_next
            self.dma_next = (k + 1) % (self.ND - self.NDSW)
        sid = self.dma_sid[k]
        j = self.dma_uses[k]
        self._waits(Q, self._need(r, w, {sid: 16 * j} if j else None))
        Q.eng.dma_start(out=out, in_=in_).then_inc(self.sems[sid], 16)
        self.dma_uses[k] = j + 1
        val = 16 * (j + 1)
        for x in r:
            x.r[sid] = val
        for x in w:
            x.w = (sid, val)
            x.r = {}

    def barrier(self):
        tg = {}
        for E in self.engs:
            if E.sid is not None and E.cnt:
                tg[E.sid] = E.cnt
        for k in range(self.ND):
            if self.dma_uses[k]:
                tg[self.dma_sid[k]] = 16 * self.dma_uses[k]
        for E in self.engs:
            self._waits(E, tg)

    def sb(self, st, name, shape, dt):
        self.uid = getattr(self, "uid", 0) + 1
        return st.enter_context(self.nc.sbuf_tensor("%s_%d" % (name, self.uid), list(shape), dt))

    def ps(self, st, name, shape):
        self.uid = getattr(self, "uid", 0) + 1
        return st.enter_context(self.nc.psum_tensor("%s_%d" % (name, self.uid), list(shape), F32))


def build_program(nl=4, nseq=2, debug=False):
    nc = bass.Bass("TRN2", target_bir_lowering=False)
    k = K(nc, nl, nseq, debug)
    PE, ACT, DVE, POOL, SP = k.PE, k.ACT, k.DVE, k.POOL, k.SP
    V, S, G, T = nc.vector, nc.scalar, nc.gpsimd, nc.tensor
    op, mmg, dma = k.op, k.mmg, k.dma

    def din(name, shape, dt=F32):
        return nc.dram_tensor(name, list(shape), dt, kind="ExternalInput").ap()

    skind = "ExternalOutput" if debug else "Internal"

    def dscr(name, shape, dt):
        return nc.dram_tensor(name, list(shape), dt, kind=skind).ap()

    xT = din("xT", [nseq, D, L])
    memT = din("memT", [nseq, D, NMEM])
    w_in = din("w_in", [nl, D, DIN])
    w_kv = din("w_kv", [nl, D, 512])
    w_out = din("w_out", [nl, D, D])
    fw1 = din("fw1", [nl, 33, 64])
    fw2 = din("fw2", [nl, 64, 64])
    fw3 = din("fw3", [nl, 64, 64])
    fw4 = din("fw4", [nl, 64, 1024])
    spar = din("spar", [nl, 128, NSP])
    cfft = din("cfft", [128, 1156], BF16)
    ctw = din("ctw", [128, 2, 128])
    cz = din("cz", [2, 33, L])
    ctp = din("ctp", [2, L])
    cnd = din("cnd", [128, 4])
    cid = din("cid", [128, 128], BF16)
    yT = nc.dram_tensor("yT", [nseq, D, L], F32, kind="ExternalOutput").ap()

    pscr = dscr("pscr", [22 * 128, L], BF16)
    mixs = dscr("mixs", [D, L], BF16)
    vx1s = dscr("vx1s", [512, L], BF16)
    g0s = dscr("g0s", [512, L], BF16)
    yscr = dscr("yscr", [512, L], F32)
    kfil = dscr("kfil", [512, 2 * L], BF16)
    kfs = dscr("kfs", [128, 2, 512, 65], BF16)

    top = ExitStack()
    fftc = k.sb(top, "fftc", [128, 1156], BF16)
    twt = k.sb(top, "twt", [128, 2, 128], F32)
    twb = k.sb(top, "twb", [128, 2, 128], BF16)
    ident = k.sb(top, "ident", [128, 128], BF16)
    ones_d = k.sb(top, "ones_d", [128, 128], BF16)
    ones_c = k.sb(top, "ones_c", [128, 128], BF16)
    ind = k.sb(top, "ind", [128, 2, 128], BF16)
    ndl = k.sb(top, "ndl", [128, 4], F32)
    sp = k.sb(top, "sp", [128, NSP], F32)
    frb = k.sb(top, "frb", [128, 3], F32)
    epsb = k.sb(top, "epsb", [128, 1], F32)
    win = k.sb(top, "win", [128, 8, DIN], BF16)
    wkv = k.sb(top, "wkv", [128, 8, 512], BF16)
    r_win, r_wkv = Res(), Res()

    def load_layer_weights(l):
        dma(wkv[:], w_kv[l].rearrange("(kt p) c -> p kt c", p=128), w=[r_wkv], Q=POOL)
        for kt in range(8):
            dma(win[:, kt, :], w_in[l][kt * 128:(kt + 1) * 128, :], w=[r_win], Q=POOL)
    Rc = Res()
    Rsp = Res()
    dma(fftc[:], cfft, w=[Rc])
    dma(twt[:], ctw, w=[Rc])
    dma(ident[:], cid, w=[Rc])
    dma(ndl[:], cnd, w=[Rc])
    op(DVE, lambda: V.tensor_copy(out=twb[:], in_=twt[:]), r=[Rc], w=[Rc])
    op(DVE, lambda: V.memset(epsb[:], EPS), w=[Rc])
    op(DVE, lambda: V.memset(ones_d[:], 1.0 / 1024.0), w=[Rc])
    op(DVE, lambda: V.memset(ones_c[:], 1.0 / 256.0), w=[Rc])
    op(DVE, lambda: V.memset(ind[:], 0.0), w=[Rc])
    op(DVE, lambda: V.memset(ind[:, 0, 0:64], 1.0), w=[Rc])
    op(DVE, lambda: V.memset(ind[:, 1, 64:128], 1.0), w=[Rc])
    k.barrier()
    Cm = fftc[:, 0:128]
    NSm = fftc[:, 128:256]
    Sm = fftc[:, 256:384]
    P1h = fftc[:, 384:514]
    C4w = fftc[:, 514:578]
    NS4w = fftc[:, 578:642]
    P2 = fftc[:, 644:900]
    P3 = fftc[:, 900:1156]

    C_NG, C_MNG, C_SHW, C_SHB, C_SKIP, C_DWW, C_DWB, C_LNG, C_LNB, C_FG, C_FB = 0, 8, 16, 52, 64, 68, 130, 132, 134, 136, 144

    KH = 65
    SBC = 8
    NSUB = 4

    def fft_pipeline(nsb, Kdim, load_in, is_filter, store_out=None):
        st = ExitStack()
        R = lambda n: [Res() for _ in range(n)]
        xin = [k.sb(st, "xin%d" % i, [128, SBC, 128], BF16) for i in range(2)]
        tA = [k.sb(st, "tA%d" % i, [128, SBC, 2, KH], BF16) for i in range(2)]
        mm = [k.sb(st, "cm%d" % i, [128, SBC, KH], BF16) for i in range(4)]
        Bb = [k.sb(st, "Bb%d" % i, [128, 2, SBC, KH], BF16) for i in range(2)]
        tX = [k.sb(st, "tX%d" % i, [128, 2, SBC, KH], BF16) for i in range(2)]
        psA = [k.ps(st, "psA%d" % i, [128, 512]) for i in range(2)]
        psX = [k.ps(st, "psX%d" % i, [128, 512]) for i in range(2)]
        r_xin, r_B, r_psA, r_psX = R(2), R(2), R(2), R(2)
        r_tA = [R(NSUB), R(NSUB)]
        r_tX = [R(NSUB), R(NSUB)]
        r_mm = R(4)
        if not is_filter:
            kft = [k.sb(st, "kft%d" % i, [128, 2, SBC, KH], BF16) for i in range(2)]
            Yb = [k.sb(st, "Yb%d" % i, [128, 2, SBC, KH], BF16) for i in range(2)]
            tC = [k.sb(st, "tC%d" % i, [KH, SBC, 2, 128], BF16) for i in range(2)]
            Dd = [k.sb(st, "Dd%d" % i, [KH, SBC, 2, 128], BF16) for i in range(2)]
            mT = [k.sb(st, "cmT%d" % i, [KH, SBC, 128], BF16) for i in range(4)]
            yb = [k.sb(st, "yb%d" % i, [64, SBC, 128], F32) for i in range(2)]
            psC = [k.ps(st, "psC%d" % i, [128, 512]) for i in range(2)]
            psY = k.ps(st, "psY", [128, 512])
            r_kft, r_Y, r_D, r_psC = R(2), R(2), R(2), R(2)
            r_tC = [R(NSUB), R(NSUB)]
            r_yb = [R(2), R(2)]
            r_psY = Res()
            r_mT = R(4)
            TreT = twb[0:KH, 0, :].unsqueeze(1).to_broadcast([KH, SBC, 128])
            TimT = twb[0:KH, 1, :].unsqueeze(1).to_broadcast([KH, SBC, 128])
        TreH = twb[:, 0, 0:KH].unsqueeze(1).to_broadcast([128, SBC, KH])
        TimH = twb[:, 1, 0:KH].unsqueeze(1).to_broadcast([128, SBC, KH])

        def cmul(a_re, a_im, b_re, b_im, o_re, o_im, conj, rs, ro, mm=mm, r_mm=r_mm):
            op(DVE, lambda: V.tensor_tensor(out=mm[0][:], in0=a_re, in1=b_re, op=ALU.mult), r=rs, w=[r_mm[0]])
            op(DVE, lambda: V.tensor_tensor(out=mm[1][:], in0=a_im, in1=b_im, op=ALU.mult), r=rs, w=[r_mm[1]])
            op(DVE, lambda: V.tensor_tensor(out=mm[2][:], in0=a_re, in1=b_im, op=ALU.mult), r=rs, w=[r_mm[2]])
            op(DVE, lambda: V.tensor_tensor(out=mm[3][:], in0=a_im, in1=b_re, op=ALU.mult), r=rs, w=[r_mm[3]])
            if not conj:
                op(DVE, lambda: V.tensor_tensor(out=o_re, in0=mm[0][:], in1=mm[1][:], op=ALU.subtract), r=[r_mm[0], r_mm[1]], w=[ro])
                op(DVE, lambda: V.tensor_tensor(out=o_im, in0=mm[2][:], in1=mm[3][:], op=ALU.add), r=[r_mm[2], r_mm[3]], w=[ro])
            else:
                op(DVE, lambda: V.tensor_tensor(out=o_re, in0=mm[0][:], in1=mm[1][:], op=ALU.add), r=[r_mm[0], r_mm[1]], w=[ro])
                op(DVE, lambda: V.tensor_tensor(out=o_im, in0=mm[3][:], in1=mm[2][:], op=ALU.subtract), r=[r_mm[2], r_mm[3]], w=[ro])

        def s_load(b):
            load_in(b, xin[b % 2], r_xin[b % 2])

        def s_S1(b):
            i = b % 2
            for sub in range(NSUB):
                s_ = sub % 2
                pa = psA[s_][:, 0:2 * 2 * KH].rearrange("p (c n) -> p c n", c=2)
                k._waits(PE, k._need([], [r_psA[s_]]))
                for cl in range(2):
                    c = sub * 2 + cl
                    mmg([(pa[:, cl, :], xin[i][0:Kdim, c, :], P1h[0:Kdim, :])], r=[r_xin[i], Rc], w=[r_psA[s_]] if cl == 1 else [])
                op(ACT, lambda sub=sub, pa=pa: S.copy(out=tA[i][:, sub * 2:sub * 2 + 2].rearrange("p c r k -> p c (r k)"), in_=pa),
                   r=[r_psA[s_]], w=[r_tA[i][sub]])

        def s_TW1(b):
            i = b % 2
            cmul(tA[i][:, :, 0, :], tA[i][:, :, 1, :], TreH, TimH, Bb[i][:, 0], Bb[i][:, 1], False, r_tA[i] + [Rc], r_B[i])

        def table_stage(src, r_src, ps, r_ps, dst, r_dst, i, fwd):
            for sub in range(NSUB):
                s_ = sub % 2
                px = ps[s_][:, 0:2 * 2 * KH].rearrange("p (r n) -> p r n", r=2)
                bre = src[i][:, 0, sub * 2:sub * 2 + 2, :].rearrange("p c k -> p (c k)")
                bim = src[i][:, 1, sub * 2:sub * 2 + 2, :].rearrange("p c k -> p (c k)")
                k._waits(PE, k._need([], [r_ps[s_]]))
                if fwd:
                    mmg([(px[:, 0, :], Cm, bre), (px[:, 0, :], Sm, bim)], r=[r_src[i], Rc], w=[])
                    mmg([(px[:, 1, :], NSm, bre), (px[:, 1, :], Cm, bim)], r=[r_src[i], Rc], w=[r_ps[s_]])
                else:
                    mmg([(px[:, 0, :], Cm, bre), (px[:, 0, :], NSm, bim)], r=[r_src[i], Rc], w=[])
                    mmg([(px[:, 1, :], Sm, bre), (px[:, 1, :], Cm, bim)], r=[r_src[i], Rc], w=[r_ps[s_]])
                op(ACT, lambda sub=sub, px=px: S.copy(out=dst[i][:, :, sub * 2:sub * 2 + 2, :],
                                                      in_=px.rearrange("p r (c k) -> p r c k", c=2)),
                   r=[r_ps[s_]], w=[r_dst[i][sub]])

        def s_S2(b):
            i = b % 2
            if not is_filter:
                dma(kft[i][:], kfs[:, :, b * SBC:(b + 1) * SBC, :], w=[r_kft[i]])
            table_stage(Bb, r_B, psX, r_psX, tX, r_tX, i, True)

        def s_PM(b):
            i = b % 2
            if is_filter:
                dma(kfs[:, :, b * SBC:(b + 1) * SBC, :], tX[i][:], r=r_tX[i])
            else:
                cmul(tX[i][:, 0], tX[i][:, 1], kft[i][:, 0], kft[i][:, 1], Yb[i][:, 0], Yb[i][:, 1], False, r_tX[i] + [r_kft[i]], r_Y[i])

        def s_S3(b):
            i = b % 2
            for sub in range(NSUB):
                s_ = sub % 2
                pc_ = psC[s_][0:KH, :].rearrange("p (c n) -> p c n", c=2)
                k._waits(PE, k._need([], [r_psC[s_]]))
                for cl in range(2):
                    c = sub * 2 + cl
                    mmg([(pc_[:, cl, :], Yb[i][:, 0, c, :], P2), (pc_[:, cl, :], Yb[i][:, 1, c, :], P3)],
                        r=[r_Y[i], Rc], w=[r_psC[s_]] if cl == 1 else [])
                op(ACT, lambda sub=sub, s_=s_: S.copy(out=tC[i][:, sub * 2:sub * 2 + 2].rearrange("p c r n -> p (c r n)"), in_=psC[s_][0:KH, :]),
                   r=[r_psC[s_]], w=[r_tC[i][sub]])

        def s_TW2(b):
            i = b % 2
            cmul(tC[i][:, :, 0, :], tC[i][:, :, 1, :], TreT, TimT, Dd[i][:, :, 0, :], Dd[i][:, :, 1, :], True, r_tC[i] + [Rc], r_D[i],
                 mm=mT, r_mm=r_mT)

        def s_S4(b):
            i = b % 2
            for h in range(2):
                mmg([(psY[0:64, :], C4w[0:KH, :], Dd[i][:, h * 4:(h + 1) * 4, 0, :]),
                     (psY[0:64, :], NS4w[0:KH, :], Dd[i][:, h * 4:(h + 1) * 4, 1, :])],
                    r=[r_D[i], Rc], w=[r_psY])
                op(ACT, lambda h=h: S.copy(out=yb[i][:, h * 4:(h + 1) * 4, :].rearrange("p c k -> p (c k)"), in_=psY[0:64, :]),
                   r=[r_psY], w=[r_yb[i][h]])
            store_out(b, yb[i], r_yb[i])

        stages = [s_load, s_S1, s_TW1, s_S2, s_PM]
        if not is_filter:
            stages += [s_S3, s_TW2, s_S4]
        ns = len(stages)
        for step in range(nsb + ns - 1):
            for si in range(ns - 1, -1, -1):
                b = step - si
                if 0 <= b < nsb:
                    stages[si](b)
        k.barrier()
        st.close()

    def gen_filter(l):
        st = ExitStack()
        w1 = k.sb(st, "fw1", [66, 128], F32)
        w2 = k.sb(st, "fw2", [128, 128], F32)
        w3 = k.sb(st, "fw3", [128, 128], F32)
        w4 = k.sb(st, "fw4", [128, 1024], F32)
        zt = [k.sb(st, "zt%d" % i, [66, 512], F32) for i in range(2)]
        tb = [k.sb(st, "tb%d" % i, [128, 2, 512], F32) for i in range(2)]
        ha = [k.sb(st, "ha%d" % i, [128, 512], F32) for i in range(3)]
        hb = [k.sb(st, "hb%d" % i, [128, 512], F32) for i in range(3)]
        hc = [k.sb(st, "hc%d" % i, [128, 512], F32) for i in range(3)]
        hh = [[k.sb(st, "hh%d%d" % (j, i), [128, 512], F32) for i in range(2)] for j in range(3)]
        win = [k.sb(st, "fwin%d" % i, [128, 512], F32) for i in range(2)]
        kst = [k.sb(st, "kst%d" % i, [128, 2, 4, 512], BF16) for i in range(2)]
        psh = [k.ps(st, "psh%d" % i, [128, 512]) for i in range(3)]
        psk = [k.ps(st, "psk%d" % i, [128, 512]) for i in range(2)]
        R3 = lambda: [Res(), Res(), Res()]
        Rw = Res()
        r_ha, r_hb, r_hc, r_psh = R3(), R3(), R3(), R3()
        r_zt, r_tb, r_kst, r_psk, r_win = [Res(), Res()], [Res(), Res()], [Res(), Res()], [Res(), Res()], [Res(), Res()]
        r_hh = [[Res(), Res()] for _ in range(3)]
        op(DVE, lambda: V.memset(w1[:], 0.0), w=[Rw])
        op(DVE, lambda: V.memset(w2[:], 0.0), w=[Rw])
        op(DVE, lambda: V.memset(w3[:], 0.0), w=[Rw])
        for hf in range(2):
            dma(w1[hf * 33:(hf + 1) * 33, hf * 64:(hf + 1) * 64], fw1[l], w=[Rw])
            dma(w2[hf * 64:(hf + 1) * 64, hf * 64:(hf + 1) * 64], fw2[l], w=[Rw])
            dma(w3[hf * 64:(hf + 1) * 64, hf * 64:(hf + 1) * 64], fw3[l], w=[Rw])
            dma(w4[hf * 64:(hf + 1) * 64, :], fw4[l], w=[Rw])
        for j in range(3):
            op(DVE, lambda j=j: V.tensor_tensor(out=frb[:, j:j + 1], in0=sp[:, C_FB + 2 * j:C_FB + 2 * j + 1],
                                                in1=sp[:, C_FB + 2 * j + 1:C_FB + 2 * j + 2], op=ALU.mult), r=[Rsp], w=[Rw])
        wl = [w1, w2, w3]

        def f_load(ch):
            i = ch % 2
            Q = []
            for hf in range(2):
                Q.append(lambda hf=hf: dma(zt[i][hf * 33:(hf + 1) * 33, :], cz[hf, :, ch * 512:(ch + 1) * 512], w=[r_zt[i]]))
            return Q

        def f_layer(j):
            def f(ch):
                i = ch % 2
                src = zt[i][:] if j == 0 else hh[j - 1][i][:]
                rsrc = r_zt[i] if j == 0 else r_hh[j - 1][i]
                Q = []
                Q.append(lambda: mmg([(psh[j][:], wl[j][:], src)], r=[Rw, rsrc], w=[r_psh[j]]))
                Q.append(lambda: op(ACT, lambda: S.activation(out=ha[j][:], in_=psh[j][:], func=AF.Identity, bias=frb[:, j:j + 1],
                                                              scale=sp[:, C_FB + 2 * j + 1:C_FB + 2 * j + 2]), r=[r_psh[j], Rw, Rsp], w=[r_ha[j]]))
                Q.append(lambda: op(DVE, lambda: V.tensor_scalar(out=hb[j][:], in0=ha[j][:], scalar1=math.pi, scalar2=-2.0 * math.pi,
                                                                 op0=ALU.is_gt, op1=ALU.mult), r=[r_ha[j]], w=[r_hb[j]]))
                Q.append(lambda: op(DVE, lambda: V.tensor_scalar(out=hc[j][:], in0=ha[j][:], scalar1=-math.pi, scalar2=2.0 * math.pi,
                                                                 op0=ALU.is_lt, op1=ALU.mult), r=[r_ha[j]], w=[r_hc[j]]))
                Q.append(lambda: op(DVE, lambda: V.tensor_tensor(out=hb[j][:], in0=hb[j][:], in1=hc[j][:], op=ALU.add), r=[r_hb[j], r_hc[j]], w=[r_hb[j]]))
                Q.append(lambda: op(DVE, lambda: V.tensor_tensor(out=hb[j][:], in0=hb[j][:], in1=ha[j][:], op=ALU.add), r=[r_hb[j], r_ha[j]], w=[r_hb[j]]))
                Q.append(lambda: op(ACT, lambda: S.activation(out=hh[j][i][:], in_=hb[j][:], func=AF.Sin), r=[r_hb[j]], w=[r_hh[j][i]]))
                if j == 2:
                    for hf in range(2):
                        Q.append(lambda hf=hf: dma(tb[i][:, hf, :], ctp[hf, ch * 512:(ch + 1) * 512].partition_broadcast(128), w=[r_tb[i]]))
                return Q
            return f

        def f_out(ch):
            i = ch % 2
            Q = []
            n = 0
            for hf in range(2):
                for j4 in range(4):
                    pi = n % 2
                    n += 1
                    Q.append(lambda hf=hf, j4=j4, pi=pi: mmg(
                        [(psk[pi][:], w4[hf * 64:(hf + 1) * 64, hf * 512 + j4 * 128: hf * 512 + (j4 + 1) * 128], hh[2][i][hf * 64:(hf + 1) * 64, :])],
                        r=[Rw, r_hh[2][i]], w=[r_psk[pi]]))
                    Q.append(lambda hf=hf, j4=j4, pi=pi: op(ACT, lambda: S.activation(out=win[pi][:], in_=tb[i][:, hf, :], func=AF.Exp, scale=ndl[:, j4:j4 + 1]),
                                                            r=[r_tb[i], Rc], w=[r_win[pi]]))
                    Q.append(lambda hf=hf, j4=j4, pi=pi: op(DVE, lambda: V.tensor_tensor(out=kst[i][:, hf, j4, :], in0=psk[pi][:], in1=win[pi][:], op=ALU.mult),
                                                            r=[r_psk[pi], r_win[pi]], w=[r_kst[i]]))
            if ch == 0:
                Q.append(lambda: op(DVE, lambda: V.memset(kst[i][:, 1, :, 0:1], 0.0), w=[r_kst[i]]))
            for hf in range(2):
                c0 = hf * L + ch * 512
                Q.append(lambda hf=hf, c0=c0: dma(kfil.rearrange("(j p) t -> p j t", p=128)[:, :, c0:c0 + 512], kst[i][:, hf], r=[r_kst[i]]))
            return Q

        fst = [f_load, f_layer(0), f_layer(1), f_layer(2), f_out]
        for step in range(NT + len(fst) - 1):
            qs = []
            for si in range(len(fst) - 1, -1, -1):
                ch = step - si
                if 0 <= ch < NT:
                    qs.append(fst[si](ch))
            while any(qs):
                for q in qs:
                    take = 3 if len(q) > 12 else 1
                    for _ in range(take):
                        if q:
                            q.pop(0)()
        k.barrier()
        st.close()
        kv = kfil.rearrange("c (a b) -> a c b", b=128)

        def load_f(b, dst, rdst):
            dma(dst[:], kv[:, b * SBC:(b + 1) * SBC, :], w=[rdst])

        fft_pipeline(512 // SBC, 128, load_f, True)

    def load_cast(dst, src_l, ncols, gcol, wst, r_wst, r_dst):
        srcv = src_l.rearrange("(kt p) c -> p kt c", p=128)
        half = 1664
        it = 0
        for kt in range(8):
            for c0 in range(0, ncols, half):
                cw = min(half, ncols - c0)
                i = it % 2
                it += 1
                dma(wst[i][:, 0:cw], srcv[:, kt, c0:c0 + cw], w=[r_wst[i]])
                if gcol is None:
                    if it % 2:
                        op(ACT, lambda i=i, kt=kt, c0=c0, cw=cw: S.copy(out=dst[:, kt, c0:c0 + cw], in_=wst[i][:, 0:cw]),
                           r=[r_wst[i]], w=[r_dst])
                    else:
                        op(DVE, lambda i=i, kt=kt, c0=c0, cw=cw: V.tensor_copy(out=dst[:, kt, c0:c0 + cw], in_=wst[i][:, 0:cw]),
                           r=[r_wst[i]], w=[r_dst])
                else:
                    if it % 2:
                        op(ACT, lambda i=i, kt=kt, c0=c0, cw=cw: S.activation(out=dst[:, kt, c0:c0 + cw], in_=wst[i][:, 0:cw],
                                                                              func=AF.Identity, scale=sp[:, gcol + kt:gcol + kt + 1]),
                           r=[r_wst[i], Rsp], w=[r_dst])
                    else:
                        op(DVE, lambda i=i, kt=kt, c0=c0, cw=cw: V.tensor_scalar(out=dst[:, kt, c0:c0 + cw], in0=wst[i][:, 0:cw],
                                                                                 scalar1=sp[:, gcol + kt:gcol + kt + 1], scalar2=None,
                                                                                 op0=ALU.mult),
                           r=[r_wst[i], Rsp], w=[r_dst])

    kT = k.sb(top, "kT", [128, 2, 256], BF16)
    vpad = k.sb(top, "vpad", [128, 4, 2, 128], BF16)
    r_kT, r_vpad = Res(), Res()

    def kv_prep(l, s, with_barrier):
        st2 = ExitStack()
        memx = k.sb(st2, "memx", [128, 8, 256], F32)
        msq = k.sb(st2, "msq", [128, 8, 256], BF16)
        memn = k.sb(st2, "memn", [128, 8, 256], BF16)
        mrs = k.sb(st2, "mrs", [128, 256], F32)
        pkv = k.ps(st2, "pkv", [128, 256])
        r_memx, r_msq, r_memn, r_mrs, r_pkv = Res(), Res(), Res(), Res(), Res()
        dma(memx[:], memT[s].rearrange("(kt p) m -> p kt m", p=128), w=[r_memx])
        op(ACT, lambda: S.activation(out=msq[:], in_=memx[:], func=AF.Square), r=[r_memx], w=[r_msq])
        mmg([(pkv[:], ones_d[:], msq[:, kt, :]) for kt in range(8)], r=[r_msq, Rc], w=[r_pkv])
        op(ACT, lambda: S.activation(out=mrs[:], in_=pkv[:], func=AF.Ln, bias=epsb[:, 0:1], scale=1.0), r=[r_pkv, Rc], w=[r_mrs])
        op(ACT, lambda: S.activation(out=mrs[:], in_=mrs[:], func=AF.Exp, scale=-0.5), r=[r_mrs], w=[r_mrs])
        for kt in range(8):
            op(DVE, lambda kt=kt: V.scalar_tensor_tensor(out=memn[:, kt, :], in0=memx[:, kt, :], scalar=sp[:, C_MNG + kt:C_MNG + kt + 1],
                                                         in1=mrs[:], op0=ALU.mult, op1=ALU.mult), r=[r_memx, r_mrs, Rsp], w=[r_memn])
        for g in range(2):
            mmg([(pkv[:], wkv[:, kt, g * 128:(g + 1) * 128], memn[:, kt, :]) for kt in range(8)], r=[r_wkv, r_memn], w=[r_pkv])
            op(DVE, lambda g=g: V.tensor_copy(out=kT[:, g, :], in_=pkv[:]), r=[r_pkv], w=[r_kT])
        op(POOL, lambda: G.memset(vpad[:], 0.0), w=[r_vpad])
        for mt in range(2):
            mmg([(pkv[:], memn[:, kt, mt * 128:(mt + 1) * 128], wkv[:, kt, 256:512]) for kt in range(8)], r=[r_wkv, r_memn], w=[r_pkv])
            for h in range(4):
                hb = (h % 2) * 64
                op(DVE, lambda h=h, hb=hb, mt=mt: V.tensor_copy(out=vpad[:, h, mt, hb:hb + 64], in_=pkv[:, h * 64:(h + 1) * 64]),
                   r=[r_pkv], w=[r_vpad])
        return st2

    def phase_a(l, s, xsrc):
        st = ExitStack()
        xt = [k.sb(st, "xt%d" % i, [128, 8, 512], F32) for i in range(2)]
        sq = k.sb(st, "sq", [128, 8, 512], BF16)
        hh = [k.sb(st, "hh%d" % i, [128, 8, 512], BF16) for i in range(2)]
        stage = k.sb(st, "stage", [128, 4, 512], BF16)
        sg = k.sb(st, "sg", [128, 2, 512], F32)
        r_sg = [Res(), Res()]
        GORD = list(range(16)) + [18, 19, 16, 17] + list(range(20, 26))
        hst = k.sb(st, "hst", [128, 12, 515], BF16)
        zst = k.sb(st, "zst", [128, 4, 513], BF16)
        tm1 = k.sb(st, "tm1", [128, 4, 513], F32)
        tm2 = k.sb(st, "tm2", [128, 4, 513], F32)
        vxo = k.sb(st, "vxo", [128, 4, 513], BF16)
        g0o = k.sb(st, "g0o", [128, 4, 513], BF16)
        qT = [k.sb(st, "qT%d" % i, [128, 2, 512], BF16) for i in range(2)]
        zs = [k.sb(st, "zs%d" % i, [128, 2, 512], BF16) for i in range(2)]
        pT = k.sb(st, "pT", [128, 4, 512], BF16)
        rstd = k.sb(st, "rstd", [128, 512], F32)
        rden = k.sb(st, "rden", [128, 512], F32)
        ot = k.sb(st, "ot", [128, 512], F32)
        yxa = [k.sb(st, "yxa%d" % i, [128, 2, 512], BF16) for i in range(2)]
        pj = [k.ps(st, "pj%d" % i, [128, 512]) for i in range(4)]
        sc = [k.ps(st, "sc_%d" % i, [128, 512]) for i in range(2)]
        pv = k.ps(st, "pv", [128, 512])
        dn = k.ps(st, "dn", [128, 512])
        nrm = dn
        r_xt, r_pj, r_sc = [Res(), Res()], [Res() for _ in range(4)], [Res(), Res()]
        r_rstd, r_rden, r_ot, r_pv, r_dn, r_tm1, r_tm2, r_vxo, r_g0o, r_zst = (Res() for _ in range(10))
        r_nrm = r_dn
        r_sq = [Res() for _ in range(8)]
        r_hh, r_qT, r_zs, r_yxa = [Res(), Res()], [Res(), Res()], [Res(), Res()], [Res(), Res()]
        r_stage = [Res() for _ in range(4)]
        r_hst = [Res() for _ in range(12)]
        r_pT = [Res() for _ in range(4)]
        xv = xsrc.rearrange("(kt p) t -> p kt t", p=128)
        pv3 = pscr.rearrange("(g p) t -> p g t", p=128)
        mv3 = mixs.rearrange("(g p) t -> p g t", p=128)
        vxv = vx1s.rearrange("(j p) t -> p j t", p=128)
        g0v = g0s.rearrange("(j p) t -> p j t", p=128)
        op(POOL, lambda: G.memset(hst[:], 0.0), w=r_hst)
        op(POOL, lambda: G.memset(zst[:], 0.0), w=[r_zst])

        def norm_a(i, q):
            b = i % 2
            op(ACT, lambda: S.activation(out=sq[:, q, :], in_=xt[b][:, q, :], func=AF.Square), r=[r_xt[b]], w=[r_sq[q]])

        def norm(i):
            mmg([(nrm[:], ones_d[:], sq[:, kt, :]) for kt in range(8)], r=r_sq + [Rc], w=[r_nrm])
            op(ACT, lambda: S.activation(out=rstd[:], in_=nrm[:], func=AF.Ln, bias=epsb[:, 0:1], scale=1.0), r=[r_nrm, Rc], w=[r_rstd])
            op(ACT, lambda: S.activation(out=rstd[:], in_=rstd[:], func=AF.Exp, scale=-0.5), r=[r_rstd], w=[r_rstd])

        def norm_h(i, q):
            b = i % 2
            for kt in (2 * q, 2 * q + 1):
                op(DVE, lambda kt=kt: V.scalar_tensor_tensor(out=hh[b][:, kt, :], in0=xt[b][:, kt, :], scalar=sp[:, C_NG + kt:C_NG + kt + 1],
                                                             in1=rstd[:], op0=ALU.mult, op1=ALU.mult), r=[r_xt[b], r_rstd, Rsp], w=[r_hh[b]])

        def attn_micro(i):
            b = i % 2
            t0 = i * 512
            steps = []
            for pr in range(2):
                for hl in range(2):
                    for mt in range(2):
                        def f(pr=pr, hl=hl, mt=mt):
                            hb = hl * 64
                            pi = hl * 2 + mt
                            mmg([(sc[mt][:], kT[hb:hb + 64, pr, mt * 128:(mt + 1) * 128], qT[b][hb:hb + 64, pr, :])],
                                r=[r_kT, r_qT[b]], w=[r_sc[mt]])
                            op(ACT, lambda: S.activation(out=pT[:, pi, :], in_=sc[mt][:], func=AF.Exp, scale=0.125),
                               r=[r_sc[mt]], w=[r_pT[pi]])
                        steps.append(f)

                def fin(pr=pr):
                    mmg([(pv[:], vpad[:, 2 * pr + (pi // 2), pi % 2, :], pT[:, pi, :]) for pi in range(4)], r=[r_vpad] + r_pT, w=[r_pv])
                    mmg([(dn[:], ind[:, pi // 2, :], pT[:, pi, :]) for pi in range(4)], r=[Rc] + r_pT, w=[r_dn])
                    op(ACT, lambda: S.activation(out=rden[:], in_=dn[:], func=AF.Ln), r=[r_dn], w=[r_rden])
                    op(ACT, lambda: S.activation(out=rden[:], in_=rden[:], func=AF.Exp, scale=-1.0), r=[r_rden], w=[r_rden])
                    op(DVE, lambda: V.tensor_tensor(out=ot[:], in0=pv[:], in1=rden[:], op=ALU.mult), r=[r_pv, r_rden], w=[r_ot])
                    op(DVE, lambda: V.tensor_tensor(out=yxa[b][:, pr, :], in0=ot[:], in1=zs[b][:, pr, :], op=ALU.mult),
                       r=[r_ot, r_zs[b]], w=[r_yxa[b]])
                    if pr == 1:
                        dma(mv3[:, 6:8, t0:t0 + 512], yxa[b][:], r=[r_yxa[b]])
                steps.append(fin)
            return steps

        def hy_micro(i):
            t0 = i * 512
            W = 513 if i == NT - 1 else 512
            steps = []

            def conv_step(g, dst, rdst, jj):
                def f():
                    w0 = sp[:, C_SHW + g * 3:C_SHW + g * 3 + 1]
                    w1 = sp[:, C_SHW + g * 3 + 1:C_SHW + g * 3 + 2]
                    w2 = sp[:, C_SHW + g * 3 + 2:C_SHW + g * 3 + 3]
                    bb = sp[:, C_SHB + g:C_SHB + g + 1]
                    o = dst[:, jj, 0:W]
                    op(ACT, lambda: S.activation(out=o, in_=hst[:, g, 1:1 + W], func=AF.Identity, bias=bb, scale=w1),
                       r=[r_hst[g], Rsp], w=[rdst[jj]])
                    op(DVE, lambda: V.scalar_tensor_tensor(out=o, in0=hst[:, g, 0:W], scalar=w0, in1=o, op0=ALU.mult, op1=ALU.add),
                       r=[r_hst[g], Rsp, rdst[jj]], w=[rdst[jj]])
                    op(DVE, lambda: V.scalar_tensor_tensor(out=o, in0=hst[:, g, 2:2 + W], scalar=w2, in1=o, op0=ALU.mult, op1=ALU.add),
                       r=[r_hst[g], Rsp, rdst[jj]], w=[rdst[jj]])
                    op(DVE, lambda: V.tensor_copy(out=hst[:, g, 0:2], in_=hst[:, g, 512:514]), r=[r_hst[g]], w=[r_hst[g]])
                return f

            def store(dstv, src, rsrc):
                if i == 0:
                    dma(dstv[:, :, 0:W - 1], src[:, :, 1:W], r=rsrc)
                else:
                    dma(dstv[:, :, t0 - 1:t0 - 1 + W], src[:, :, 0:W], r=rsrc)

            def silu_z():
                op(ACT, lambda: S.activation(out=tm2[:, :, 0:W], in_=zst[:, :, 0:W], func=AF.Silu), r=[r_zst], w=r_t2)
                op(DVE, lambda: V.tensor_copy(out=zst[:, :, 0:1], in_=zst[:, :, 512:513]), r=[r_zst], w=[r_zst])
            steps.append(silu_z)
            for jj in range(4):
                steps.append(conv_step(jj, tm1, r_t1, jj))

            def g0_prod():
                op(DVE, lambda: V.tensor_tensor(out=g0o[:, :, 0:W], in0=tm1[:, :, 0:W], in1=tm2[:, :, 0:W], op=ALU.mult),
                   r=r_t1 + r_t2, w=[r_g0o])
                store(g0v, g0o, [r_g0o])
            steps.append(g0_prod)
            for jj in range(4):
                steps.append(conv_step(4 + jj, tm1, r_t1, jj))
            for jj in range(4):
                steps.append(conv_step(8 + jj, tm2, r_t2, jj))

            def vx_prod():
                op(DVE, lambda: V.tensor_tensor(out=vxo[:, :, 0:W], in0=tm1[:, :, 0:W], in1=tm2[:, :, 0:W], op=ALU.mult),
                   r=r_t1 + r_t2, w=[r_vxo])
                store(vxv, vxo, [r_vxo])
            steps.append(vx_prod)
            return steps

        r_t1 = [Res() for _ in range(4)]
        r_t2 = [Res() for _ in range(4)]
        dma(xt[0][:], xv[:, :, 0:512], w=[r_xt[0]])
        for q in range(8):
            norm_a(0, q)
        norm(0)
        for q in range(4):
            norm_h(0, q)
        pending = []
        ev = 0
        gi = 0
        for i in range(NT):
            b = i % 2
            t0 = i * 512
            if i + 1 < NT:
                dma(xt[1 - b][:], xv[:, :, t0 + 512:t0 + 1024], w=[r_xt[1 - b]])
            for gq, g in enumerate(GORD):
                pi = gi % 4
                gi += 1
                mmg([(pj[pi][:], win[:, kt, g * 128:(g + 1) * 128], hh[b][:, kt, :]) for kt in range(8)], r=[r_win, r_hh[b]], w=[r_pj[pi]])
                if g < 16:
                    if g < 12:
                        dst, rd = hst[:, g, 2:514], r_hst[g]
                    else:
                        dst, rd = zst[:, g - 12, 1:513], r_zst
                    ev += 1
                    if ev % 4:
                        op(ACT, lambda dst=dst, pi=pi: S.copy(out=dst, in_=pj[pi][:]), r=[r_pj[pi]], w=[rd])
                    else:
                        op(DVE, lambda dst=dst, pi=pi: V.tensor_copy(out=dst, in_=pj[pi][:]), r=[r_pj[pi]], w=[rd])
                elif g in (18, 19):
                    op(ACT, lambda g=g, pi=pi: S.activation(out=sg[:, g - 18, :], in_=pj[pi][:], func=AF.Sigmoid), r=[r_pj[pi]], w=[r_sg[g - 18]])
                elif g in (16, 17):
                    op(DVE, lambda g=g, pi=pi: V.tensor_tensor(out=stage[:, g - 16, :], in0=pj[pi][:], in1=sg[:, g - 16, :], op=ALU.mult),
                       r=[r_pj[pi], r_sg[g - 16]], w=[r_stage[g - 16]])
                elif g in (20, 21):
                    op(ACT, lambda g=g, pi=pi: S.activation(out=stage[:, g - 18, :], in_=pj[pi][:], func=AF.Silu), r=[r_pj[pi]], w=[r_stage[g - 18]])
                    if g == 21:
                        dma(pv3[:, 16:20, t0:t0 + 512], stage[:], r=r_stage)
                elif g < 24:
                    op(DVE, lambda g=g, pi=pi: V.tensor_copy(out=qT[b][:, g - 22, :], in_=pj[pi][:]), r=[r_pj[pi]], w=[r_qT[b]])
                else:
                    op(ACT, lambda g=g, pi=pi: S.activation(out=zs[b][:, g - 24, :], in_=pj[pi][:], func=AF.Silu), r=[r_pj[pi]], w=[r_zs[b]])
                if gq < 8 and i + 1 < NT:
                    norm_a(i + 1, gq)
                if gq == 10 and i + 1 < NT:
                    norm(i + 1)
                if 13 <= gq < 17 and i + 1 < NT:
                    norm_h(i + 1, gq - 13)
                if gq == 15:
                    pending = pending + hy_micro(i)
                if pending and (gq >= 16 or i > 0):
                    pending.pop(0)()
            att = attn_micro(i)
            mix_ = []
            while pending or att:
                if pending:
                    mix_.append(pending.pop(0))
                if att:
                    mix_.append(att.pop(0))
            pending = mix_
        while pending:
            pending.pop(0)()
        k.barrier()
        st.close()

    def phase_b2():
        xv = vx1s.rearrange("c (a b) -> a c b", b=128)
        yv = yscr.rearrange("c (a b) -> a c b", b=128)

        def load_d(b, dst, rdst):
            dma(dst[0:64, :, :], xv[:, b * SBC:(b + 1) * SBC, :], w=[rdst])

        def store_d(b, src, rsrc):
            dma(yv[:, b * SBC:(b + 1) * SBC, :], src[:], r=rsrc)

        fft_pipeline(512 // SBC, 64, load_d, False, store_d)

    CH = 2048

    def phase_b3():
        st = ExitStack()
        CW = 1024
        NB = 3
        g0c = [k.sb(st, "g0c%d" % i, [128, CW], BF16) for i in range(NB)]
        vxc = [k.sb(st, "vxc%d" % i, [128, CW], BF16) for i in range(NB)]
        yc = [k.sb(st, "yc%d" % i, [128, CW], F32) for i in range(NB)]
        ttc = [k.sb(st, "ttc%d" % i, [128, CW], F32) for i in range(NB)]
        ohc = [k.sb(st, "ohc%d" % i, [128, CW], BF16) for i in range(NB)]
        RN = lambda: [Res() for _ in range(NB)]
        r_g0c, r_vxc, r_yc, r_ttc, r_ohc = RN(), RN(), RN(), RN(), RN()
        chunks = [(j, c0) for j in range(4) for c0 in range(0, L, CW)]

        def b3_load(n):
            j, c0 = chunks[n]
            i = n % NB
            dma(g0c[i][:], g0s[j * 128:(j + 1) * 128, c0:c0 + CW], w=[r_g0c[i]])
            dma(vxc[i][:], vx1s[j * 128:(j + 1) * 128, c0:c0 + CW], w=[r_vxc[i]])
            dma(yc[i][:], yscr[j * 128:(j + 1) * 128, c0:c0 + CW], w=[r_yc[i]], Q=ACT)

        def b3_comp(n):
            j, c0 = chunks[n]
            i = n % NB
            op(DVE, lambda: V.scalar_tensor_tensor(out=ttc[i][:], in0=vxc[i][:], scalar=sp[:, C_SKIP + j:C_SKIP + j + 1],
                                                   in1=yc[i][:], op0=ALU.mult, op1=ALU.add), r=[r_vxc[i], r_yc[i], Rsp], w=[r_ttc[i]])
            op(DVE, lambda: V.tensor_tensor(out=ohc[i][:], in0=ttc[i][:], in1=g0c[i][:], op=ALU.mult), r=[r_ttc[i], r_g0c[i]], w=[r_ohc[i]])
            dma(mixs[j * 128:(j + 1) * 128, c0:c0 + CW], ohc[i][:], r=[r_ohc[i]], Q=POOL)

        for n in range(min(NB - 1, len(chunks))):
            b3_load(n)
        for n in range(len(chunks)):
            if n + NB - 1 < len(chunks):
                b3_load(n + NB - 1)
            b3_comp(n)
        k.barrier()
        st.close()

    def phase_b4():
        st = ExitStack()
        hrow = k.sb(st, "hrow", [128, 2, L + 30], BF16)
        r_hrow = [Res(), Res()]
        diag = k.sb(st, "diag", [128, 62, 128], BF16)
        r_diag = [Res() for _ in range(62)]
        for jk in range(62):
            op(DVE, lambda jk=jk: V.tensor_scalar(out=diag[:, jk, :], in0=ident[:], scalar1=sp[:, C_DWW + jk:C_DWW + jk + 1], scalar2=None,
                                                  op0=ALU.mult), r=[Rc, Rsp], w=[r_diag[jk]])
        for j in range(2):
            op(POOL, lambda j=j: G.memset(hrow[:, j, 0:15], 0.0), w=[r_hrow[j]])
            op(POOL, lambda j=j: G.memset(hrow[:, j, L + 15:L + 30], 0.0), w=[r_hrow[j]])
            dma(hrow[:, j, 15:15 + L], pscr[(16 + j) * 128:(17 + j) * 128, :], w=[r_hrow[j]])
        czc = [k.sb(st, "czc%d" % i, [128, 2, 512], BF16) for i in range(4)]
        ouc = [k.sb(st, "ouc%d" % i, [128, 2, 512], BF16) for i in range(2)]
        r_czc, r_ouc = [Res() for _ in range(4)], [Res(), Res()]
        czv = pscr[18 * 128:20 * 128, :].rearrange("(j p) t -> p j t", p=128)
        mxv = mixs[512:768, :].rearrange("(j p) t -> p j t", p=128)
        P2_ = range(2)
        cb = [k.sb(st, "cb%d" % i, [128, 2, 512], F32) for i in range(4)]
        cbb = [k.sb(st, "cbb%d" % i, [128, 2, 512], BF16) for i in P2_]
        sqb = [k.sb(st, "sqb%d" % i, [128, 2, 512], BF16) for i in P2_]
        m2 = [k.sb(st, "m2%d" % i, [128, 512], F32) for i in P2_]
        var = [k.sb(st, "var%d" % i, [128, 512], F32) for i in P2_]
        dd = [k.sb(st, "dd%d" % i, [128, 2, 512], F32) for i in P2_]
        zz = [k.sb(st, "zz%d" % i, [128, 2, 512], F32) for i in P2_]
        pc = [[k.ps(st, "pc%d%d" % (i, j), [128, 512]) for j in range(2)] for i in P2_]
        pm = [k.ps(st, "pm%d" % i, [128, 512]) for i in P2_]
        pq = [k.ps(st, "pq%d" % i, [128, 512]) for i in P2_]
        RR = lambda: [Res(), Res()]
        r_cb, r_cbb, r_sqb, r_m2, r_var, r_zz, r_pm, r_pq = [Res() for _ in range(4)], RR(), RR(), RR(), RR(), RR(), RR(), RR()
        r_dd = [RR(), RR()]
        r_pc = [RR(), RR()]

        def c0_(ci):
            i = ci % 2
            t0 = ci * 512
            for j in range(2):
                mmg([(pc[i][j][:], diag[:, j * 31 + kk, :], hrow[:, j, t0 + kk:t0 + kk + 512]) for kk in range(31)],
                    r=r_diag[j * 31:(j + 1) * 31] + [r_hrow[j]], w=[r_pc[i][j]])

        def c1_(ci):
            i = ci % 2
            for j in range(2):
                op(ACT, lambda j=j: S.activation(out=cb[ci % 4][:, j, :], in_=pc[i][j][:], func=AF.Identity, bias=sp[:, C_DWB + j:C_DWB + j + 1], scale=1.0),
                   r=[r_pc[i][j], Rsp], w=[r_cb[ci % 4]])

        def c2_(ci):
            i = ci % 2
            op(DVE, lambda: V.tensor_copy(out=cbb[i][:], in_=cb[ci % 4][:]), r=[r_cb[ci % 4]], w=[r_cbb[i]])
            op(ACT, lambda: S.activation(out=sqb[i][:], in_=cb[ci % 4][:], func=AF.Square), r=[r_cb[ci % 4]], w=[r_sqb[i]])
            mmg([(pm[i][:], ones_c[:], cbb[i][:, j, :]) for j in range(2)], r=[Rc, r_cbb[i]], w=[r_pm[i]])
            mmg([(pq[i][:], ones_c[:], sqb[i][:, j, :]) for j in range(2)], r=[Rc, r_sqb[i]], w=[r_pq[i]])

        def c3_(ci):
            i = ci % 2
            dma(czc[ci % 4][:], czv[:, :, ci * 512:ci * 512 + 512], w=[r_czc[ci % 4]], Q=ACT)
            op(ACT, lambda: S.activation(out=m2[i][:], in_=pm[i][:], func=AF.Square), r=[r_pm[i]], w=[r_m2[i]])
            op(DVE, lambda: V.tensor_tensor(out=var[i][:], in0=pq[i][:], in1=m2[i][:], op=ALU.subtract), r=[r_pq[i], r_m2[i]], w=[r_var[i]])
            op(DVE, lambda: V.tensor_scalar(out=var[i][:], in0=var[i][:], scalar1=0.0, scalar2=None, op0=ALU.max), r=[r_var[i]], w=[r_var[i]])
            op(ACT, lambda: S.activation(out=var[i][:], in_=var[i][:], func=AF.Sqrt, bias=EPS, scale=1.0), r=[r_var[i]], w=[r_var[i]])
            op(DVE, lambda: V.reciprocal(out=var[i][:], in_=var[i][:]), r=[r_var[i]], w=[r_var[i]])

        def c4_(ci):
            i = ci % 2
            for j in range(2):
                op(DVE, lambda j=j: V.tensor_tensor(out=dd[i][:, j, :], in0=cb[ci % 4][:, j, :], in1=pm[i][:], op=ALU.subtract),
                   r=[r_cb[ci % 4], r_pm[i]], w=[r_dd[i][j]])
            for j in range(2):
                op(DVE, lambda j=j: V.tensor_tensor(out=dd[i][:, j, :], in0=dd[i][:, j, :], in1=var[i][:], op=ALU.mult),
                   r=[r_dd[i][j], r_var[i]], w=[r_dd[i][j]])
            for j in range(2):
                op(ACT, lambda j=j: S.activation(out=dd[i][:, j, :], in_=dd[i][:, j, :], func=AF.Silu, bias=sp[:, C_LNB + j:C_LNB + j + 1],
                                                 scale=sp[:, C_LNG + j:C_LNG + j + 1]), r=[r_dd[i][j], Rsp], w=[r_dd[i][j]])

        def c5_(ci):
            i = ci % 2
            t0 = ci * 512
            op(DVE, lambda: V.tensor_tensor(out=ouc[i][:], in0=dd[i][:], in1=czc[ci % 4][:], op=ALU.mult),
               r=r_dd[i] + [r_czc[ci % 4]], w=[r_ouc[i]])
            dma(mxv[:, :, t0:t0 + 512], ouc[i][:], r=[r_ouc[i]])

        cst = [c0_, c1_, c2_, c3_, c4_, c5_]
        nsteps = NT + len(cst) - 1
        for step in range(nsteps):
            for si in range(len(cst) - 1, -1, -1):
                ci = step - si
                if 0 <= ci < NT:
                    cst[si](ci)
        k.barrier()
        st.close()

    def phase_c(l, s, xsrc, last, wo, r_wo, nxt):
        st_kv = kv_prep(nxt[0], nxt[1], False) if nxt is not None else None
        st = ExitStack()
        mt_ = [k.sb(st, "mixt%d" % i, [128, 8, 512], BF16) for i in range(2)]
        NXB = 4 if last else 3
        xt = [k.sb(st, "xtc%d" % i, [128, 8, 512], F32) for i in range(NXB)]
        xo = xt
        po = [k.ps(st, "po%d" % i, [128, 512]) for i in range(4)]
        r_xt = [Res() for _ in range(NXB)]
        r_xo = r_xt
        r_mt, r_po = [Res(), Res()], [Res() for _ in range(4)]
        xv = xsrc.rearrange("(kt p) t -> p kt t", p=128)
        ov = yT[s].rearrange("(kt p) t -> p kt t", p=128)
        mv3 = mixs.rearrange("(g p) t -> p g t", p=128)
        if last:
            sq = k.sb(st, "sqc", [128, 8, 512], BF16)
            rstd = k.sb(st, "rstdc", [128, 512], F32)
            pn = k.ps(st, "pn", [128, 512])
            r_sq, r_rstd, r_pn = [Res() for _ in range(4)], Res(), Res()

        def fin_a(i, q):
            b = i % NXB
            op(ACT, lambda: S.activation(out=sq[:, 2 * q:2 * q + 2, :], in_=xo[b][:, 2 * q:2 * q + 2, :], func=AF.Square), r=[r_xo[b]], w=[r_sq[q]])

        def fin_b(i):
            mmg([(pn[:], ones_d[:], sq[:, kt, :]) for kt in range(8)], r=r_sq + [Rc], w=[r_pn])
            op(ACT, lambda: S.activation(out=rstd[:], in_=pn[:], func=AF.Ln, bias=epsb[:, 0:1], scale=1.0), r=[r_pn, Rc], w=[r_rstd])
            op(ACT, lambda: S.activation(out=rstd[:], in_=rstd[:], func=AF.Exp, scale=-0.5), r=[r_rstd], w=[r_rstd])

        def fin_c(i, g):
            b = i % NXB
            op(DVE, lambda: V.scalar_tensor_tensor(out=xo[b][:, g, :], in0=xo[b][:, g, :], scalar=sp[:, C_FG + g:C_FG + g + 1],
                                                   in1=rstd[:], op0=ALU.mult, op1=ALU.mult), r=[r_xo[b], r_rstd, Rsp], w=[r_xo[b]])

        def store(i):
            dma(ov[:, :, i * 512:(i + 1) * 512], xo[i % NXB][:], r=[r_xo[i % NXB]])

        r_mh = [Res(), Res()]

        def loads(i):
            bb = i % 2
            t1 = i * 512
            dma(mt_[bb][:], mv3[:, :, t1:t1 + 512], w=[r_mt[bb]])
            dma(xt[i % NXB][:], xv[:, :, t1:t1 + 512], w=[r_xt[i % NXB]], Q=ACT)

        loads(0)
        for i in range(NT):
            b = i % 2
            t0 = i * 512
            if i + 1 < NT:
                loads(i + 1)
            for g in range(8):
                pi = g % 4
                mmg([(po[pi][:], wo[:, kt, g * 128:(g + 1) * 128], mt_[b][:, kt, :]) for kt in range(8)], r=[r_wo, r_mt[b], r_mh[b]], w=[r_po[pi]])

                op(DVE, lambda g=g, pi=pi, i=i: V.tensor_tensor(out=xt[i % NXB][:, g, :], in0=po[pi][:], in1=xt[i % NXB][:, g, :], op=ALU.add),
                   r=[r_po[pi], r_xt[i % NXB]], w=[r_xt[i % NXB]])
                if last and i > 0:
                    if g < 4:
                        fin_a(i - 1, g)
                    if g == 4:
                        fin_b(i - 1)
            if last:
                if i > 0:
                    for g in range(8):
                        fin_c(i - 1, g)
                    store(i - 1)
            else:
                store(i)
        if last:
            i = NT - 1
            for q in range(4):
                fin_a(i, q)
            fin_b(i)
            for g in range(8):
                fin_c(i, g)
            store(i)
        k.barrier()
        st.close()
        if st_kv is not None:
            st_kv.close()

    load_layer_weights(0)
    order = [(l, s) for l in range(nl) for s in range(nseq)]
    for idx, (l, s) in enumerate(order):
        if s == 0:
            dma(sp[:], spar[l], w=[Rsp])
            k.barrier()
            st_kv0 = kv_prep(l, s, True)
            gen_filter(l)
            st_kv0.close()
        xsrc = xT[s] if l == 0 else yT[s]
        phase_a(l, s, xsrc)
        if s == nseq - 1 and l + 1 < nl:
            load_layer_weights(l + 1)
        wst_ = ExitStack()
        wo = k.sb(wst_, "wo", [128, 8, D], BF16)
        r_wo = Res()
        dma(wo[:], w_out[l].rearrange("(kt p) c -> p kt c", p=128), w=[r_wo], Q=POOL)
        phase_b2()
        phase_b3()
        phase_b4()
        nxt = order[idx + 1] if idx + 1 < len(order) else None
        phase_c(l, s, xsrc, l == nl - 1, wo, r_wo, nxt if (nxt is not None and nxt[0] == l) else None)
        if nxt is not None and nxt[0] != l:
            pass
        wst_.close()
        if nxt is not None and nxt[0] != l:
            pass
    k.barrier()
    top.close()
    k.es.close()
    return nc


def _consts():
    a = np.arange(128, dtype=np.float64)
    th = 2.0 * np.pi * np.outer(a, a) / 128.0
    C = np.cos(th)
    S = np.sin(th)
    wk = np.full((128, 1), 2.0); wk[0] = 1.0; wk[64] = 1.0; wk[65:] = 0.0
    cf = np.concatenate([C, -S, S, C[:, 0:65], -S[:, 0:65], wk * C[:, 0:64] / 16384.0, -wk * S[:, 0:64] / 16384.0,
                         np.zeros((128, 2)), C, S, -S, C], axis=1)
    cfft = cf.astype(np.float32).astype(ml_dtypes.bfloat16)
    tt = 2.0 * np.pi * np.outer(a, a) / 16384.0
    ctw = np.stack([np.cos(tt), -np.sin(tt)], axis=1).astype(np.float32)
    t = np.linspace(0.0, 1.0, L, dtype=np.float32)[:, None]
    n = np.arange(L, dtype=np.float32)[:, None]
    bands = np.linspace(1e-4, 15, 16, dtype=np.float32)[None, :]
    ang = (np.float32(2.0 * math.pi / L) * bands * n).astype(np.float32)
    z = np.concatenate([t, np.cos(ang), -np.sin(ang)], axis=-1).astype(np.float32)
    idx_b = (L - np.arange(L)) % L
    cz = np.stack([z.T, z[idx_b].T], axis=0).astype(np.float32)
    ctp = np.stack([t[:, 0], t[idx_b, 0]], axis=0).astype(np.float32)
    max_decay = math.log(1e-2) / 0.3
    min_decay = math.log(1e-2) / 1.5
    deltas = np.abs(np.linspace(min_decay, max_decay, 512, dtype=np.float32))
    cnd = np.ascontiguousarray((-deltas).reshape(4, 128).T).astype(np.float32)
    cid = np.eye(128, dtype=np.float32).astype(ml_dtypes.bfloat16)
    return dict(cfft=cfft, ctw=ctw, cz=np.ascontiguousarray(cz), ctp=np.ascontiguousarray(ctp), cnd=cnd, cid=cid)


def _spar(inp, nl):
    sp = np.zeros((nl, 128, NSP), np.float32)
    f = lambda a: np.asarray(a, np.float32)
    for l in range(nl):
        sp[l, :, 0:8] = f(inp["norm_g"])[l].reshape(8, 128).T
        sp[l, :, 8:16] = f(inp["mem_norm_g"])[l].reshape(8, 128).T
        sw = f(inp["hy_short_w"])[l]
        sp[l, :, 16:52] = sw.reshape(3, 12, 128).transpose(2, 1, 0).reshape(128, 36)
        sp[l, :, 52:64] = f(inp["hy_short_b"])[l].reshape(12, 128).T
        sp[l, :, 64:68] = f(inp["hy_skip"])[l].reshape(4, 128).T
        dw = f(inp["cf_dw_w"])[l]
        sp[l, :, 68:130] = dw.reshape(31, 2, 128).transpose(2, 1, 0).reshape(128, 62)
        sp[l, :, 130:132] = f(inp["cf_dw_b"])[l].reshape(2, 128).T
        sp[l, :, 132:134] = f(inp["cf_ln_g"])[l].reshape(2, 128).T
        sp[l, :, 134:136] = f(inp["cf_ln_b"])[l].reshape(2, 128).T
        sp[l, :, 136:144] = f(inp["final_g"]).reshape(8, 128).T
        for j, (bn, fn) in enumerate((("hy_f_b1", "hy_f_fr1"), ("hy_f_b2", "hy_f_fr2"), ("hy_f_b3", "hy_f_fr3"))):
            sp[l, 0:64, 144 + 2 * j] = f(inp[bn])[l]
            sp[l, 0:64, 144 + 2 * j + 1] = f(inp[fn])[l]
            sp[l, 64:128, 144 + 2 * j] = f(inp[bn])[l]
            sp[l, 64:128, 144 + 2 * j + 1] = f(inp[fn])[l]
    return sp


_NC_CACHE = {}


def run(inp, nl=4, nseq=2, seq_lists=None, debug=False, trace=False):
    key = (nl, nseq, debug)
    if key not in _NC_CACHE:
        _NC_CACHE[key] = build_program(nl, nseq, debug)
    nc = _NC_CACHE[key]
    f = lambda a: np.ascontiguousarray(np.asarray(a, np.float32))
    shared = _consts()
    shared.update(
        w_in=f(inp["w_in"])[:nl], w_kv=f(inp["xa_w_kv"])[:nl], w_out=f(inp["w_out"])[:nl],
        fw1=f(inp["hy_f_w1"])[:nl], fw2=f(inp["hy_f_w2"])[:nl], fw3=f(inp["hy_f_w3"])[:nl], fw4=f(inp["hy_f_w4"])[:nl],
        spar=_spar(inp, nl),
    )
    srcs = {"s": (inp["x_sample"], inp["mem_sample"]), "p": (inp["x_prompt"], inp["mem_prompt"])}
    tcache = {}

    def getT(kind, idx):
        if (kind, idx) not in tcache:
            x, m = srcs[kind]
            tcache[(kind, idx)] = (np.ascontiguousarray(np.asarray(x[idx], np.float32).T),
                                   np.ascontiguousarray(np.asarray(m[idx], np.float32).T))
        return tcache[(kind, idx)]

    in_maps = []
    for c, sl in enumerate(seq_lists):
        xs = np.stack([getT(*q)[0] for q in sl], axis=0)
        ms = np.stack([getT(*q)[1] for q in sl], axis=0)
        d = dict(shared)
        d["xT"] = xs
        d["memT"] = ms
        in_maps.append(d)
    res = run_bass_kernel_spmd(nc, in_maps, core_ids=list(range(len(seq_lists))), **({"trace": True} if trace else {}))
    return res


def kernel(**inp):
    seq_lists = [[("s", c), ("p", c % 2)] for c in range(N_CORES)]
    res = run(inp, 4, 2, seq_lists)
    y_s = np.empty((8, L, D), np.float32)
    y_p = np.empty((2, L, D), np.float32)
    for c in range(N_CORES):
        yt = res.results[c]["yT"]
        y_s[c] = yt[0].T
        if c < 2:
            y_p[c] = yt[1].T
    return (y_p, y_s)
```
